# Optimizing a Trainium2 kernel written in Bass

```python
import math
import jax, jax.numpy as jnp
from jax import lax
import numpy as np

D_MODEL = 2048
BATCH = 2
SEQ = 8192
DEPTH = 1

ATT_HEADS = 8
ATT_HEAD_DIM = 64
ATT_V_DIM = 2 * ATT_HEAD_DIM
ATT_QK_WIDTH = ATT_HEADS * 2 * ATT_HEAD_DIM
ATT_WIDTH = ATT_HEADS * ATT_V_DIM
ROPE_DIM = ATT_HEAD_DIM // 4
ROPE_THETA = 500000.0
Q_BLOCK = 128

HY_WIDTH = 1024
HY_ORDER = 2
HY_SHORT_CONV = 3
HY_EMB_BANDS = 16
HY_EMB_DIM = 1 + 2 * HY_EMB_BANDS
HY_FILTER_HIDDEN = 64
HY_DECAY_TARGET = 1e-2
HY_FAST_DECAY_PCT = 0.3
HY_SLOW_DECAY_PCT = 1.5
HY_MIN_DECAY = math.log(HY_DECAY_TARGET) / HY_FAST_DECAY_PCT
HY_MAX_DECAY = math.log(HY_DECAY_TARGET) / HY_SLOW_DECAY_PCT

N_BRANCHES = 2
W_IN_COLS = (HY_ORDER + 1) * HY_WIDTH + 2 * ATT_QK_WIDTH + ATT_WIDTH + N_BRANCHES * D_MODEL

FFN_HIDDEN = (((8 * D_MODEL + 2) // 3 + 255) // 256) * 256

N_MOD = 6
EPS = 1e-6

kernel_name = "hybrid_hyena_diffattn_encoder_block"


def rmsnorm(x, g):
    xf = x.astype(jnp.float32)
    y = xf * lax.rsqrt(jnp.mean(xf * xf, axis=-1, keepdims=True) + EPS)
    return (y * g).astype(x.dtype)


def short_conv_centred(u, w, b):
    L = u.shape[1]
    pad = HY_SHORT_CONV // 2
    up = jnp.pad(u, ((0, 0), (pad, pad), (0, 0)))
    out = up[:, 0:L] * w[0]
    for i in range(1, HY_SHORT_CONV):
        out = out + up[:, i:i + L] * w[i]
    return out + b


def hyena_filters(L, w1, b1, w2, b2, w3, b3, freq, w_out):
    t = jnp.linspace(0.0, 1.0, L, dtype=jnp.float32)[:, None]
    w = (2.0 * math.pi / L) * jnp.arange(L, dtype=jnp.float32)[:, None]
    bands = jnp.linspace(1e-4, HY_EMB_BANDS - 1, HY_EMB_BANDS, dtype=jnp.float32)
    z = jnp.concatenate([t, jnp.cos(bands * w), -jnp.sin(bands * w)], axis=-1)
    h = jnp.sin(freq[0] * (z @ w1 + b1))
    h = jnp.sin(freq[1] * (h @ w2 + b2))
    h = jnp.sin(freq[2] * (h @ w3 + b3))
    h = (h @ w_out).reshape(L, HY_ORDER, 2, HY_WIDTH)
    deltas = jnp.abs(jnp.linspace(HY_MIN_DECAY, HY_MAX_DECAY, HY_WIDTH, dtype=jnp.float32))
    decay = jnp.exp(-t * deltas)
    return h * decay[:, None, None, :]


def bidir_long_conv(u, h_fwd, h_bwd, bias):
    L, C = h_fwd.shape
    k = jnp.concatenate([h_fwd, jnp.zeros((1, C), h_fwd.dtype), h_bwd[1:][::-1]], axis=0)
    uf = jnp.fft.rfft(u.astype(jnp.float32), n=2 * L, axis=1)
    kf = jnp.fft.rfft(k.astype(jnp.float32), n=2 * L, axis=0)
    y = jnp.fft.irfft(uf * kf[None], n=2 * L, axis=1)[:, :L]
    return (y + u * bias).astype(u.dtype)


def partial_rope(x, pos):
    half = ROPE_DIM // 2
    inv = ROPE_THETA ** (-jnp.arange(half, dtype=jnp.float32) * (2.0 / ROPE_DIM))
    ang = pos.astype(jnp.float32)[:, :, None] * inv
    cos = jnp.cos(ang)[:, :, None, None, :]
    sin = jnp.sin(ang)[:, :, None, None, :]
    xr = x[..., :ROPE_DIM].astype(jnp.float32)
    x1, x2 = xr[..., :half], xr[..., half:]
    rot = jnp.concatenate([x1 * cos - x2 * sin, x2 * cos + x1 * sin], axis=-1)
    return jnp.concatenate([rot.astype(x.dtype), x[..., ROPE_DIM:]], axis=-1)


def diff_attention(q, k, v, pos, q_g, k_g, lam, subln_g, lam_init):
    B, L = q.shape[0], q.shape[1]
    q = partial_rope(rmsnorm(q, q_g), pos) * (ATT_HEAD_DIM ** -0.5)
    k = partial_rope(rmsnorm(k, k_g), pos)
    qh = q.reshape(B, L, 2 * ATT_HEADS, ATT_HEAD_DIM).transpose(0, 2, 1, 3)
    kh = k.reshape(B, L, 2 * ATT_HEADS, ATT_HEAD_DIM).transpose(0, 2, 1, 3)
    vh = v.transpose(0, 2, 1, 3)
    nb = L // Q_BLOCK
    qb = qh.reshape(B, 2 * ATT_HEADS, nb, Q_BLOCK, ATT_HEAD_DIM).transpose(2, 0, 1, 3, 4)

    def block(q_blk):
        s = jnp.einsum('bhqd,bhkd->bhqk', q_blk, kh, preferred_element_type=jnp.float32)
        p = jax.nn.softmax(s, axis=-1).reshape(B, ATT_HEADS, 2, Q_BLOCK, L)
        a = p[:, :, 0] - lam * p[:, :, 1]
        return jnp.einsum('bhqk,bhkv->bhqv', a, vh, preferred_element_type=jnp.float32)

    o = lax.map(block, qb)
    o = o.transpose(1, 0, 3, 2, 4).reshape(B, L, ATT_HEADS, ATT_V_DIM)
    o = rmsnorm(o, subln_g) * (1.0 - lam_init)
    return o.reshape(B, L, ATT_WIDTH).astype(v.dtype)


def setup_inputs(seed: int = 0) -> dict:
    key = jax.random.key(seed)
    ks = jax.random.split(key, 32)
    f32 = jnp.float32

    def nrm(k, shape, scale):
        return jax.random.normal(k, shape, f32) * scale

    Dp = DEPTH
    return {
        "x": nrm(ks[0], (BATCH, SEQ, D_MODEL), 1.0),
        "c": nrm(ks[1], (BATCH, D_MODEL), 1.0),
        "positions": (jnp.arange(SEQ, dtype=jnp.int32)[None, :]
                      + jax.random.randint(ks[2], (BATCH, 1), 0, 4096, dtype=jnp.int32)),
        "w_ada": nrm(ks[3], (Dp, D_MODEL, N_MOD * D_MODEL), 0.5 * D_MODEL ** -0.5),
        "b_ada": nrm(ks[4], (Dp, N_MOD * D_MODEL), 0.01),
        "norm1_g": 1.0 + nrm(ks[5], (Dp, D_MODEL), 0.02),
        "w_in": nrm(ks[6], (Dp, D_MODEL, W_IN_COLS), D_MODEL ** -0.5),
        "hy_conv_w": nrm(ks[7], (Dp, HY_SHORT_CONV, (HY_ORDER + 1) * HY_WIDTH), HY_SHORT_CONV ** -0.5),
        "hy_conv_b": nrm(ks[8], (Dp, (HY_ORDER + 1) * HY_WIDTH), 0.01),
        "hy_filt_w1": nrm(ks[9], (Dp, HY_EMB_DIM, HY_FILTER_HIDDEN), HY_EMB_DIM ** -0.5),
        "hy_filt_b1": nrm(ks[10], (Dp, HY_FILTER_HIDDEN), 0.1),
        "hy_filt_w2": nrm(ks[11], (Dp, HY_FILTER_HIDDEN, HY_FILTER_HIDDEN), HY_FILTER_HIDDEN ** -0.5),
        "hy_filt_b2": nrm(ks[12], (Dp, HY_FILTER_HIDDEN), 0.1),
        "hy_filt_w3": nrm(ks[13], (Dp, HY_FILTER_HIDDEN, HY_FILTER_HIDDEN), HY_FILTER_HIDDEN ** -0.5),
        "hy_filt_b3": nrm(ks[14], (Dp, HY_FILTER_HIDDEN), 0.1),
        "hy_filt_freq": 1.0 + nrm(ks[15], (Dp, 3, HY_FILTER_HIDDEN), 0.1),
        "hy_filt_w_out": nrm(ks[16], (Dp, HY_FILTER_HIDDEN, HY_ORDER * 2 * HY_WIDTH), 0.03 * HY_FILTER_HIDDEN ** -0.5),
        "hy_bias": nrm(ks[17], (Dp, HY_ORDER, HY_WIDTH), 1.0),
        "q_norm_g": 1.0 + nrm(ks[18], (Dp, ATT_HEAD_DIM), 0.02),
        "k_norm_g": 1.0 + nrm(ks[19], (Dp, ATT_HEAD_DIM), 0.02),
        "lam_q1": nrm(ks[20], (Dp, ATT_HEAD_DIM), 0.1),
        "lam_k1": nrm(ks[21], (Dp, ATT_HEAD_DIM), 0.1),
        "lam_q2": nrm(ks[22], (Dp, ATT_HEAD_DIM), 0.1),
        "lam_k2": nrm(ks[23], (Dp, ATT_HEAD_DIM), 0.1),
        "subln_g": 1.0 + nrm(ks[24], (Dp, ATT_V_DIM), 0.02),
        "w_proj_hy": nrm(ks[25], (Dp, HY_WIDTH, D_MODEL), HY_WIDTH ** -0.5),
        "w_proj_att": nrm(ks[26], (Dp, ATT_WIDTH, D_MODEL), ATT_WIDTH ** -0.5),
        "w_out": nrm(ks[27], (Dp, D_MODEL, D_MODEL), D_MODEL ** -0.5),
        "norm2_g": 1.0 + nrm(ks[28], (Dp, D_MODEL), 0.02),
        "w_gate": nrm(ks[29], (Dp, D_MODEL, FFN_HIDDEN), D_MODEL ** -0.5),
        "w_up": nrm(ks[30], (Dp, D_MODEL, FFN_HIDDEN), D_MODEL ** -0.5),
        "w_down": nrm(ks[31], (Dp, FFN_HIDDEN, D_MODEL), FFN_HIDDEN ** -0.5),
    }


def reference(x, c, positions, w_ada, b_ada, norm1_g, w_in, hy_conv_w, hy_conv_b,
              hy_filt_w1, hy_filt_b1, hy_filt_w2, hy_filt_b2, hy_filt_w3, hy_filt_b3,
              hy_filt_freq, hy_filt_w_out, hy_bias, q_norm_g, k_norm_g,
              lam_q1, lam_k1, lam_q2, lam_k2, subln_g, w_proj_hy, w_proj_att, w_out,
              norm2_g, w_gate, w_up, w_down):
    B, L, _ = x.shape
    s_hy = (HY_ORDER + 1) * HY_WIDTH
    s_q = s_hy + ATT_QK_WIDTH
    s_k = s_q + ATT_QK_WIDTH
    s_v = s_k + ATT_WIDTH
    for l in range(DEPTH):
        lam_init = 0.8 - 0.6 * math.exp(-0.3 * l)
        mod = (jax.nn.silu(c) @ w_ada[l] + b_ada[l])[:, None, :]
        sh1, sc1, g1, sh2, sc2, g2 = jnp.split(mod, N_MOD, axis=-1)

        h = rmsnorm(x, norm1_g[l]) * (1.0 + sc1) + sh1
        proj = h @ w_in[l]
        hy_in, q, k, v, gates = jnp.split(proj, [s_hy, s_q, s_k, s_v], axis=-1)

        hy_in = short_conv_centred(hy_in, hy_conv_w[l], hy_conv_b[l])
        hv, hx1, hx2 = jnp.split(hy_in, HY_ORDER + 1, axis=-1)
        filt = hyena_filters(L, hy_filt_w1[l], hy_filt_b1[l], hy_filt_w2[l], hy_filt_b2[l],
                             hy_filt_w3[l], hy_filt_b3[l], hy_filt_freq[l], hy_filt_w_out[l])
        z = hx1 * bidir_long_conv(hv, filt[:, 0, 0], filt[:, 0, 1], hy_bias[l, 0])
        y_hy = hx2 * bidir_long_conv(z, filt[:, 1, 0], filt[:, 1, 1], hy_bias[l, 1])

        lam = (jnp.exp(jnp.sum(lam_q1[l].astype(jnp.float32) * lam_k1[l]))
               - jnp.exp(jnp.sum(lam_q2[l].astype(jnp.float32) * lam_k2[l])) + lam_init)
        y_att = diff_attention(q.reshape(B, L, ATT_HEADS, 2, ATT_HEAD_DIM),
                               k.reshape(B, L, ATT_HEADS, 2, ATT_HEAD_DIM),
                               v.reshape(B, L, ATT_HEADS, ATT_V_DIM),
                               positions, q_norm_g[l], k_norm_g[l], lam, subln_g[l], lam_init)

        g_hy, g_att = jnp.split(jax.nn.sigmoid(gates), N_BRANCHES, axis=-1)
        merged = g_hy * (y_hy @ w_proj_hy[l]) + g_att * (y_att @ w_proj_att[l])
        x = x + g1 * (merged @ w_out[l])

        h2 = rmsnorm(x, norm2_g[l]) * (1.0 + sc2) + sh2
        f = jax.nn.silu(h2 @ w_gate[l]) * (h2 @ w_up[l])
        x = x + g2 * (f @ w_down[l])
    return x
```

```python
import math, contextlib
import numpy as np
import ml_dtypes
import concourse.bass as bass
import concourse.mybir as mybir
from concourse.bass_utils import run_bass_kernel_spmd

F32, BF16, I32 = mybir.dt.float32, mybir.dt.bfloat16, mybir.dt.int32
AF = mybir.ActivationFunctionType
ALU = mybir.AluOpType
AX = mybir.AxisListType

D = 2048
L = 8192
NB = 64
FF = 5632
EPS = 1e-6
LAM_INIT = 0.8 - 0.6 * math.exp(0.0)
TWO_PI = 2.0 * math.pi
PI_B = 3.1415925
NDS = 6
SHIFT = -8.0


def _phase(active):
    if active:
        with contextlib.ExitStack() as st:
            yield st


class Res:
    __slots__ = ("w", "rd")

    def __init__(self):
        self.w = None
        self.rd = {}


class Eng:
    def __init__(self, name, eng, sem):
        self.name, self.eng, self.sem = name, eng, sem
        self.count = 0
        self.waited = {}


class T:
    def __init__(self, t):
        self.t = t
        self.r = Res()
        self.r2 = Res()

    def __getitem__(self, k):
        return self.t[k]


class KB:
    def __init__(self, nc, st):
        self.nc = nc
        mk = lambda n: st.enter_context(nc.semaphore(n))
        self.E = {n: Eng(n, e, mk("s_" + n)) for n, e in
                  [("pe", nc.tensor), ("act", nc.scalar), ("dve", nc.vector), ("pool", nc.gpsimd), ("sp", nc.sync)]}
        self.dsem = {q: [mk(f"d_{q}{i}") for i in range(NDS)] for q in ("sp", "pool", "act")}
        self.dval = {q: [0] * NDS for q in ("sp", "pool", "act")}
        self.di = {q: 0 for q in ("sp", "pool", "act")}

    def _wait(self, E, tok):
        kind, a, v = tok
        if kind == "c":
            if a is E and E.name == "pe":
                return
            sem, key = a.sem, a.name
        else:
            sem, key = a
        if E.waited.get(key, 0) >= v:
            return
        E.eng.wait_ge(sem, v)
        E.waited[key] = v

    def _deps(self, E, reads, writes):
        for r in reads:
            if r.w is not None:
                self._wait(E, r.w)
        for w in writes:
            if w.w is not None:
                self._wait(E, w.w)
            for t in w.rd.values():
                self._wait(E, t)

    def _commit(self, tok, key, reads, writes):
        for r in reads:
            r.rd[key] = tok
        for w in writes:
            w.w = tok
            w.rd = {}

    def op(self, en, fn, reads=(), writes=(), inc=True):
        E = self.E[en]
        reads = [x.r if isinstance(x, T) else x for x in reads]
        writes = [x.r if isinstance(x, T) else x for x in writes]
        self._deps(E, reads, writes)
        inst = fn()
        if inc:
            E.count += 1
            inst.then_inc(E.sem, 1)
            self._commit(("c", E, E.count), E.name, reads, writes)
        else:
            self._commit(("c", E, E.count + 1), E.name, reads, writes)

    def dma(self, q, out, in_, reads=(), writes=(), **kw):
        E = self.E[q]
        reads = [x.r if isinstance(x, T) else x for x in reads]
        writes = [x.r if isinstance(x, T) else x for x in writes]
        self._deps(E, reads, writes)
        i = self.di[q] % NDS
        self.di[q] += 1
        sem = self.dsem[q][i]
        v = self.dval[q][i]
        key = f"{q}{i}"
        if v > 0:
            self._wait(E, ("d", (sem, key), v))
        inst = E.eng.dma_start(out=out, in_=in_, **kw)
        inst.then_inc(sem, 16)
        self.dval[q][i] = v + 16
        self._commit(("d", (sem, key), v + 16), key, reads, writes)

    def barrier(self):
        for E in self.E.values():
            for Fe in self.E.values():
                if Fe is not E and Fe.count > 0:
                    sem, key, v = Fe.sem, Fe.name, Fe.count
                    if E.waited.get(key, 0) < v:
                        E.eng.wait_ge(sem, v)
                        E.waited[key] = v
            for q in self.dsem:
                for i in range(NDS):
                    v = self.dval[q][i]
                    if v > 0:
                        self._wait(E, ("d", (self.dsem[q][i], f"{q}{i}"), v))


def V(t, off, dims, p0=0, pn=None):
    tt = t.t if isinstance(t, T) else t
    base = tt[:]
    ps, pc = base.ap[0]
    if pn is None:
        pn = pc
    return bass.AP(tensor=base.tensor, offset=base.offset + p0 * ps + off, ap=[[ps, pn]] + [list(d) for d in dims])


def build(upto=99, dbg=None):
    nc = bass.Bass("TRN2", target_bir_lowering=False)
    dt_in = lambda n, s, d=F32: nc.dram_tensor(n, s, d, kind="ExternalInput").ap()
    xb = dt_in("xb", [L, D])
    xo = dt_in("xo", [2048, D])
    cb = dt_in("cb", [128, 16])
    posd = dt_in("pos", [128, NB], I32)
    w_ada = dt_in("w_ada", [D, 6 * D])
    badaT = dt_in("badaT", [128, 96])
    n1g = dt_in("n1g", [128, 16])
    n2g = dt_in("n2g", [128, 16])
    w_hy = dt_in("w_hy", [D, 768])
    w_qkv = dt_in("w_qkv", [D, 768])
    w_g = dt_in("w_g", [D, 4096])
    convw = dt_in("convw", [128, 18])
    convb = dt_in("convb", [128, 6])
    fw1 = dt_in("fw1", [33, 64])
    fw2 = dt_in("fw2", [64, 64])
    fw3 = dt_in("fw3", [64, 64])
    fbias = dt_in("fbias", [64, 3])
    ffreq = dt_in("ffreq", [64, 3])
    fwout = dt_in("fwout", [64, 1024])
    ndelta = dt_in("ndelta", [128, 2])
    hyb_bc = dt_in("hyb_bc", [128, 512])
    gqk_bc = dt_in("gqk_bc", [128, 512])
    lamv = dt_in("lamv", [1, 256])
    subln_bc = dt_in("subln_bc", [128, 128])
    w_ph = dt_in("w_ph", [1024, D])
    w_pa = dt_in("w_pa", [1024, D])
    w_o = dt_in("w_o", [D, D])
    w_gate = dt_in("w_gate", [D, FF])
    w_up = dt_in("w_up", [D, FF])
    w_down = dt_in("w_down", [FF, D])
    sel = dt_in("sel", [128, 4])
    ident_bf_d = dt_in("ident_bf", [128, 128], BF16)
    ident_f_d = dt_in("ident_f", [128, 128])
    zT2 = dt_in("zT2", [33, 4 * L])
    tgrid = dt_in("tgrid", [1, 4 * L])
    jrev_d = dt_in("jrev", [128, 128], BF16)
    hybT_d = dt_in("hybT", [128, 4])
    invf_bc = dt_in("invf_bc", [128, 8])
    out = nc.dram_tensor("out", [2048, D], F32, kind="ExternalOutput").ap()

    hy_pre = nc.dram_tensor("hy_pre", [768, L + 2], BF16)
    qkT_d = nc.dram_tensor("qkT_d", [512, L], BF16)
    vtok = nc.dram_tensor("vtok", [L, 256], BF16)
    k2 = nc.dram_tensor("k2", [512, 2 * L], BF16)
    ag_in = [nc.dram_tensor(f"ag_in{i}", [512, 1024], BF16) for i in range(8)]
    ag_out = [nc.dram_tensor(f"ag_out{i}", [2048, 1024], BF16) for i in range(8)]
    r_hy_pre, r_qkT, r_vtok, r_k2, r_agin, r_agout, r_out = [Res() for _ in range(7)]
    wsrc = {"w_g": w_g, "w_ph": w_ph, "w_pa": w_pa, "w_o": w_o, "w_gate": w_gate, "w_up": w_up, "w_down": w_down}
    wcw = {k: (128 if k == "w_down" else 256) for k in wsrc}
    wkb = {k: v.shape[0] // 128 for k, v in wsrc.items()}
    wbf = {k: nc.dram_tensor(k + "_bf", [(v.shape[1] // wcw[k]) * 128, wkb[k] * wcw[k]], BF16) for k, v in wsrc.items()}
    r_wbf = {k: Res() for k in wsrc}

    with contextlib.ExitStack() as top:
        kb = KB(nc, top)
        op, dma = kb.op, kb.dma
        _cnt = [0]

        def wrap(dst, src, shift, tmp, dres, sres, tres):
            op("dve", lambda: nc.vector.tensor_scalar(dst, src, float(shift), None, ALU.add), reads=[sres], writes=[dres])
            op("dve", lambda: nc.vector.tensor_scalar(tmp, dst, PI_B, -TWO_PI, ALU.is_gt, ALU.mult), reads=[dres], writes=[tres])
            op("dve", lambda: nc.vector.tensor_tensor(out=dst, in0=dst, in1=tmp, op=ALU.add), reads=[dres, tres], writes=[dres])
            op("dve", lambda: nc.vector.tensor_scalar(tmp, dst, -PI_B, TWO_PI, ALU.is_lt, ALU.mult), reads=[dres], writes=[tres])
            op("dve", lambda: nc.vector.tensor_tensor(out=dst, in0=dst, in1=tmp, op=ALU.add), reads=[dres, tres], writes=[dres])

        def sbuf(st, shape, dt, name=None):
            _cnt[0] += 1
            return T(st.enter_context(nc.sbuf_tensor(f"{name or 'sb'}_{_cnt[0]}", list(shape), dt)))

        def psum(st, shape, dt, name=None):
            _cnt[0] += 1
            return T(st.enter_context(nc.psum_tensor(f"{name or 'ps'}_{_cnt[0]}", list(shape), dt)))

        ident_bf = sbuf(top, [128, 128], BF16, "identb")
        ident_f = sbuf(top, [128, 128], F32, "identf")
        modT = sbuf(top, [128, 96], F32, "modT")
        G1 = sbuf(top, [128, 16], F32, "G1")
        G2 = sbuf(top, [128, 16], F32, "G2")
        cos_t = sbuf(top, [128, NB * 8], F32, "cos")
        sin_t = sbuf(top, [128, NB * 8], F32, "sin")
        nlam = sbuf(top, [128, 1], F32, "nlam")
        epsb = sbuf(top, [128, 1], F32, "epsb")
        shiftb = sbuf(top, [128, 1], F32, "shiftb")
        gqk = sbuf(top, [128, 512], F32, "gqk")
        subg = sbuf(top, [128, 128], F32, "subg")
        selt = sbuf(top, [128, 4], F32, "selt")
        dma("sp", ident_bf[:], ident_bf_d, writes=[ident_bf])
        dma("sp", ident_f[:], ident_f_d, writes=[ident_f])
        dma("sp", gqk[:], gqk_bc, writes=[gqk])
        dma("sp", subg[:], subln_bc, writes=[subg])
        dma("sp", selt[:], sel, writes=[selt])
        op("dve", lambda: nc.vector.memset(epsb[:], EPS), writes=[epsb])
        op("dve", lambda: nc.vector.memset(shiftb[:], SHIFT), writes=[shiftb])
        op("dve", lambda: nc.vector.tensor_scalar(gqk[:, 0:256], gqk[:, 0:256], 0.125, None, ALU.mult), reads=[gqk], writes=[gqk])
        op("dve", lambda: nc.vector.tensor_scalar(subg[:], subg[:], 1.0 - LAM_INIT, None, ALU.mult), reads=[subg], writes=[subg])

        def precast():
            for k_, src_ in wsrc.items():
                cw_ = wcw[k_]
                sv = src_.rearrange("(i p) n -> p i n", p=128)
                for ch_ in range(src_.shape[1] // cw_):
                    dv = wbf[k_].ap()[ch_ * 128:(ch_ + 1) * 128, :].rearrange("p (i c) -> p i c", c=cw_)
                    dma("pool", dv, sv[:, :, ch_ * cw_:(ch_ + 1) * cw_], writes=[r_wbf[k_]])

        with contextlib.ExitStack() as st:
            cS = sbuf(st, [128, 16], F32)
            wa = [sbuf(st, [128, 16, 512], F32) for _ in range(2)]
            modrow = sbuf(st, [1, 6 * D], F32)
            bT = sbuf(st, [128, 96], F32)
            one11 = sbuf(st, [1, 128], F32)
            n1t = sbuf(st, [128, 16], F32)
            n2t = sbuf(st, [128, 16], F32)
            psr = [psum(st, [128, 512], F32) for _ in range(2)]
            psT = psum(st, [128, 512], F32)
            dma("sp", cS[:], cb, writes=[cS])
            dma("sp", bT[:], badaT, writes=[bT])
            dma("sp", n1t[:], n1g, writes=[n1t])
            dma("sp", n2t[:], n2g, writes=[n2t])
            op("dve", lambda: nc.vector.memset(one11[:], 1.0), writes=[one11])
            op("act", lambda: nc.scalar.activation(out=cS[:], in_=cS[:], func=AF.Silu), reads=[cS], writes=[cS])
            wav = w_ada.rearrange("(i p) n -> p i n", p=128)
            for n in range(24):
                w_ = wa[n % 2]
                dma("sp" if n % 2 == 0 else "act", w_[:], wav[:, :, n * 512:(n + 1) * 512], writes=[w_])
                pr = psr[n % 2]
                for i in range(16):
                    op("pe", lambda i=i, w_=w_, pr=pr: nc.tensor.matmul(pr[0:1, :], cS[:, i:i + 1], w_[:, i, :], start=(i == 0), stop=(i == 15)),
                       reads=[cS, w_], writes=[pr], inc=(i == 15))
                op("dve", lambda n=n, pr=pr: nc.vector.tensor_copy(out=modrow[0:1, n * 512:(n + 1) * 512], in_=pr[0:1, :]), reads=[pr], writes=[modrow])
            for k in range(96):
                op("pe", lambda k=k: nc.tensor.matmul(psT[:, k:k + 1], modrow[0:1, k * 128:(k + 1) * 128], one11[0:1, 0:1], start=True, stop=True),
                   reads=[modrow, one11], writes=[psT], inc=(k == 95))
            op("dve", lambda: nc.vector.tensor_tensor(out=modT[:], in0=psT[:, 0:96], in1=bT[:], op=ALU.add), reads=[psT, bT], writes=[modT])
            op("dve", lambda: nc.vector.scalar_tensor_tensor(out=G1[:], in0=modT[:, 16:32], scalar=1.0, in1=n1t[:], op0=ALU.add, op1=ALU.mult), reads=[modT, n1t], writes=[G1])
            op("dve", lambda: nc.vector.scalar_tensor_tensor(out=G2[:], in0=modT[:, 64:80], scalar=1.0, in1=n2t[:], op0=ALU.add, op1=ALU.mult), reads=[modT, n2t], writes=[G2])
            posi = sbuf(st, [128, NB], I32)
            posf = sbuf(st, [128, NB], F32)
            invf = sbuf(st, [128, 8], F32)
            ang = sbuf(st, [128, NB * 8], F32)
            angi = sbuf(st, [128, NB * 8], I32)
            angf = sbuf(st, [128, NB * 8], F32)
            dma("sp", posi[:], posd, writes=[posi])
            dma("sp", invf[:], invf_bc, writes=[invf])
            op("dve", lambda: nc.vector.tensor_copy(out=posf[:], in_=posi[:]), reads=[posi], writes=[posf])
            op("dve", lambda: nc.vector.tensor_tensor(out=V(ang, 0, [(8, NB), (1, 8)]), in0=V(posf, 0, [(1, NB), (0, 8)]), in1=V(invf, 0, [(0, NB), (1, 8)]), op=ALU.mult),
               reads=[posf, invf], writes=[ang])
            op("dve", lambda: nc.vector.tensor_scalar(ang[:], ang[:], 1.0 / TWO_PI, None, ALU.mult), reads=[ang], writes=[ang])
            op("dve", lambda: nc.vector.tensor_copy(out=angi[:], in_=ang[:]), reads=[ang], writes=[angi])
            op("dve", lambda: nc.vector.tensor_copy(out=angf[:], in_=angi[:]), reads=[angi], writes=[angf])
            op("dve", lambda: nc.vector.tensor_tensor(out=ang[:], in0=ang[:], in1=angf[:], op=ALU.subtract), reads=[ang, angf], writes=[ang])
            op("dve", lambda: nc.vector.tensor_scalar(ang[:], ang[:], TWO_PI, None, ALU.mult), reads=[ang], writes=[ang])
            wtmp = sbuf(st, [128, NB * 8], F32)
            wrap(angf[:], ang[:], 0.0, wtmp[:], angf, ang, wtmp)
            op("act", lambda: nc.scalar.activation(out=sin_t[:], in_=angf[:], func=AF.Sin), reads=[angf], writes=[sin_t])
            wrap(angf[:], ang[:], math.pi / 2, wtmp[:], angf, ang, wtmp)
            op("act", lambda: nc.scalar.activation(out=cos_t[:], in_=angf[:], func=AF.Sin), reads=[angf], writes=[cos_t])
            lv = sbuf(st, [1, 256], F32)
            lp = sbuf(st, [1, 128], F32)
            ls = sbuf(st, [1, 2], F32)
            dma("sp", lv[:], lamv, writes=[lv])
            op("dve", lambda: nc.vector.tensor_tensor(out=V(lp, 0, [(64, 2), (1, 64)], 0, 1), in0=V(lv, 0, [(128, 2), (1, 64)], 0, 1), in1=V(lv, 64, [(128, 2), (1, 64)], 0, 1), op=ALU.mult),
               reads=[lv], writes=[lp])
            op("dve", lambda: nc.vector.tensor_reduce(out=ls[:], in_=V(lp, 0, [(64, 2), (1, 64)], 0, 1), axis=AX.X, op=ALU.add), reads=[lp], writes=[ls])
            op("act", lambda: nc.scalar.activation(out=ls[:], in_=ls[:], func=AF.Exp), reads=[ls], writes=[ls])
            op("dve", lambda: nc.vector.tensor_tensor(out=ls[0:1, 0:1], in0=ls[0:1, 1:2], in1=ls[0:1, 0:1], op=ALU.subtract), reads=[ls], writes=[ls])
            op("dve", lambda: nc.vector.tensor_scalar(ls[0:1, 0:1], ls[0:1, 0:1], -LAM_INIT, None, ALU.add), reads=[ls], writes=[ls])
            op("pe", lambda: nc.tensor.matmul(psT[:, 100:101], one11[0:1, :], ls[0:1, 0:1], start=True, stop=True), reads=[one11, ls, modT], writes=[psT])
            op("dve", lambda: nc.vector.tensor_copy(out=nlam[:], in_=psT[:, 100:101]), reads=[psT], writes=[nlam])
            kb.barrier()

        def S1(i):
            return modT[:, i:i + 1]

        def norm_block(xt, rstd_col, junk, xn, psTb, hT, col0, Gm, s_off, ssq_col, rt_col):
            op("act", lambda: nc.scalar.activation(out=junk[:], in_=xt[:], func=AF.Square, accum_out=ssq_col[0][:, ssq_col[1]:ssq_col[1] + 1]),
               reads=[xt], writes=[junk, ssq_col[0]])
            op("act", lambda: nc.scalar.activation(out=rt_col[0][:, rt_col[1]:rt_col[1] + 1], in_=ssq_col[0][:, ssq_col[1]:ssq_col[1] + 1], func=AF.Sqrt, scale=1.0 / D, bias=epsb[:]),
               reads=[ssq_col[0], epsb], writes=[rt_col[0]])
            op("dve", lambda: nc.vector.reciprocal(out=rstd_col[0][:, rstd_col[1]:rstd_col[1] + 1], in_=rt_col[0][:, rt_col[1]:rt_col[1] + 1]),
               reads=[rt_col[0]], writes=[rstd_col[0]])
            op("act", lambda: nc.scalar.activation(out=xn[:], in_=xt[:], func=AF.Identity, scale=rstd_col[0][:, rstd_col[1]:rstd_col[1] + 1]),
               reads=[xt, rstd_col[0]], writes=[xn])
            for half in range(2):
                ph_ = psTb[half]
                for i8 in range(8):
                    i = half * 8 + i8
                    op("pe", lambda i=i, i8=i8, ph_=ph_: nc.tensor.transpose(ph_[:, i8 * 128:(i8 + 1) * 128], xn[:, i * 128:(i + 1) * 128], ident_bf[:]),
                       reads=[xn, ident_bf], writes=[ph_], inc=(i8 == 7))
                for i8 in range(8):
                    i = half * 8 + i8
                    if half == 0:
                        op("act", lambda i=i, i8=i8, ph_=ph_: nc.scalar.activation(out=hT[:, i, col0:col0 + 128], in_=ph_[:, i8 * 128:(i8 + 1) * 128], func=AF.Identity,
                                                                                  scale=Gm[:, i:i + 1], bias=modT[:, s_off + i:s_off + i + 1]),
                           reads=[ph_, Gm, modT], writes=[hT.r])
                    else:
                        op("dve", lambda i=i, i8=i8, ph_=ph_: nc.vector.tensor_scalar(hT[:, i, col0:col0 + 128], ph_[:, i8 * 128:(i8 + 1) * 128], Gm[:, i:i + 1],
                                                                                     modT[:, s_off + i:s_off + i + 1], ALU.mult, ALU.add),
                           reads=[ph_, Gm, modT], writes=[hT.r2])

        for st in _phase(upto >= 1):
            whyb = sbuf(st, [128, 16, 768], BF16)
            wqkvb = sbuf(st, [128, 16, 768], BF16)
            dma("pool", whyb[:], w_hy.rearrange("(i p) n -> p i n", p=128), writes=[whyb])
            dma("pool", wqkvb[:], w_qkv.rearrange("(i p) n -> p i n", p=128), writes=[wqkvb])
            wob = sbuf(st, [64, 1024], BF16)
            dma("pool", wob[:], fwout, writes=[wob])
            precast()
            zt = sbuf(st, [128, 2], BF16)
            op("dve", lambda: nc.vector.memset(zt[:], 0.0), writes=[zt])
            for cbk in range(6):
                dma("sp", bass.AP(tensor=hy_pre, offset=cbk * 128 * (L + 2), ap=[[L + 2, 128], [L + 1, 2], [1, 1]]), V(zt, 0, [(1, 2), (1, 1)]), reads=[zt], writes=[r_hy_pre], allow_slow_non_contiguous=True)
            xts = [sbuf(st, [128, D], F32) for _ in range(3)]
            xns = [sbuf(st, [128, D], BF16) for _ in range(2)]
            junk = sbuf(st, [128, D], BF16)
            hTs = [sbuf(st, [128, 16, 512], BF16) for _ in range(2)]
            ssq = sbuf(st, [128, NB], F32)
            rt = sbuf(st, [128, NB], F32)
            rstd = sbuf(st, [128, NB], F32)
            hst = [sbuf(st, [128, 512], BF16) for _ in range(2)]
            sqj = sbuf(st, [128, 512], F32)
            ss8 = sbuf(st, [128, 8], F32)
            rt8 = sbuf(st, [128, 8], F32)
            rs8 = sbuf(st, [128, 8], F32)
            qn = sbuf(st, [128, 512], F32)
            rtmp = sbuf(st, [128, 4 * 64], F32)
            qbf = sbuf(st, [128, 512], BF16)
            qst = [sbuf(st, [128, 4, 512], BF16) for _ in range(2)]
            vst = [sbuf(st, [128, 4, 256], BF16) for _ in range(2)]
            psTb = [psum(st, [128, 1024], BF16) for _ in range(2)]
            psH = [psum(st, [128, 512], F32) for _ in range(2)]
            psQ0s = [psum(st, [128, 512], F32) for _ in range(2)]
            psQ1s = [psum(st, [128, 512], F32) for _ in range(2)]
            w1t = sbuf(st, [33, 64], F32)
            w2t = sbuf(st, [64, 64], F32)
            w3t = sbuf(st, [64, 64], F32)
            fbt = sbuf(st, [64, 3], F32)
            fft = sbuf(st, [64, 3], F32)
            fbs = sbuf(st, [64, 3], F32)
            ndl = sbuf(st, [128, 2], F32)
            hbT = sbuf(st, [128, 4], F32)
            fsets = [dict(zl=sbuf(st, [33, 512], F32), s3=sbuf(st, [64, 512], BF16), aa=[sbuf(st, [64, 512], F32) for _ in range(2)],
                          wtm=sbuf(st, [64, 512], F32), tg=sbuf(st, [128, 512], F32), dec=[sbuf(st, [128, 512], F32) for _ in range(2)],
                          k2t=[sbuf(st, [128, 512], BF16) for _ in range(2)]) for _ in range(2)]
            pmf = psH[1]
            dma("sp", w1t[:], fw1, writes=[w1t])
            dma("sp", w2t[:], fw2, writes=[w2t])
            dma("sp", w3t[:], fw3, writes=[w3t])
            dma("sp", fbt[:], fbias, writes=[fbt])
            dma("sp", fft[:], ffreq, writes=[fft])
            dma("sp", ndl[:], ndelta, writes=[ndl])
            dma("sp", hbT[:], hybT_d, writes=[hbT])
            op("dve", lambda: nc.vector.tensor_tensor(out=fbs[:], in0=fbt[:], in1=fft[:], op=ALU.mult), reads=[fbt, fft], writes=[fbs])

            def filt_gen(ti, fs):
                rev = ti >= 32
                tl = ti % 32
                o = 0 if rev else 1
                zl, s3, tgt, wtm = fs["zl"], fs["s3"], fs["tg"], fs["wtm"]
                dma("sp", zl[:], zT2[:, ti * 512:(ti + 1) * 512], writes=[zl])
                dma("sp", tgt[:], bass.AP(tensor=tgrid.tensor, offset=ti * 512, ap=[[0, 128], [1, 512]]), writes=[tgt])
                cur = None
                for l_, wl in enumerate([w1t, w2t, w3t]):
                    a_ = fs["aa"][l_ % 2]
                    if l_ == 0:
                        op("pe", lambda: nc.tensor.matmul(pmf[0:64, :], w1t[:], zl[:], start=True, stop=True), reads=[w1t, zl], writes=[pmf])
                    else:
                        op("pe", lambda: nc.tensor.matmul(pmf[0:64, :], wl[:], cur[:], start=True, stop=True), reads=[wl, cur], writes=[pmf])
                    op("act", lambda: nc.scalar.activation(out=a_[:], in_=pmf[0:64, :], func=AF.Identity, scale=fft[:, l_:l_ + 1], bias=fbs[:, l_:l_ + 1]),
                       reads=[pmf, fft, fbs], writes=[a_])
                    wrap(a_[:], a_[:], 0.0, wtm[:], a_, a_, wtm)
                    if l_ < 2:
                        op("act", lambda: nc.scalar.activation(out=a_[:], in_=a_[:], func=AF.Sin), reads=[a_], writes=[a_])
                        cur = a_
                    else:
                        op("act", lambda: nc.scalar.activation(out=s3[:], in_=a_[:], func=AF.Sin), reads=[a_], writes=[s3])
                    yield
                rdir = (1 if tl < 16 else 0) if not rev else (0 if tl < 16 else 1)
                for cbk in range(2):
                    dc = fs["dec"][cbk]
                    kt = fs["k2t"][cbk]
                    op("act", lambda: nc.scalar.activation(out=dc[:], in_=tgt[:], func=AF.Exp, scale=ndl[:, cbk:cbk + 1]), reads=[tgt, ndl], writes=[dc])
                    c0 = o * 512 + rdir * 256 + cbk * 128
                    op("pe", lambda: nc.tensor.matmul(pmf[:], wob[:, c0:c0 + 128], s3[:], start=True, stop=True), reads=[wob, s3], writes=[pmf])
                    op("dve", lambda: nc.vector.tensor_tensor(out=kt[:], in0=pmf[:], in1=dc[:], op=ALU.mult), reads=[pmf, dc], writes=[kt])
                    zc = (0 if tl == 0 else None) if not rev else (511 if tl == 31 else None)
                    bc_ = (0 if tl == 16 else None) if not rev else (511 if tl == 15 else None)
                    if zc is not None:
                        op("dve", lambda: nc.vector.memset(kt[:, zc:zc + 1], 0.0), writes=[kt])
                    if bc_ is not None:
                        op("dve", lambda: nc.vector.tensor_scalar(kt[:, bc_:bc_ + 1], kt[:, bc_:bc_ + 1], hbT[:, o * 2 + cbk:o * 2 + cbk + 1], None, ALU.add),
                           reads=[kt, hbT], writes=[kt])
                    row0 = o * 256 + cbk * 128
                    dma("sp", k2[row0:row0 + 128, tl * 512:(tl + 1) * 512], kt[:], reads=[kt], writes=[r_k2])
                    yield

            fq = {"next": 0, "gens": [None, None], "turn": 0}

            def filt_step(n=1):
                for _ in range(n):
                    k_ = fq["turn"]
                    fq["turn"] = 1 - k_
                    for _try in range(2):
                        if fq["gens"][k_] is None:
                            if fq["next"] >= 64:
                                break
                            fq["gens"][k_] = filt_gen(fq["next"], fsets[k_])
                            fq["next"] += 1
                        try:
                            next(fq["gens"][k_])
                            break
                        except StopIteration:
                            fq["gens"][k_] = None

            def load_x(r):
                if r < NB:
                    dma("act", xts[r % 3][:], xb[r * 128:(r + 1) * 128, :], writes=[xts[r % 3]])

            load_x(0)
            load_x(1)
            for tt in range(16):
                hT = hTs[tt % 2]
                for s in range(4):
                    r = tt * 4 + s
                    xt = xts[r % 3]
                    load_x(r + 2)
                    norm_block(xt, (rstd, r), junk, xns[r % 2], psTb, hT, s * 128, G1, 0, (ssq, r), (rt, r))
                for cbk in range(6):
                    ph = psH[0]
                    for i in range(16):
                        op("pe", lambda i=i, cbk=cbk, ph=ph: nc.tensor.matmul(ph[:], whyb[:, i, cbk * 128:(cbk + 1) * 128], hT[:, i, :], start=(i == 0), stop=(i == 15)),
                           reads=[whyb, hT, hT.r2], writes=[ph], inc=(i == 15))
                    hs = hst[cbk % 2]
                    if cbk % 2 == 0:
                        op("act", lambda ph=ph, hs=hs: nc.scalar.copy(out=hs[:], in_=ph[:]), reads=[ph], writes=[hs])
                    else:
                        op("dve", lambda ph=ph, hs=hs: nc.vector.tensor_copy(out=hs[:], in_=ph[:]), reads=[ph], writes=[hs])
                    dma("sp", hy_pre[cbk * 128:(cbk + 1) * 128, 1 + tt * 512:1 + (tt + 1) * 512], hs[:], reads=[hs], writes=[r_hy_pre])
                    filt_step(2)
                qs = qst[tt % 2]
                vs = vst[tt % 2]
                def qkv_mm(s):
                    r = tt * 4 + s
                    psQ0 = psQ0s[s % 2]
                    psQ1 = psQ1s[s % 2]
                    for i in range(16):
                        op("pe", lambda i=i, s=s, psQ0=psQ0: nc.tensor.matmul(psQ0[:], hT[:, i, s * 128:(s + 1) * 128], wqkvb[:, i, 0:512], start=(i == 0), stop=(i == 15)),
                           reads=[hT, hT.r2, wqkvb], writes=[psQ0], inc=(i == 15))
                    for i in range(16):
                        op("pe", lambda i=i, s=s, psQ1=psQ1: nc.tensor.matmul(psQ1[:, 0:256], hT[:, i, s * 128:(s + 1) * 128], wqkvb[:, i, 512:768], start=(i == 0), stop=(i == 15)),
                           reads=[hT, hT.r2, wqkvb], writes=[psQ1], inc=(i == 15))
                    op("act", lambda s=s, psQ1=psQ1: nc.scalar.copy(out=vs[:, s, :], in_=psQ1[:, 0:256]), reads=[psQ1], writes=[vs])
                    filt_step(2)

                def qkv_post(s):
                    r = tt * 4 + s
                    psQ0 = psQ0s[s % 2]
                    op("act", lambda psQ0=psQ0: nc.scalar.activation(out=sqj[:], in_=psQ0[:], func=AF.Square), reads=[psQ0], writes=[sqj])
                    op("dve", lambda: nc.vector.tensor_reduce(out=ss8[:], in_=V(sqj, 0, [(64, 8), (1, 64)]), axis=AX.X, op=ALU.add), reads=[sqj], writes=[ss8])
                    op("act", lambda: nc.scalar.activation(out=rt8[:], in_=ss8[:], func=AF.Sqrt, scale=1.0 / 64, bias=epsb[:]), reads=[ss8, epsb], writes=[rt8])
                    op("dve", lambda: nc.vector.reciprocal(out=rs8[:], in_=rt8[:]), reads=[rt8], writes=[rs8])
                    op("dve", lambda psQ0=psQ0: nc.vector.tensor_tensor(out=V(qn, 0, [(64, 8), (1, 64)]), in0=V(psQ0, 0, [(64, 8), (1, 64)]), in1=V(rs8, 0, [(1, 8), (0, 64)]), op=ALU.mult),
                       reads=[psQ0, rs8], writes=[qn])
                    op("dve", lambda: nc.vector.tensor_tensor(out=qn[:], in0=qn[:], in1=gqk[:], op=ALU.mult), reads=[qn, gqk], writes=[qn])
                    x1 = V(qn, 0, [(64, 8), (1, 8)])
                    x2 = V(qn, 8, [(64, 8), (1, 8)])
                    cs = V(cos_t, r * 8, [(0, 8), (1, 8)])
                    sn = V(sin_t, r * 8, [(0, 8), (1, 8)])
                    tv = lambda k: V(rtmp, k * 64, [(8, 8), (1, 8)])
                    op("dve", lambda: nc.vector.tensor_tensor(out=tv(0), in0=x1, in1=cs, op=ALU.mult), reads=[qn, cos_t], writes=[rtmp])
                    op("dve", lambda: nc.vector.tensor_tensor(out=tv(1), in0=x2, in1=sn, op=ALU.mult), reads=[qn, sin_t], writes=[rtmp])
                    op("dve", lambda: nc.vector.tensor_tensor(out=tv(2), in0=x2, in1=cs, op=ALU.mult), reads=[qn, cos_t], writes=[rtmp])
                    op("dve", lambda: nc.vector.tensor_tensor(out=tv(3), in0=x1, in1=sn, op=ALU.mult), reads=[qn, sin_t], writes=[rtmp])
                    op("dve", lambda: nc.vector.tensor_tensor(out=x1, in0=tv(0), in1=tv(1), op=ALU.subtract), reads=[rtmp], writes=[qn])
                    op("dve", lambda: nc.vector.tensor_tensor(out=x2, in0=tv(2), in1=tv(3), op=ALU.add), reads=[rtmp], writes=[qn])
                    op("dve", lambda: nc.vector.tensor_copy(out=qbf[:], in_=qn[:]), reads=[qn], writes=[qbf])
                    pt = psTb[r % 2]
                    for blk in range(4):
                        op("pe", lambda blk=blk, pt=pt: nc.tensor.transpose(pt[:, blk * 128:(blk + 1) * 128], qbf[:, blk * 128:(blk + 1) * 128], ident_bf[:]),
                           reads=[qbf, ident_bf], writes=[pt], inc=(blk == 3))
                    op("dve", lambda s=s, pt=pt: nc.vector.tensor_copy(out=qs[:, :, s * 128:(s + 1) * 128], in_=V(pt, 0, [(128, 4), (1, 128)])), reads=[pt], writes=[qs])

                qkv_mm(0)
                for s in range(4):
                    if s + 1 < 4:
                        qkv_mm(s + 1)
                    qkv_post(s)
                dma("sp", qkT_d.ap().rearrange("(k p) n -> p k n", p=128)[:, :, tt * 512:(tt + 1) * 512], qs[:], reads=[qs], writes=[r_qkT])
                dma("sp", vtok.ap().rearrange("(r p) c -> p r c", p=128)[:, tt * 4:(tt + 1) * 4, :], vs[:], reads=[vs], writes=[r_vtok])
            kb.barrier()

        for st in _phase(upto >= 2):
            cwt = sbuf(st, [128, 18], F32)
            cbt = sbuf(st, [128, 6], F32)
            jrev = sbuf(st, [128, 128], BF16)
            dma("sp", cwt[:], convw, writes=[cwt])
            dma("sp", cbt[:], convb, writes=[cbt])
            dma("sp", jrev[:], jrev_d, writes=[jrev])
            Ut = sbuf(st, [128, NB, 128], BF16)
            X1t = sbuf(st, [128, NB, 128], BF16)
            X2t = sbuf(st, [128, NB, 128], BF16)
            Zt = sbuf(st, [128, NB, 128], BF16)
            X1r = sbuf(st, [128, NB, 128], BF16)
            for g in range(2):
                with contextlib.ExitStack() as s2:
                    pre = sbuf(s2, [128, L + 2], BF16)
                    acc = sbuf(s2, [128, L], F32)
                    hc = sbuf(s2, [128, L], BF16)
                    psb = [psum(s2, [128, 1024], BF16) for _ in range(2)]
                    for si, dst in enumerate([Ut, X1t, X2t]):
                        cbk = si * 2 + g
                        dma("sp", pre[:], hy_pre[cbk * 128:(cbk + 1) * 128, :], reads=[r_hy_pre], writes=[pre])
                        for hh in range(4):
                            a0, a1 = hh * 2048, (hh + 1) * 2048
                            op("dve", lambda a0=a0, a1=a1, cbk=cbk: nc.vector.tensor_scalar(acc[:, a0:a1], pre[:, 1 + a0:1 + a1], cwt[:, cbk * 3 + 1:cbk * 3 + 2], cbt[:, cbk:cbk + 1], ALU.mult, ALU.add),
                               reads=[pre, cwt, cbt], writes=[acc])
                            op("dve", lambda a0=a0, a1=a1, cbk=cbk: nc.vector.scalar_tensor_tensor(out=acc[:, a0:a1], in0=pre[:, a0:a1], scalar=cwt[:, cbk * 3:cbk * 3 + 1], in1=acc[:, a0:a1], op0=ALU.mult, op1=ALU.add),
                               reads=[pre, cwt, acc], writes=[acc])
                            op("dve", lambda a0=a0, a1=a1, cbk=cbk: nc.vector.scalar_tensor_tensor(out=hc[:, a0:a1], in0=pre[:, 2 + a0:2 + a1], scalar=cwt[:, cbk * 3 + 2:cbk * 3 + 3], in1=acc[:, a0:a1], op0=ALU.mult, op1=ALU.add),
                               reads=[pre, cwt, acc], writes=[hc])
                        for j8 in range(8):
                            pb = psb[j8 % 2]
                            for jj in range(8):
                                j = j8 * 8 + jj
                                op("pe", lambda j=j, jj=jj, pb=pb: nc.tensor.transpose(pb[:, jj * 128:(jj + 1) * 128], hc[:, j * 128:(j + 1) * 128], ident_bf[:]),
                                   reads=[hc, ident_bf], writes=[pb], inc=(jj == 7))
                            op("act", lambda j8=j8, pb=pb, dst=dst: nc.scalar.copy(out=dst[:, j8 * 8:(j8 + 1) * 8, :], in_=V(pb, 0, [(128, 8), (1, 128)])), reads=[pb], writes=[dst])
                    psr_ = [psum(s2, [128, 512], F32) for _ in range(2)]
                    for j4 in range(16):
                        pr_ = psr_[j4 % 2]
                        op("pe", lambda j4=j4, pr_=pr_: nc.tensor.matmul(pr_[:], jrev[:], V(X1t, j4 * 512, [(1, 512)]), start=True, stop=True), reads=[jrev, X1t], writes=[pr_])
                        op("dve", lambda j4=j4, pr_=pr_: nc.vector.tensor_copy(out=V(X1r, j4 * 512, [(1, 512)]), in_=pr_[:]), reads=[pr_], writes=[X1r])
                    kb.barrier()
                with contextlib.ExitStack() as s2:
                    Tt = [sbuf(s2, [128, 127 * 128], BF16) for _ in range(2)]
                    psY = [psum(s2, [128, 512], F32) for _ in range(2)]
                    psb = [psum(s2, [128, 1024], BF16) for _ in range(2)]
                    psq = [psum(s2, [128, 512], BF16) for _ in range(2)]
                    yst = [sbuf(s2, [128, 1024], BF16) for _ in range(2)]
                    ysb = [sbuf(s2, [64, 512], BF16) for _ in range(2)]
                    Upc = [sbuf(s2, [128, 190], BF16) for _ in range(2)]
                    for u_ in Upc:
                        op("dve", lambda u_=u_: nc.vector.memset(u_[:], 0.0), writes=[u_])
                    for o in range(2):
                        Uin = Ut if o == 0 else Zt
                        Xm = X1r if o == 0 else X2t
                        Zo = Zt if o == 0 else Ut
                        for c in range(128):
                            tt_ = Tt[c % 2]
                            up = Upc[c % 2]
                            row = o * 256 + g * 128 + c
                            for hq in range(2):
                                q0 = hq * 8128
                                dma("sp" if hq == 0 else "act", tt_[:, q0:q0 + 8128],
                                    bass.AP(tensor=k2, offset=row * 2 * L + o + q0, ap=[[1, 128], [1, 8128]]), reads=[r_k2], writes=[tt_])
                            op("dve", lambda up=up, Uin=Uin, c=c: nc.vector.tensor_copy(out=up[:, 63:127], in_=V(Uin, c, [(128, 64)])), reads=[Uin], writes=[up])
                            py = psY[(c // 4) % 2]
                            sl = (c % 4) * 128
                            order = [0] + [d for k_ in range(1, 64) for d in (k_, -k_)]
                            for n_, d in enumerate(order):
                                qd = (63 - d) * 128 if o == 0 else (d + 63) * 128
                                op("pe", lambda d=d, tt_=tt_, py=py, sl=sl, n_=n_, qd=qd, up=up: nc.tensor.matmul(
                                    py[0:64, sl:sl + 128], up[:, 63 - d:127 - d], tt_[:, qd:qd + 128], start=(n_ == 0), stop=(n_ == 126)),
                                   reads=[tt_, up], writes=[py], inc=(n_ == 126))
                            if c % 4 == 3:
                                c0 = c - 3
                                yb_ = ysb[(c // 4) % 2]
                                pq = psq[(c // 4) % 2]
                                op("act", lambda py=py, yb_=yb_: nc.scalar.copy(out=yb_[:], in_=py[0:64, :]), reads=[py], writes=[yb_])
                                for k4 in range(4):
                                    op("pe", lambda k4=k4, yb_=yb_, pq=pq: nc.tensor.transpose(pq[:, k4 * 64:(k4 + 1) * 64], yb_[:, k4 * 128:(k4 + 1) * 128], ident_bf[0:64, 0:64]),
                                       reads=[yb_, ident_bf], writes=[pq], inc=(k4 == 3))
                                op("dve", lambda pq=pq, c0=c0, Xm=Xm, Zo=Zo: nc.vector.tensor_tensor(out=V(Zo, c0, [(1, 4), (128, 64)]), in0=V(pq, 0, [(64, 4), (1, 64)]),
                                                                                             in1=V(Xm, c0, [(1, 4), (128, 64)]), op=ALU.mult),
                                   reads=[pq, Xm], writes=[Zo])
                    for j8 in range(8):
                        pb = psb[j8 % 2]
                        ys = yst[j8 % 2]
                        for jj in range(8):
                            j = j8 * 8 + jj
                            op("pe", lambda j=j, jj=jj, pb=pb: nc.tensor.transpose(pb[:, jj * 128:(jj + 1) * 128], Ut[:, j, :], ident_bf[:]),
                               reads=[Ut, ident_bf], writes=[pb], inc=(jj == 7))
                        op("act", lambda pb=pb, ys=ys: nc.scalar.copy(out=ys[:], in_=pb[:]), reads=[pb], writes=[ys])
                        dma("sp", ag_in[j8][g * 128:(g + 1) * 128, :], ys[:], reads=[ys], writes=[r_agin])
                    kb.barrier()
            kb.barrier()

        for st in _phase(upto >= 3):
            QZ = [[sbuf(st, [128, L], BF16) for _ in range(2)] for _ in range(2)]
            KT = [sbuf(st, [128, L], BF16) for _ in range(2)]
            Va = sbuf(st, [128, NB, 2, 129], BF16)
            for h in range(2):
                for comp in range(2):
                    oc = 1 - comp
                    op("dve", lambda h=h, comp=comp, oc=oc: nc.vector.memset(QZ[h][comp][oc * 64:(oc + 1) * 64, :], 0.0), writes=[QZ[h][comp]])
                    dma("sp", QZ[h][comp][comp * 64:(comp + 1) * 64, :], qkT_d[h * 128 + comp * 64:h * 128 + (comp + 1) * 64, :], reads=[r_qkT], writes=[QZ[h][comp]])
                dma("act", KT[h][:], qkT_d[(2 + h) * 128:(3 + h) * 128, :], reads=[r_qkT], writes=[KT[h]])
            op("pool", lambda: nc.gpsimd.memset(Va[:], 1.0), writes=[Va])
            vv = vtok.ap().rearrange("(r p) c -> p r c", p=128)
            for h in range(2):
                dma("sp", Va[:, :, h, 0:128], vv[:, :, h * 128:(h + 1) * 128], reads=[r_vtok], writes=[Va])
            Pt = [sbuf(st, [128, 1024], BF16) for _ in range(3)]
            O0 = [sbuf(st, [128, 128], F32) for _ in range(4)]
            Dd = [sbuf(st, [128, 128], F32) for _ in range(4)]
            rz = sbuf(st, [128, 8], F32)
            sj = sbuf(st, [128, 128], F32)
            sq1 = sbuf(st, [128, 8], F32)
            yab = [sbuf(st, [128, 128], BF16) for _ in range(2)]
            yaT = [sbuf(st, [128, 512], BF16) for _ in range(2)]
            psS = [psum(st, [128, 1024], F32) for _ in range(2)]
            psOb = [psum(st, [128, 512], F32) for _ in range(2)]
            psA = psum(st, [128, 1024], BF16)
            it = 0
            for h in range(2):
                for qb in range(16):
                    for comp in range(2):
                        p0 = comp * 64

                        def acc(s4):
                            return psOb[s4 // 2], (s4 % 2) * 256

                        def qk2(p, h=h, qb=qb, comp=comp):
                            ps = psS[p % 2]
                            for hf in range(2):
                                kbk = 2 * p + hf
                                op("pe", lambda: nc.tensor.matmul(ps[:, hf * 512:(hf + 1) * 512], KT[h][:, kbk * 128:(kbk + 1) * 128], QZ[h][comp][:, qb * 512:(qb + 1) * 512], start=True, stop=True),
                                   reads=[KT[h], QZ[h][comp]], writes=[ps], inc=(hf == 1))

                        qk2(0)
                        for p in range(NB // 2):
                            if p + 1 < NB // 2:
                                qk2(p + 1)
                            ps = psS[p % 2]
                            pt = Pt[p % 3]
                            op("act", lambda: nc.scalar.activation(out=pt[:], in_=ps[:], func=AF.Exp, bias=shiftb[:]), reads=[ps, shiftb], writes=[pt])
                            for hf in range(2):
                                kbk = 2 * p + hf
                                for s4 in range(4):
                                    pb_, c0_ = acc(s4)
                                    op("pe", lambda: nc.tensor.matmul(pb_[:, c0_:c0_ + 129], pt[:, hf * 512 + s4 * 128:hf * 512 + (s4 + 1) * 128], Va[:, kbk, h, :],
                                                                      start=(kbk == 0 and s4 % 2 == 0), stop=(kbk == NB - 1), skip_group_check=True),
                                       reads=[pt, Va], writes=[pb_], inc=(hf == 1 and s4 == 3))
                        for s4 in range(4):
                            rc = rz[:, s4:s4 + 1]
                            pb_, c0_ = acc(s4)
                            op("dve", lambda: nc.vector.reciprocal(out=rc, in_=pb_[:, c0_ + 128:c0_ + 129]), reads=[pb_], writes=[rz])
                            if comp == 0:
                                op("dve", lambda: nc.vector.tensor_scalar(O0[s4][:], pb_[:, c0_:c0_ + 128], rc, None, ALU.mult), reads=[pb_, rz], writes=[O0[s4]])
                            else:
                                op("dve", lambda: nc.vector.tensor_tensor(out=rc, in0=rc, in1=nlam[:], op=ALU.mult), reads=[rz, nlam], writes=[rz])
                                op("dve", lambda: nc.vector.scalar_tensor_tensor(out=Dd[s4][:], in0=pb_[:, c0_:c0_ + 128], scalar=rc, in1=O0[s4][:], op0=ALU.mult, op1=ALU.add),
                                   reads=[pb_, rz, O0[s4]], writes=[Dd[s4]])
                    yt = yaT[it % 2]
                    it += 1
                    for s4 in range(4):
                        yb = yab[s4 % 2]
                        op("act", lambda s4=s4: nc.scalar.activation(out=sj[:], in_=Dd[s4][:], func=AF.Square, accum_out=sq1[:, s4:s4 + 1]), reads=[Dd[s4]], writes=[sj, sq1])
                        op("act", lambda s4=s4: nc.scalar.activation(out=sq1[:, 4 + s4:5 + s4], in_=sq1[:, s4:s4 + 1], func=AF.Sqrt, scale=1.0 / 128, bias=epsb[:]), reads=[sq1, epsb], writes=[sq1])
                        op("dve", lambda s4=s4: nc.vector.reciprocal(out=sq1[:, 4 + s4:5 + s4], in_=sq1[:, 4 + s4:5 + s4]), reads=[sq1], writes=[sq1])
                        op("dve", lambda s4=s4, yb=yb: nc.vector.scalar_tensor_tensor(out=yb[:], in0=Dd[s4][:], scalar=sq1[:, 4 + s4:5 + s4], in1=subg[:], op0=ALU.mult, op1=ALU.mult),
                           reads=[Dd[s4], sq1, subg], writes=[yb])
                        op("pe", lambda s4=s4, yb=yb: nc.tensor.transpose(psA[:, s4 * 128:(s4 + 1) * 128], yb[:], ident_bf[:]), reads=[yb, ident_bf], writes=[psA])
                    op("act", lambda yt=yt: nc.scalar.copy(out=yt[:], in_=psA[:, 0:512]), reads=[psA], writes=[yt])
                    dma("sp", ag_in[qb // 2][256 + h * 128:256 + (h + 1) * 128, (qb % 2) * 512:(qb % 2 + 1) * 512], yt[:], reads=[yt], writes=[r_agin])
            kb.barrier()

        if upto >= 4:
            for i in range(8):
                op("pool", lambda i=i: nc.gpsimd.collective_compute("AllGather", ALU.bypass, replica_groups=[[0, 1, 2, 3], [4, 5, 6, 7]],
                                                                    ins=[ag_in[i].ap().opt()], outs=[ag_out[i].ap().opt()]), reads=[r_agin], writes=[r_agout])

        for st in _phase(upto >= 5):
            xs = [sbuf(st, [128, D], F32) for _ in range(4)]
            xns = [sbuf(st, [128, D], BF16) for _ in range(1)] * 2
            junk = sbuf(st, [128, D], BF16)
            hT = sbuf(st, [128, 16, 512], BF16)
            ssq = sbuf(st, [128, 32], F32)
            rt = sbuf(st, [128, 32], F32)
            rstd = sbuf(st, [128, 32], F32)
            wb = [sbuf(st, [128, 16, 256], BF16) for _ in range(3)]
            rT = [sbuf(st, [128, 512], F32) for _ in range(2)]
            psTb = [psum(st, [128, 1024], BF16) for _ in range(2)]
            psM = [psum(st, [128, 512], F32) for _ in range(5)]
            psX = psum(st, [128, 512], F32)

            def back_to_tokens(pm, gcol, dblk, tix):
                rt_ = rT[tix % 2]
                op("act", lambda: nc.scalar.activation(out=rt_[:], in_=pm[:], func=AF.Identity, scale=modT[:, gcol + dblk:gcol + dblk + 1]), reads=[pm, modT], writes=[rt_])
                for s in range(4):
                    op("pe", lambda s=s: nc.tensor.transpose(psX[:, s * 128:(s + 1) * 128], rt_[:, s * 128:(s + 1) * 128], ident_f[:]), reads=[rt_, ident_f], writes=[psX], inc=(s == 3))
                for s in range(4):
                    op("dve", lambda s=s: nc.vector.tensor_tensor(out=xs[s][:, dblk * 128:(dblk + 1) * 128], in0=xs[s][:, dblk * 128:(dblk + 1) * 128], in1=psX[:, s * 128:(s + 1) * 128], op=ALU.add),
                       reads=[psX, xs[s]], writes=[xs[s]])

            wcnt = [0]

            def load_w(name, c0, ncols, kblocks=16):
                w_ = wb[wcnt[0] % 3]
                q_ = "sp" if wcnt[0] % 2 == 0 else "act"
                wcnt[0] += 1
                ch_ = c0 // 256
                dma(q_, w_[:, 0:kblocks, 0:ncols], wbf[name].ap()[ch_ * 128:(ch_ + 1) * 128, :].rearrange("p (i c) -> p i c", c=256), reads=[r_wbf[name]], writes=[w_])
                return w_

            tix = 0
            for tt in range(4):
                for s in range(4):
                    dma("sp", xs[s][:], xo[(tt * 4 + s) * 128:(tt * 4 + s + 1) * 128, :], writes=[xs[s]])
                    r = tt * 4 + s
                    norm_block(xs[s], (rstd, r), junk, xns[s % 2], psTb, hT, s * 128, G1, 0, (ssq, r), (rt, r))
                with contextlib.ExitStack() as s2:
                    ghT = sbuf(s2, [128, 32, 512], BF16)
                    Yh = sbuf(s2, [128, 8, 512], BF16)
                    Ya = sbuf(s2, [128, 8, 512], BF16)
                    yld = [sbuf(s2, [128, 8, 512], BF16)] * 2
                    mT = sbuf(s2, [128, 16, 512], BF16)
                    t1 = sbuf(s2, [128, 512], F32)
                    t2 = sbuf(s2, [128, 512], F32)
                    for n8 in range(16):
                        w_ = load_w("w_g", n8 * 256, 256)
                        for fb in range(2):
                            pm = psM[(n8 * 2 + fb) % 3]
                            for i in range(16):
                                op("pe", lambda i=i, fb=fb, pm=pm, w_=w_: nc.tensor.matmul(pm[:], w_[:, i, fb * 128:(fb + 1) * 128], hT[:, i, :], start=(i == 0), stop=(i == 15)),
                                   reads=[w_, hT, hT.r2], writes=[pm], inc=(i == 15))
                            op("act", lambda pm=pm, n8=n8, fb=fb: nc.scalar.activation(out=ghT[:, n8 * 2 + fb, :], in_=pm[:], func=AF.Sigmoid), reads=[pm], writes=[ghT])
                    for which, dst in ((0, Yh), (1, Ya)):
                        for q in range(4):
                            yl = yld[q % 2]
                            for rr in range(4):
                                src = ag_out[q * 2 + tt // 2].ap()[rr * 512 + which * 256:rr * 512 + which * 256 + 256, (tt % 2) * 512:(tt % 2 + 1) * 512]
                                dma("sp" if rr % 2 == 0 else "act", yl[:, rr * 2:rr * 2 + 2, :], src.rearrange("(k p) n -> p k n", p=128), reads=[r_agout], writes=[yl])
                            if q == 0:
                                op("dve", lambda yl=yl, dst=dst, q=q: nc.vector.tensor_scalar(dst[:], yl[:], selt[:, q:q + 1], None, ALU.mult), reads=[yl, selt], writes=[dst])
                            else:
                                op("dve", lambda yl=yl, dst=dst, q=q: nc.vector.scalar_tensor_tensor(out=dst[:], in0=yl[:], scalar=selt[:, q:q + 1], in1=dst[:], op0=ALU.mult, op1=ALU.add),
                                   reads=[yl, selt, dst], writes=[dst])
                    for n4 in range(8):
                        wh = load_w("w_ph", n4 * 256, 256, 8)
                        wa_ = load_w("w_pa", n4 * 256, 256, 8)
                        for fb in range(2):
                            dblk = n4 * 2 + fb
                            pmh = psM[(2 * dblk) % 4]
                            pma = psM[(2 * dblk + 1) % 4]
                            for k_ in range(8):
                                op("pe", lambda k_=k_, fb=fb: nc.tensor.matmul(pmh[:], wh[:, k_, fb * 128:(fb + 1) * 128], Yh[:, k_, :], start=(k_ == 0), stop=(k_ == 7)), reads=[wh, Yh], writes=[pmh], inc=(k_ == 7))
                            for k_ in range(8):
                                op("pe", lambda k_=k_, fb=fb: nc.tensor.matmul(pma[:], wa_[:, k_, fb * 128:(fb + 1) * 128], Ya[:, k_, :], start=(k_ == 0), stop=(k_ == 7)), reads=[wa_, Ya], writes=[pma], inc=(k_ == 7))
                            op("dve", lambda dblk=dblk: nc.vector.tensor_tensor(out=t1[:], in0=pmh[:], in1=ghT[:, dblk, :], op=ALU.mult), reads=[pmh, ghT], writes=[t1])
                            op("dve", lambda dblk=dblk: nc.vector.tensor_tensor(out=t2[:], in0=pma[:], in1=ghT[:, 16 + dblk, :], op=ALU.mult), reads=[pma, ghT], writes=[t2])
                            op("dve", lambda dblk=dblk: nc.vector.tensor_tensor(out=mT[:, dblk, :], in0=t1[:], in1=t2[:], op=ALU.add), reads=[t1, t2], writes=[mT])
                    for n4 in range(8):
                        w_ = load_w("w_o", n4 * 256, 256)
                        for fb in range(2):
                            dblk = n4 * 2 + fb
                            pm = psM[dblk % 3]
                            for i in range(16):
                                op("pe", lambda i=i, fb=fb, pm=pm, w_=w_: nc.tensor.matmul(pm[:], w_[:, i, fb * 128:(fb + 1) * 128], mT[:, i, :], start=(i == 0), stop=(i == 15)),
                                   reads=[w_, mT], writes=[pm], inc=(i == 15))
                            back_to_tokens(pm, 32, dblk, tix)
                            tix += 1
                    kb.barrier()
                for s in range(4):
                    r = 16 + tt * 4 + s
                    norm_block(xs[s], (rstd, r), junk, xns[s % 2], psTb, hT, s * 128, G2, 48, (ssq, r), (rt, r))
                with contextlib.ExitStack() as s2:
                    actT = sbuf(s2, [128, 44, 512], BF16)
                    sg = [sbuf(s2, [128, 512], F32) for _ in range(2)]
                    wd = [sbuf(s2, [128, 44, 128], BF16) for _ in range(2)]
                    for n11 in range(22):
                        wg_ = load_w("w_gate", n11 * 256, 256)
                        wu_ = load_w("w_up", n11 * 256, 256)
                        for fb in range(2):
                            f = n11 * 2 + fb
                            pg = psM[(2 * f) % 4]
                            pu = psM[(2 * f + 1) % 4]
                            for i in range(16):
                                op("pe", lambda i=i, fb=fb: nc.tensor.matmul(pg[:], wg_[:, i, fb * 128:(fb + 1) * 128], hT[:, i, :], start=(i == 0), stop=(i == 15)), reads=[wg_, hT, hT.r2], writes=[pg], inc=(i == 15))
                            for i in range(16):
                                op("pe", lambda i=i, fb=fb: nc.tensor.matmul(pu[:], wu_[:, i, fb * 128:(fb + 1) * 128], hT[:, i, :], start=(i == 0), stop=(i == 15)), reads=[wu_, hT, hT.r2], writes=[pu], inc=(i == 15))
                            sg_ = sg[f % 2]
                            op("act", lambda sg_=sg_: nc.scalar.activation(out=sg_[:], in_=pg[:], func=AF.Silu), reads=[pg], writes=[sg_])
                            op("dve", lambda sg_=sg_, f=f: nc.vector.tensor_tensor(out=actT[:, f, :], in0=pu[:], in1=sg_[:], op=ALU.mult), reads=[pu, sg_], writes=[actT])
                    for dblk in range(16):
                        wd_ = wd[dblk % 2]
                        dma("sp" if dblk % 2 == 0 else "act", wd_[:], wbf["w_down"].ap()[dblk * 128:(dblk + 1) * 128, :].rearrange("p (f c) -> p f c", c=128), reads=[r_wbf["w_down"]], writes=[wd_])
                        pm = psM[dblk % 3]
                        for f in range(44):
                            op("pe", lambda f=f, pm=pm, wd_=wd_: nc.tensor.matmul(pm[:], wd_[:, f, :], actT[:, f, :], start=(f == 0), stop=(f == 43)), reads=[wd_, actT], writes=[pm], inc=(f == 43))
                        back_to_tokens(pm, 80, dblk, tix)
                        tix += 1
                    kb.barrier()
                for s in range(4):
                    dma("sp", out[(tt * 4 + s) * 128:(tt * 4 + s + 1) * 128, :], xs[s][:], reads=[xs[s]], writes=[r_out])
            kb.barrier()
    return nc


_CACHE = {}
_UPTO = 99


def _bf(a):
    return np.ascontiguousarray(a).astype(ml_dtypes.bfloat16)


def kernel(**inputs):
    f32 = np.float32
    g = {k: np.asarray(v) for k, v in inputs.items()}
    x = g["x"].astype(f32)
    l = 0

    def pp(v):
        return np.ascontiguousarray(v.reshape(16, 128).T).astype(f32)

    s_hy, s_q, s_k, s_v = 3072, 4096, 5120, 6144
    w_in = g["w_in"][l]
    tt = np.linspace(0.0, 1.0, L, dtype=f32)
    idx = np.arange(2 * L)
    posn = np.where(idx >= L, idx - L, np.clip(L - idx, 0, L - 1))
    wv = (2.0 * math.pi / L) * posn.astype(f32)
    bands = np.linspace(1e-4, 15, 16, dtype=f32)
    zT2 = np.concatenate([tt[posn][None, :], np.cos(bands[:, None] * wv[None, :]), -np.sin(bands[:, None] * wv[None, :])], axis=0).astype(f32)
    zT2 = np.ascontiguousarray(np.concatenate([zT2, zT2[:, ::-1]], axis=1))
    tgrid = tt[posn][None, :].astype(f32)
    tgrid = np.ascontiguousarray(np.concatenate([tgrid, tgrid[:, ::-1]], axis=1))
    min_decay = math.log(1e-2) / 0.3
    max_decay = math.log(1e-2) / 1.5
    deltas = np.abs(np.linspace(min_decay, max_decay, 1024, dtype=f32))
    invf = (500000.0 ** (-np.arange(8, dtype=f32) * (2.0 / 16))).astype(f32)
    ident = np.eye(128, dtype=f32)

    shared = {
        "w_ada": np.ascontiguousarray(g["w_ada"][l]).astype(f32),
        "badaT": np.ascontiguousarray(g["b_ada"][l].reshape(96, 128).T).astype(f32),
        "n1g": pp(g["norm1_g"][l]), "n2g": pp(g["norm2_g"][l]),
        "w_g": np.ascontiguousarray(w_in[:, s_v:]).astype(f32),
        "fw1": g["hy_filt_w1"][l].astype(f32), "fw2": g["hy_filt_w2"][l].astype(f32), "fw3": g["hy_filt_w3"][l].astype(f32),
        "fbias": np.ascontiguousarray(np.stack([g["hy_filt_b1"][l], g["hy_filt_b2"][l], g["hy_filt_b3"][l]], axis=1)).astype(f32),
        "ffreq": np.ascontiguousarray(g["hy_filt_freq"][l].T).astype(f32),
        "gqk_bc": np.ascontiguousarray(np.broadcast_to(np.concatenate([np.tile(g["q_norm_g"][l], 4), np.tile(g["k_norm_g"][l], 4)])[None, :], (128, 512))).astype(f32),
        "lamv": np.concatenate([g["lam_q1"][l], g["lam_k1"][l], g["lam_q2"][l], g["lam_k2"][l]])[None, :].astype(f32),
        "subln_bc": np.ascontiguousarray(np.broadcast_to(g["subln_g"][l][None, :], (128, 128))).astype(f32),
        "w_ph": g["w_proj_hy"][l].astype(f32), "w_pa": g["w_proj_att"][l].astype(f32), "w_o": g["w_out"][l].astype(f32),
        "w_gate": g["w_gate"][l].astype(f32), "w_up": g["w_up"][l].astype(f32), "w_down": g["w_down"][l].astype(f32),
        "ident_bf": ident.astype(ml_dtypes.bfloat16), "ident_f": ident,
        "jrev": np.ascontiguousarray(ident[::-1]).astype(ml_dtypes.bfloat16),
        "zT2": zT2, "tgrid": tgrid,
        "invf_bc": np.ascontiguousarray(np.broadcast_to(invf[None, :], (128, 8))).astype(f32),
    }
    in_maps = []
    for core in range(8):
        b, j = core // 4, core % 4
        ch = slice(256 * j, 256 * j + 256)
        hy_cols = np.concatenate([np.arange(256 * j, 256 * j + 256) + o for o in (0, 1024, 2048)])
        qkv_cols = np.concatenate([np.arange(256 * j, 256 * j + 256) + o for o in (s_hy, s_q, s_k)])
        cw = g["hy_conv_w"][l][:, hy_cols]
        convw = np.ascontiguousarray(cw.reshape(3, 6, 128).transpose(2, 1, 0).reshape(128, 18)).astype(f32)
        convb = np.ascontiguousarray(g["hy_conv_b"][l][hy_cols].reshape(6, 128).T).astype(f32)
        fwo = g["hy_filt_w_out"][l].reshape(64, 2, 2, 1024)[:, :, :, ch].reshape(64, 1024)
        hyb = g["hy_bias"][l][:, ch].reshape(1, 512)
        selv = np.zeros((128, 4), f32)
        selv[:, j] = 1.0
        m = dict(shared)
        m.update({
            "xb": np.ascontiguousarray(x[b]), "xo": np.ascontiguousarray(x[b, 2048 * j:2048 * (j + 1)]),
            "cb": pp(g["c"][b]),
            "pos": np.ascontiguousarray(g["positions"][b].reshape(NB, 128).T).astype(np.int32),
            "w_hy": np.ascontiguousarray(w_in[:, hy_cols]).astype(f32),
            "w_qkv": np.ascontiguousarray(w_in[:, qkv_cols]).astype(f32),
            "convw": convw, "convb": convb,
            "fwout": np.ascontiguousarray(fwo).astype(f32),
            "ndelta": np.ascontiguousarray((-deltas[ch]).reshape(2, 128).T).astype(f32),
            "hyb_bc": np.ascontiguousarray(np.broadcast_to(hyb, (128, 512))).astype(f32),
            "hybT": np.ascontiguousarray(g["hy_bias"][l][:, ch].reshape(4, 128).T).astype(f32),
            "sel": selv,
        })
        in_maps.append(m)
    if "nc" not in _CACHE:
        _CACHE["nc"] = build(_UPTO)
    res = run_bass_kernel_spmd(_CACHE["nc"], in_maps, core_ids=list(range(8)))
    outp = np.zeros((2, L, D), f32)
    for core in range(8):
        b, j = core // 4, core % 4
        outp[b, 2048 * j:2048 * (j + 1)] = np.asarray(res.results[core]["out"]).astype(f32)
    return outp
```

```python
import math, contextlib
import numpy as np
import ml_dtypes
import concourse.bass as bass
import concourse.mybir as mybir
from concourse.bass_utils import run_bass_kernel_spmd

F32, BF16, I32 = mybir.dt.float32, mybir.dt.bfloat16, mybir.dt.int32
AF = mybir.ActivationFunctionType
ALU = mybir.AluOpType
AX = mybir.AxisListType

D = 2048
L = 8192
NB = 64
FF = 5632
EPS = 1e-6
LAM_INIT = 0.8 - 0.6 * math.exp(0.0)
TWO_PI = 2.0 * math.pi
PI_B = 3.1415925
NDS = 6
SHIFT = -8.0


def _phase(active):
    if active:
        with contextlib.ExitStack() as st:
            yield st


class Res:
    __slots__ = ("w", "rd")

    def __init__(self):
        self.w = None
        self.rd = {}


class Eng:
    def __init__(self, name, eng, sem):
        self.name, self.eng, self.sem = name, eng, sem
        self.count = 0
        self.waited = {}


class T:
    def __init__(self, t):
        self.t = t
        self.r = Res()
        self.r2 = Res()

    def __getitem__(self, k):
        return self.t[k]


class KB:
    def __init__(self, nc, st):
        self.nc = nc
        mk = lambda n: st.enter_context(nc.semaphore(n))
        self.E = {n: Eng(n, e, mk("s_" + n)) for n, e in
                  [("pe", nc.tensor), ("act", nc.scalar), ("dve", nc.vector), ("pool", nc.gpsimd), ("sp", nc.sync)]}
        self.dsem = {q: [mk(f"d_{q}{i}") for i in range(NDS)] for q in ("sp", "pool", "act")}
        self.dval = {q: [0] * NDS for q in ("sp", "pool", "act")}
        self.di = {q: 0 for q in ("sp", "pool", "act")}

    def _wait(self, E, tok):
        kind, a, v = tok
        if kind == "c":
            if a is E and E.name == "pe":
                return
            sem, key = a.sem, a.name
        else:
            sem, key = a
        if E.waited.get(key, 0) >= v:
            return
        E.eng.wait_ge(sem, v)
        E.waited[key] = v

    def _deps(self, E, reads, writes):
        for r in reads:
            if r.w is not None:
                self._wait(E, r.w)
        for w in writes:
            if w.w is not None:
                self._wait(E, w.w)
            for t in w.rd.values():
                self._wait(E, t)

    def _commit(self, tok, key, reads, writes):
        for r in reads:
            r.rd[key] = tok
        for w in writes:
            w.w = tok
            w.rd = {}

    def op(self, en, fn, reads=(), writes=(), inc=True):
        E = self.E[en]
        reads = [x.r if isinstance(x, T) else x for x in reads]
        writes = [x.r if isinstance(x, T) else x for x in writes]
        self._deps(E, reads, writes)
        inst = fn()
        if inc:
            E.count += 1
            inst.then_inc(E.sem, 1)
            self._commit(("c", E, E.count), E.name, reads, writes)
        else:
            self._commit(("c", E, E.count + 1), E.name, reads, writes)

    def dma(self, q, out, in_, reads=(), writes=(), **kw):
        E = self.E[q]
        reads = [x.r if isinstance(x, T) else x for x in reads]
        writes = [x.r if isinstance(x, T) else x for x in writes]
        self._deps(E, reads, writes)
        i = self.di[q] % NDS
        self.di[q] += 1
        sem = self.dsem[q][i]
        v = self.dval[q][i]
        key = f"{q}{i}"
        if v > 0:
            self._wait(E, ("d", (sem, key), v))
        inst = E.eng.dma_start(out=out, in_=in_, **kw)
        inst.then_inc(sem, 16)
        self.dval[q][i] = v + 16
        self._commit(("d", (sem, key), v + 16), key, reads, writes)

    def barrier(self):
        for E in self.E.values():
            for Fe in self.E.values():
                if Fe is not E and Fe.count > 0:
                    sem, key, v = Fe.sem, Fe.name, Fe.count
                    if E.waited.get(key, 0) < v:
                        E.eng.wait_ge(sem, v)
                        E.waited[key] = v
            for q in self.dsem:
                for i in range(NDS):
                    v = self.dval[q][i]
                    if v > 0:
                        self._wait(E, ("d", (self.dsem[q][i], f"{q}{i}"), v))


def V(t, off, dims, p0=0, pn=None):
    tt = t.t if isinstance(t, T) else t
    base = tt[:]
    ps, pc = base.ap[0]
    if pn is None:
        pn = pc
    return bass.AP(tensor=base.tensor, offset=base.offset + p0 * ps + off, ap=[[ps, pn]] + [list(d) for d in dims])


def build(upto=99, dbg=None):
    nc = bass.Bass("TRN2", target_bir_lowering=False)
    dt_in = lambda n, s, d=F32: nc.dram_tensor(n, s, d, kind="ExternalInput").ap()
    xb = dt_in("xb", [L, D])
    xo = dt_in("xo", [2048, D])
    cb = dt_in("cb", [128, 16])
    posd = dt_in("pos", [128, NB], I32)
    w_ada = dt_in("w_ada", [D, 6 * D])
    badaT = dt_in("badaT", [128, 96])
    n1g = dt_in("n1g", [128, 16])
    n2g = dt_in("n2g", [128, 16])
    w_hy = dt_in("w_hy", [D, 768])
    w_qkv = dt_in("w_qkv", [D, 768])
    w_g = dt_in("w_g", [D, 4096])
    convw = dt_in("convw", [128, 18])
    convb = dt_in("convb", [128, 6])
    fw1 = dt_in("fw1", [33, 64])
    fw2 = dt_in("fw2", [64, 64])
    fw3 = dt_in("fw3", [64, 64])
    fbias = dt_in("fbias", [64, 3])
    ffreq = dt_in("ffreq", [64, 3])
    fwout = dt_in("fwout", [64, 1024])
    ndelta = dt_in("ndelta", [128, 2])
    hyb_bc = dt_in("hyb_bc", [128, 512])
    gqk_bc = dt_in("gqk_bc", [128, 512])
    lamv = dt_in("lamv", [1, 256])
    subln_bc = dt_in("subln_bc", [128, 128])
    w_ph = dt_in("w_ph", [1024, D])
    w_pa = dt_in("w_pa", [1024, D])
    w_o = dt_in("w_o", [D, D])
    w_gate = dt_in("w_gate", [D, FF])
    w_up = dt_in("w_up", [D, FF])
    w_down = dt_in("w_down", [FF, D])
    sel = dt_in("sel", [128, 4])
    ident_bf_d = dt_in("ident_bf", [128, 128], BF16)
    ident_f_d = dt_in("ident_f", [128, 128])
    zT2 = dt_in("zT2", [33, 4 * L])
    tgrid = dt_in("tgrid", [1, 4 * L])
    jrev_d = dt_in("jrev", [128, 128], BF16)
    hybT_d = dt_in("hybT", [128, 4])
    invf_bc = dt_in("invf_bc", [128, 8])
    out = nc.dram_tensor("out", [2048, D], F32, kind="ExternalOutput").ap()

    hy_pre = nc.dram_tensor("hy_pre", [768, L + 2], BF16)
    qkT_d = nc.dram_tensor("qkT_d", [512, L], BF16)
    vtok = nc.dram_tensor("vtok", [L, 256], BF16)
    k2 = nc.dram_tensor("k2", [512, 2 * L], BF16)
    ag_in = [nc.dram_tensor(f"ag_in{i}", [512, 1024], BF16) for i in range(8)]
    ag_out = [nc.dram_tensor(f"ag_out{i}", [2048, 1024], BF16) for i in range(8)]
    r_hy_pre, r_qkT, r_vtok, r_k2, r_agin, r_agout, r_out = [Res() for _ in range(7)]
    wsrc = {"w_g": w_g, "w_ph": w_ph, "w_pa": w_pa, "w_o": w_o, "w_gate": w_gate, "w_up": w_up, "w_down": w_down}
    wcw = {k: (128 if k == "w_down" else 256) for k in wsrc}
    wkb = {k: v.shape[0] // 128 for k, v in wsrc.items()}
    wbf = {k: nc.dram_tensor(k + "_bf", [(v.shape[1] // wcw[k]) * 128, wkb[k] * wcw[k]], BF16) for k, v in wsrc.items()}
    r_wbf = {k: Res() for k in wsrc}

    with contextlib.ExitStack() as top:
        kb = KB(nc, top)
        op, dma = kb.op, kb.dma
        _cnt = [0]

        def wrap(dst, src, shift, tmp, dres, sres, tres):
            op("dve", lambda: nc.vector.tensor_scalar(dst, src, float(shift), None, ALU.add), reads=[sres], writes=[dres])
            op("dve", lambda: nc.vector.tensor_scalar(tmp, dst, PI_B, -TWO_PI, ALU.is_gt, ALU.mult), reads=[dres], writes=[tres])
            op("dve", lambda: nc.vector.tensor_tensor(out=dst, in0=dst, in1=tmp, op=ALU.add), reads=[dres, tres], writes=[dres])
            op("dve", lambda: nc.vector.tensor_scalar(tmp, dst, -PI_B, TWO_PI, ALU.is_lt, ALU.mult), reads=[dres], writes=[tres])
            op("dve", lambda: nc.vector.tensor_tensor(out=dst, in0=dst, in1=tmp, op=ALU.add), reads=[dres, tres], writes=[dres])

        def sbuf(st, shape, dt, name=None):
            _cnt[0] += 1
            return T(st.enter_context(nc.sbuf_tensor(f"{name or 'sb'}_{_cnt[0]}", list(shape), dt)))

        def psum(st, shape, dt, name=None):
            _cnt[0] += 1
            return T(st.enter_context(nc.psum_tensor(f"{name or 'ps'}_{_cnt[0]}", list(shape), dt)))

        ident_bf = sbuf(top, [128, 128], BF16, "identb")
        ident_f = sbuf(top, [128, 128], F32, "identf")
        modT = sbuf(top, [128, 96], F32, "modT")
        G1 = sbuf(top, [128, 16], F32, "G1")
        G2 = sbuf(top, [128, 16], F32, "G2")
        cos_t = sbuf(top, [128, NB * 8], F32, "cos")
        sin_t = sbuf(top, [128, NB * 8], F32, "sin")
        nlam = sbuf(top, [128, 1], F32, "nlam")
        epsb = sbuf(top, [128, 1], F32, "epsb")
        shiftb = sbuf(top, [128, 1], F32, "shiftb")
        gqk = sbuf(top, [128, 512], F32, "gqk")
        subg = sbuf(top, [128, 128], F32, "subg")
        selt = sbuf(top, [128, 4], F32, "selt")
        dma("sp", ident_bf[:], ident_bf_d, writes=[ident_bf])
        dma("sp", ident_f[:], ident_f_d, writes=[ident_f])
        dma("sp", gqk[:], gqk_bc, writes=[gqk])
        dma("sp", subg[:], subln_bc, writes=[subg])
        dma("sp", selt[:], sel, writes=[selt])
        op("dve", lambda: nc.vector.memset(epsb[:], EPS), writes=[epsb])
        op("dve", lambda: nc.vector.memset(shiftb[:], SHIFT), writes=[shiftb])
        op("dve", lambda: nc.vector.tensor_scalar(gqk[:, 0:256], gqk[:, 0:256], 0.125, None, ALU.mult), reads=[gqk], writes=[gqk])
        op("dve", lambda: nc.vector.tensor_scalar(subg[:], subg[:], 1.0 - LAM_INIT, None, ALU.mult), reads=[subg], writes=[subg])

        def precast():
            for k_, src_ in wsrc.items():
                cw_ = wcw[k_]
                sv = src_.rearrange("(i p) n -> p i n", p=128)
                for ch_ in range(src_.shape[1] // cw_):
                    dv = wbf[k_].ap()[ch_ * 128:(ch_ + 1) * 128, :].rearrange("p (i c) -> p i c", c=cw_)
                    dma("pool", dv, sv[:, :, ch_ * cw_:(ch_ + 1) * cw_], writes=[r_wbf[k_]])

        with contextlib.ExitStack() as st:
            cS = sbuf(st, [128, 16], F32)
            wa = [sbuf(st, [128, 16, 512], F32) for _ in range(2)]
            modrow = sbuf(st, [1, 6 * D], F32)
            bT = sbuf(st, [128, 96], F32)
            one11 = sbuf(st, [1, 128], F32)
            n1t = sbuf(st, [128, 16], F32)
            n2t = sbuf(st, [128, 16], F32)
            psr = [psum(st, [128, 512], F32) for _ in range(2)]
            psT = psum(st, [128, 512], F32)
            dma("sp", cS[:], cb, writes=[cS])
            dma("sp", bT[:], badaT, writes=[bT])
            dma("sp", n1t[:], n1g, writes=[n1t])
            dma("sp", n2t[:], n2g, writes=[n2t])
            op("dve", lambda: nc.vector.memset(one11[:], 1.0), writes=[one11])
            op("act", lambda: nc.scalar.activation(out=cS[:], in_=cS[:], func=AF.Silu), reads=[cS], writes=[cS])
            wav = w_ada.rearrange("(i p) n -> p i n", p=128)
            for n in range(24):
                w_ = wa[n % 2]
                dma("sp" if n % 2 == 0 else "act", w_[:], wav[:, :, n * 512:(n + 1) * 512], writes=[w_])
                pr = psr[n % 2]
                for i in range(16):
                    op("pe", lambda i=i, w_=w_, pr=pr: nc.tensor.matmul(pr[0:1, :], cS[:, i:i + 1], w_[:, i, :], start=(i == 0), stop=(i == 15)),
                       reads=[cS, w_], writes=[pr], inc=(i == 15))
                op("dve", lambda n=n, pr=pr: nc.vector.tensor_copy(out=modrow[0:1, n * 512:(n + 1) * 512], in_=pr[0:1, :]), reads=[pr], writes=[modrow])
            for k in range(96):
                op("pe", lambda k=k: nc.tensor.matmul(psT[:, k:k + 1], modrow[0:1, k * 128:(k + 1) * 128], one11[0:1, 0:1], start=True, stop=True),
                   reads=[modrow, one11], writes=[psT], inc=(k == 95))
            op("dve", lambda: nc.vector.tensor_tensor(out=modT[:], in0=psT[:, 0:96], in1=bT[:], op=ALU.add), reads=[psT, bT], writes=[modT])
            op("dve", lambda: nc.vector.scalar_tensor_tensor(out=G1[:], in0=modT[:, 16:32], scalar=1.0, in1=n1t[:], op0=ALU.add, op1=ALU.mult), reads=[modT, n1t], writes=[G1])
            op("dve", lambda: nc.vector.scalar_tensor_tensor(out=G2[:], in0=modT[:, 64:80], scalar=1.0, in1=n2t[:], op0=ALU.add, op1=ALU.mult), reads=[modT, n2t], writes=[G2])
            posi = sbuf(st, [128, NB], I32)
            posf = sbuf(st, [128, NB], F32)
            invf = sbuf(st, [128, 8], F32)
            ang = sbuf(st, [128, NB * 8], F32)
            angi = sbuf(st, [128, NB * 8], I32)
            angf = sbuf(st, [128, NB * 8], F32)
            dma("sp", posi[:], posd, writes=[posi])
            dma("sp", invf[:], invf_bc, writes=[invf])
            op("dve", lambda: nc.vector.tensor_copy(out=posf[:], in_=posi[:]), reads=[posi], writes=[posf])
            op("dve", lambda: nc.vector.tensor_tensor(out=V(ang, 0, [(8, NB), (1, 8)]), in0=V(posf, 0, [(1, NB), (0, 8)]), in1=V(invf, 0, [(0, NB), (1, 8)]), op=ALU.mult),
               reads=[posf, invf], writes=[ang])
            op("dve", lambda: nc.vector.tensor_scalar(ang[:], ang[:], 1.0 / TWO_PI, None, ALU.mult), reads=[ang], writes=[ang])
            op("dve", lambda: nc.vector.tensor_copy(out=angi[:], in_=ang[:]), reads=[ang], writes=[angi])
            op("dve", lambda: nc.vector.tensor_copy(out=angf[:], in_=angi[:]), reads=[angi], writes=[angf])
            op("dve", lambda: nc.vector.tensor_tensor(out=ang[:], in0=ang[:], in1=angf[:], op=ALU.subtract), reads=[ang, angf], writes=[ang])
            op("dve", lambda: nc.vector.tensor_scalar(ang[:], ang[:], TWO_PI, None, ALU.mult), reads=[ang], writes=[ang])
            wtmp = sbuf(st, [128, NB * 8], F32)
            wrap(angf[:], ang[:], 0.0, wtmp[:], angf, ang, wtmp)
            op("act", lambda: nc.scalar.activation(out=sin_t[:], in_=angf[:], func=AF.Sin), reads=[angf], writes=[sin_t])
            wrap(angf[:], ang[:], math.pi / 2, wtmp[:], angf, ang, wtmp)
            op("act", lambda: nc.scalar.activation(out=cos_t[:], in_=angf[:], func=AF.Sin), reads=[angf], writes=[cos_t])
            lv = sbuf(st, [1, 256], F32)
            lp = sbuf(st, [1, 128], F32)
            ls = sbuf(st, [1, 2], F32)
            dma("sp", lv[:], lamv, writes=[lv])
            op("dve", lambda: nc.vector.tensor_tensor(out=V(lp, 0, [(64, 2), (1, 64)], 0, 1), in0=V(lv, 0, [(128, 2), (1, 64)], 0, 1), in1=V(lv, 64, [(128, 2), (1, 64)], 0, 1), op=ALU.mult),
               reads=[lv], writes=[lp])
            op("dve", lambda: nc.vector.tensor_reduce(out=ls[:], in_=V(lp, 0, [(64, 2), (1, 64)], 0, 1), axis=AX.X, op=ALU.add), reads=[lp], writes=[ls])
            op("act", lambda: nc.scalar.activation(out=ls[:], in_=ls[:], func=AF.Exp), reads=[ls], writes=[ls])
            op("dve", lambda: nc.vector.tensor_tensor(out=ls[0:1, 0:1], in0=ls[0:1, 1:2], in1=ls[0:1, 0:1], op=ALU.subtract), reads=[ls], writes=[ls])
            op("dve", lambda: nc.vector.tensor_scalar(ls[0:1, 0:1], ls[0:1, 0:1], -LAM_INIT, None, ALU.add), reads=[ls], writes=[ls])
            op("pe", lambda: nc.tensor.matmul(psT[:, 100:101], one11[0:1, :], ls[0:1, 0:1], start=True, stop=True), reads=[one11, ls, modT], writes=[psT])
            op("dve", lambda: nc.vector.tensor_copy(out=nlam[:], in_=psT[:, 100:101]), reads=[psT], writes=[nlam])
            kb.barrier()

        def S1(i):
            return modT[:, i:i + 1]

        def norm_block(xt, rstd_col, junk, xn, psTb, hT, col0, Gm, s_off, ssq_col, rt_col):
            op("act", lambda: nc.scalar.activation(out=junk[:], in_=xt[:], func=AF.Square, accum_out=ssq_col[0][:, ssq_col[1]:ssq_col[1] + 1]),
               reads=[xt], writes=[junk, ssq_col[0]])
            op("act", lambda: nc.scalar.activation(out=rt_col[0][:, rt_col[1]:rt_col[1] + 1], in_=ssq_col[0][:, ssq_col[1]:ssq_col[1] + 1], func=AF.Sqrt, scale=1.0 / D, bias=epsb[:]),
               reads=[ssq_col[0], epsb], writes=[rt_col[0]])
            op("dve", lambda: nc.vector.reciprocal(out=rstd_col[0][:, rstd_col[1]:rstd_col[1] + 1], in_=rt_col[0][:, rt_col[1]:rt_col[1] + 1]),
               reads=[rt_col[0]], writes=[rstd_col[0]])
            op("act", lambda: nc.scalar.activation(out=xn[:], in_=xt[:], func=AF.Identity, scale=rstd_col[0][:, rstd_col[1]:rstd_col[1] + 1]),
               reads=[xt, rstd_col[0]], writes=[xn])
            for half in range(2):
                ph_ = psTb[half]
                for i8 in range(8):
                    i = half * 8 + i8
                    op("pe", lambda i=i, i8=i8, ph_=ph_: nc.tensor.transpose(ph_[:, i8 * 128:(i8 + 1) * 128], xn[:, i * 128:(i + 1) * 128], ident_bf[:]),
                       reads=[xn, ident_bf], writes=[ph_], inc=(i8 == 7))
                for i8 in range(8):
                    i = half * 8 + i8
                    if half == 0:
                        op("act", lambda i=i, i8=i8, ph_=ph_: nc.scalar.activation(out=hT[:, i, col0:col0 + 128], in_=ph_[:, i8 * 128:(i8 + 1) * 128], func=AF.Identity,
                                                                                  scale=Gm[:, i:i + 1], bias=modT[:, s_off + i:s_off + i + 1]),
                           reads=[ph_, Gm, modT], writes=[hT.r])
                    else:
                        op("dve", lambda i=i, i8=i8, ph_=ph_: nc.vector.tensor_scalar(hT[:, i, col0:col0 + 128], ph_[:, i8 * 128:(i8 + 1) * 128], Gm[:, i:i + 1],
                                                                                     modT[:, s_off + i:s_off + i + 1], ALU.mult, ALU.add),
                           reads=[ph_, Gm, modT], writes=[hT.r2])

        for st in _phase(upto >= 1):
            whyb = sbuf(st, [128, 16, 768], BF16)
            wqkvb = sbuf(st, [128, 16, 768], BF16)
            dma("pool", whyb[:], w_hy.rearrange("(i p) n -> p i n", p=128), writes=[whyb])
            dma("pool", wqkvb[:], w_qkv.rearrange("(i p) n -> p i n", p=128), writes=[wqkvb])
            wob = sbuf(st, [64, 1024], BF16)
            dma("pool", wob[:], fwout, writes=[wob])
            precast()
            zt = sbuf(st, [128, 2], BF16)
            op("dve", lambda: nc.vector.memset(zt[:], 0.0), writes=[zt])
            for cbk in range(6):
                dma("sp", bass.AP(tensor=hy_pre, offset=cbk * 128 * (L + 2), ap=[[L + 2, 128], [L + 1, 2], [1, 1]]), V(zt, 0, [(1, 2), (1, 1)]), reads=[zt], writes=[r_hy_pre], allow_slow_non_contiguous=True)
            xts = [sbuf(st, [128, D], F32) for _ in range(3)]
            xns = [sbuf(st, [128, D], BF16) for _ in range(2)]
            junk = sbuf(st, [128, D], BF16)
            hTs = [sbuf(st, [128, 16, 512], BF16) for _ in range(2)]
            ssq = sbuf(st, [128, NB], F32)
            rt = sbuf(st, [128, NB], F32)
            rstd = sbuf(st, [128, NB], F32)
            hst = [sbuf(st, [128, 512], BF16) for _ in range(2)]
            sqj = sbuf(st, [128, 512], F32)
            ss8 = sbuf(st, [128, 8], F32)
            rt8 = sbuf(st, [128, 8], F32)
            rs8 = sbuf(st, [128, 8], F32)
            qn = sbuf(st, [128, 512], F32)
            rtmp = sbuf(st, [128, 4 * 64], F32)
            qbf = sbuf(st, [128, 512], BF16)
            qst = [sbuf(st, [128, 4, 512], BF16) for _ in range(2)]
            vst = [sbuf(st, [128, 4, 256], BF16) for _ in range(2)]
            psTb = [psum(st, [128, 1024], BF16) for _ in range(2)]
            psH = [psum(st, [128, 512], F32) for _ in range(2)]
            psQ0s = [psum(st, [128, 512], F32) for _ in range(2)]
            psQ1s = [psum(st, [128, 512], F32) for _ in range(2)]
            w1t = sbuf(st, [33, 64], F32)
            w2t = sbuf(st, [64, 64], F32)
            w3t = sbuf(st, [64, 64], F32)
            fbt = sbuf(st, [64, 3], F32)
            fft = sbuf(st, [64, 3], F32)
            fbs = sbuf(st, [64, 3], F32)
            ndl = sbuf(st, [128, 2], F32)
            hbT = sbuf(st, [128, 4], F32)
            fsets = [dict(zl=sbuf(st, [33, 512], F32), s3=sbuf(st, [64, 512], BF16), aa=[sbuf(st, [64, 512], F32) for _ in range(2)],
                          wtm=sbuf(st, [64, 512], F32), tg=sbuf(st, [128, 512], F32), dec=[sbuf(st, [128, 512], F32) for _ in range(2)],
                          k2t=[sbuf(st, [128, 512], BF16) for _ in range(2)]) for _ in range(2)]
            pmf = psH[1]
            dma("sp", w1t[:], fw1, writes=[w1t])
            dma("sp", w2t[:], fw2, writes=[w2t])
            dma("sp", w3t[:], fw3, writes=[w3t])
            dma("sp", fbt[:], fbias, writes=[fbt])
            dma("sp", fft[:], ffreq, writes=[fft])
            dma("sp", ndl[:], ndelta, writes=[ndl])
            dma("sp", hbT[:], hybT_d, writes=[hbT])
            op("dve", lambda: nc.vector.tensor_tensor(out=fbs[:], in0=fbt[:], in1=fft[:], op=ALU.mult), reads=[fbt, fft], writes=[fbs])

            def filt_gen(ti, fs):
                rev = ti >= 32
                tl = ti % 32
                o = 0 if rev else 1
                zl, s3, tgt, wtm = fs["zl"], fs["s3"], fs["tg"], fs["wtm"]
                dma("sp", zl[:], zT2[:, ti * 512:(ti + 1) * 512], writes=[zl])
                dma("sp", tgt[:], bass.AP(tensor=tgrid.tensor, offset=ti * 512, ap=[[0, 128], [1, 512]]), writes=[tgt])
                cur = None
                for l_, wl in enumerate([w1t, w2t, w3t]):
                    a_ = fs["aa"][l_ % 2]
                    if l_ == 0:
                        op("pe", lambda: nc.tensor.matmul(pmf[0:64, :], w1t[:], zl[:], start=True, stop=True), reads=[w1t, zl], writes=[pmf])
                    else:
                        op("pe", lambda: nc.tensor.matmul(pmf[0:64, :], wl[:], cur[:], start=True, stop=True), reads=[wl, cur], writes=[pmf])
                    op("act", lambda: nc.scalar.activation(out=a_[:], in_=pmf[0:64, :], func=AF.Identity, scale=fft[:, l_:l_ + 1], bias=fbs[:, l_:l_ + 1]),
                       reads=[pmf, fft, fbs], writes=[a_])
                    wrap(a_[:], a_[:], 0.0, wtm[:], a_, a_, wtm)
                    if l_ < 2:
                        op("act", lambda: nc.scalar.activation(out=a_[:], in_=a_[:], func=AF.Sin), reads=[a_], writes=[a_])
                        cur = a_
                    else:
                        op("act", lambda: nc.scalar.activation(out=s3[:], in_=a_[:], func=AF.Sin), reads=[a_], writes=[s3])
                    yield
                rdir = (1 if tl < 16 else 0) if not rev else (0 if tl < 16 else 1)
                for cbk in range(2):
                    dc = fs["dec"][cbk]
                    kt = fs["k2t"][cbk]
                    op("act", lambda: nc.scalar.activation(out=dc[:], in_=tgt[:], func=AF.Exp, scale=ndl[:, cbk:cbk + 1]), reads=[tgt, ndl], writes=[dc])
                    c0 = o * 512 + rdir * 256 + cbk * 128
                    op("pe", lambda: nc.tensor.matmul(pmf[:], wob[:, c0:c0 + 128], s3[:], start=True, stop=True), reads=[wob, s3], writes=[pmf])
                    op("dve", lambda: nc.vector.tensor_tensor(out=kt[:], in0=pmf[:], in1=dc[:], op=ALU.mult), reads=[pmf, dc], writes=[kt])
                    zc = (0 if tl == 0 else None) if not rev else (511 if tl == 31 else None)
                    bc_ = (0 if tl == 16 else None) if not rev else (511 if tl == 15 else None)
                    if zc is not None:
                        op("dve", lambda: nc.vector.memset(kt[:, zc:zc + 1], 0.0), writes=[kt])
                    if bc_ is not None:
                        op("dve", lambda: nc.vector.tensor_scalar(kt[:, bc_:bc_ + 1], kt[:, bc_:bc_ + 1], hbT[:, o * 2 + cbk:o * 2 + cbk + 1], None, ALU.add),
                           reads=[kt, hbT], writes=[kt])
                    row0 = o * 256 + cbk * 128
                    dma("sp", k2[row0:row0 + 128, tl * 512:(tl + 1) * 512], kt[:], reads=[kt], writes=[r_k2])
                    yield

            fq = {"next": 0, "gens": [None, None], "turn": 0}

            def filt_step(n=1):
                for _ in range(n):
                    k_ = fq["turn"]
                    fq["turn"] = 1 - k_
                    for _try in range(2):
                        if fq["gens"][k_] is None:
                            if fq["next"] >= 64:
                                break
                            fq["gens"][k_] = filt_gen(fq["next"], fsets[k_])
                            fq["next"] += 1
                        try:
                            next(fq["gens"][k_])
                            break
                        except StopIteration:
                            fq["gens"][k_] = None

            def load_x(r):
                if r < NB:
                    dma("act", xts[r % 3][:], xb[r * 128:(r + 1) * 128, :], writes=[xts[r % 3]])

            load_x(0)
            load_x(1)

            def norm_tile_block(tt_, s_):
                r_ = tt_ * 4 + s_
                load_x(r_ + 2)
                norm_block(xts[r_ % 3], (rstd, r_), junk, xns[r_ % 2], psTb, hTs[tt_ % 2], s_ * 128, G1, 0, (ssq, r_), (rt, r_))

            for s in range(4):
                norm_tile_block(0, s)
            for tt in range(16):
                hT = hTs[tt % 2]
                for cbk in range(6):
                    ph = psH[0]
                    for i in range(16):
                        op("pe", lambda i=i, cbk=cbk, ph=ph: nc.tensor.matmul(ph[:], whyb[:, i, cbk * 128:(cbk + 1) * 128], hT[:, i, :], start=(i == 0), stop=(i == 15)),
                           reads=[whyb, hT, hT.r2], writes=[ph], inc=(i == 15))
                    hs = hst[cbk % 2]
                    if cbk % 2 == 0:
                        op("act", lambda ph=ph, hs=hs: nc.scalar.copy(out=hs[:], in_=ph[:]), reads=[ph], writes=[hs])
                    else:
                        op("dve", lambda ph=ph, hs=hs: nc.vector.tensor_copy(out=hs[:], in_=ph[:]), reads=[ph], writes=[hs])
                    dma("sp", hy_pre[cbk * 128:(cbk + 1) * 128, 1 + tt * 512:1 + (tt + 1) * 512], hs[:], reads=[hs], writes=[r_hy_pre])
                    filt_step(2)
                qs = qst[tt % 2]
                vs = vst[tt % 2]
                def qkv_mm(s):
                    r = tt * 4 + s
                    psQ0 = psQ0s[s % 2]
                    psQ1 = psQ1s[s % 2]
                    for i in range(16):
                        op("pe", lambda i=i, s=s, psQ0=psQ0: nc.tensor.matmul(psQ0[:], hT[:, i, s * 128:(s + 1) * 128], wqkvb[:, i, 0:512], start=(i == 0), stop=(i == 15)),
                           reads=[hT, hT.r2, wqkvb], writes=[psQ0], inc=(i == 15))
                    for i in range(16):
                        op("pe", lambda i=i, s=s, psQ1=psQ1: nc.tensor.matmul(psQ1[:, 0:256], hT[:, i, s * 128:(s + 1) * 128], wqkvb[:, i, 512:768], start=(i == 0), stop=(i == 15)),
                           reads=[hT, hT.r2, wqkvb], writes=[psQ1], inc=(i == 15))
                    op("act", lambda s=s, psQ1=psQ1: nc.scalar.copy(out=vs[:, s, :], in_=psQ1[:, 0:256]), reads=[psQ1], writes=[vs])
                    filt_step(2)

                def qkv_post(s):
                    r = tt * 4 + s
                    psQ0 = psQ0s[s % 2]
                    op("act", lambda psQ0=psQ0: nc.scalar.activation(out=sqj[:], in_=psQ0[:], func=AF.Square), reads=[psQ0], writes=[sqj])
                    op("dve", lambda: nc.vector.tensor_reduce(out=ss8[:], in_=V(sqj, 0, [(64, 8), (1, 64)]), axis=AX.X, op=ALU.add), reads=[sqj], writes=[ss8])
                    op("act", lambda: nc.scalar.activation(out=rt8[:], in_=ss8[:], func=AF.Sqrt, scale=1.0 / 64, bias=epsb[:]), reads=[ss8, epsb], writes=[rt8])
                    op("dve", lambda: nc.vector.reciprocal(out=rs8[:], in_=rt8[:]), reads=[rt8], writes=[rs8])
                    op("dve", lambda psQ0=psQ0: nc.vector.tensor_tensor(out=V(qn, 0, [(64, 8), (1, 64)]), in0=V(psQ0, 0, [(64, 8), (1, 64)]), in1=V(rs8, 0, [(1, 8), (0, 64)]), op=ALU.mult),
                       reads=[psQ0, rs8], writes=[qn])
                    op("dve", lambda: nc.vector.tensor_tensor(out=qn[:], in0=qn[:], in1=gqk[:], op=ALU.mult), reads=[qn, gqk], writes=[qn])
                    x1 = V(qn, 0, [(64, 8), (1, 8)])
                    x2 = V(qn, 8, [(64, 8), (1, 8)])
                    cs = V(cos_t, r * 8, [(0, 8), (1, 8)])
                    sn = V(sin_t, r * 8, [(0, 8), (1, 8)])
                    tv = lambda k: V(rtmp, k * 64, [(8, 8), (1, 8)])
                    op("dve", lambda: nc.vector.tensor_tensor(out=tv(0), in0=x1, in1=cs, op=ALU.mult), reads=[qn, cos_t], writes=[rtmp])
                    op("dve", lambda: nc.vector.tensor_tensor(out=tv(1), in0=x2, in1=sn, op=ALU.mult), reads=[qn, sin_t], writes=[rtmp])
                    op("dve", lambda: nc.vector.tensor_tensor(out=tv(2), in0=x2, in1=cs, op=ALU.mult), reads=[qn, cos_t], writes=[rtmp])
                    op("dve", lambda: nc.vector.tensor_tensor(out=tv(3), in0=x1, in1=sn, op=ALU.mult), reads=[qn, sin_t], writes=[rtmp])
                    op("dve", lambda: nc.vector.tensor_tensor(out=x1, in0=tv(0), in1=tv(1), op=ALU.subtract), reads=[rtmp], writes=[qn])
                    op("dve", lambda: nc.vector.tensor_tensor(out=x2, in0=tv(2), in1=tv(3), op=ALU.add), reads=[rtmp], writes=[qn])
                    op("dve", lambda: nc.vector.tensor_copy(out=qbf[:], in_=qn[:]), reads=[qn], writes=[qbf])
                    pt = psTb[r % 2]
                    for blk in range(4):
                        op("pe", lambda blk=blk, pt=pt: nc.tensor.transpose(pt[:, blk * 128:(blk + 1) * 128], qbf[:, blk * 128:(blk + 1) * 128], ident_bf[:]),
                           reads=[qbf, ident_bf], writes=[pt], inc=(blk == 3))
                    op("dve", lambda s=s, pt=pt: nc.vector.tensor_copy(out=qs[:, :, s * 128:(s + 1) * 128], in_=V(pt, 0, [(128, 4), (1, 128)])), reads=[pt], writes=[qs])

                qkv_mm(0)
                for s in range(4):
                    if s + 1 < 4:
                        qkv_mm(s + 1)
                    qkv_post(s)
                    if tt + 1 < 16:
                        norm_tile_block(tt + 1, s)
                dma("sp", qkT_d.ap().rearrange("(k p) n -> p k n", p=128)[:, :, tt * 512:(tt + 1) * 512], qs[:], reads=[qs], writes=[r_qkT])
                dma("sp", vtok.ap().rearrange("(r p) c -> p r c", p=128)[:, tt * 4:(tt + 1) * 4, :], vs[:], reads=[vs], writes=[r_vtok])
            kb.barrier()

        for st in _phase(upto >= 2):
            cwt = sbuf(st, [128, 18], F32)
            cbt = sbuf(st, [128, 6], F32)
            jrev = sbuf(st, [128, 128], BF16)
            dma("sp", cwt[:], convw, writes=[cwt])
            dma("sp", cbt[:], convb, writes=[cbt])
            dma("sp", jrev[:], jrev_d, writes=[jrev])
            Ut = sbuf(st, [128, NB, 128], BF16)
            X1t = sbuf(st, [128, NB, 128], BF16)
            X2t = sbuf(st, [128, NB, 128], BF16)
            Zt = sbuf(st, [128, NB, 128], BF16)
            X1r = sbuf(st, [128, NB, 128], BF16)
            for g in range(2):
                with contextlib.ExitStack() as s2:
                    pre = sbuf(s2, [128, L + 2], BF16)
                    acc = sbuf(s2, [128, L], F32)
                    hc = sbuf(s2, [128, L], BF16)
                    psb = [psum(s2, [128, 1024], BF16) for _ in range(2)]
                    for si, dst in enumerate([Ut, X1t, X2t]):
                        cbk = si * 2 + g
                        dma("sp", pre[:], hy_pre[cbk * 128:(cbk + 1) * 128, :], reads=[r_hy_pre], writes=[pre])
                        for hh in range(4):
                            a0, a1 = hh * 2048, (hh + 1) * 2048
                            op("dve", lambda a0=a0, a1=a1, cbk=cbk: nc.vector.tensor_scalar(acc[:, a0:a1], pre[:, 1 + a0:1 + a1], cwt[:, cbk * 3 + 1:cbk * 3 + 2], cbt[:, cbk:cbk + 1], ALU.mult, ALU.add),
                               reads=[pre, cwt, cbt], writes=[acc])
                            op("dve", lambda a0=a0, a1=a1, cbk=cbk: nc.vector.scalar_tensor_tensor(out=acc[:, a0:a1], in0=pre[:, a0:a1], scalar=cwt[:, cbk * 3:cbk * 3 + 1], in1=acc[:, a0:a1], op0=ALU.mult, op1=ALU.add),
                               reads=[pre, cwt, acc], writes=[acc])
                            op("dve", lambda a0=a0, a1=a1, cbk=cbk: nc.vector.scalar_tensor_tensor(out=hc[:, a0:a1], in0=pre[:, 2 + a0:2 + a1], scalar=cwt[:, cbk * 3 + 2:cbk * 3 + 3], in1=acc[:, a0:a1], op0=ALU.mult, op1=ALU.add),
                               reads=[pre, cwt, acc], writes=[hc])
                        for j8 in range(8):
                            pb = psb[j8 % 2]
                            for jj in range(8):
                                j = j8 * 8 + jj
                                op("pe", lambda j=j, jj=jj, pb=pb: nc.tensor.transpose(pb[:, jj * 128:(jj + 1) * 128], hc[:, j * 128:(j + 1) * 128], ident_bf[:]),
                                   reads=[hc, ident_bf], writes=[pb], inc=(jj == 7))
                            op("act", lambda j8=j8, pb=pb, dst=dst: nc.scalar.copy(out=dst[:, j8 * 8:(j8 + 1) * 8, :], in_=V(pb, 0, [(128, 8), (1, 128)])), reads=[pb], writes=[dst])
                    psr_ = [psum(s2, [128, 512], F32) for _ in range(2)]
                    for j4 in range(16):
                        pr_ = psr_[j4 % 2]
                        op("pe", lambda j4=j4, pr_=pr_: nc.tensor.matmul(pr_[:], jrev[:], V(X1t, j4 * 512, [(1, 512)]), start=True, stop=True), reads=[jrev, X1t], writes=[pr_])
                        op("dve", lambda j4=j4, pr_=pr_: nc.vector.tensor_copy(out=V(X1r, j4 * 512, [(1, 512)]), in_=pr_[:]), reads=[pr_], writes=[X1r])
                    kb.barrier()
                with contextlib.ExitStack() as s2:
                    Tt = [sbuf(s2, [128, 127 * 128], BF16) for _ in range(2)]
                    psY = [psum(s2, [128, 512], F32) for _ in range(2)]
                    psb = [psum(s2, [128, 1024], BF16) for _ in range(2)]
                    psq = [psum(s2, [128, 512], BF16) for _ in range(2)]
                    yst = [sbuf(s2, [128, 1024], BF16) for _ in range(2)]
                    ysb = [sbuf(s2, [64, 512], BF16) for _ in range(2)]
                    Upc = [sbuf(s2, [128, 190], BF16) for _ in range(2)]
                    for u_ in Upc:
                        op("dve", lambda u_=u_: nc.vector.memset(u_[:], 0.0), writes=[u_])
                    for o in range(2):
                        Uin = Ut if o == 0 else Zt
                        Xm = X1r if o == 0 else X2t
                        Zo = Zt if o == 0 else Ut
                        for c in range(128):
                            tt_ = Tt[c % 2]
                            up = Upc[c % 2]
                            row = o * 256 + g * 128 + c
                            for hq in range(2):
                                q0 = hq * 8128
                                dma("sp" if hq == 0 else "act", tt_[:, q0:q0 + 8128],
                                    bass.AP(tensor=k2, offset=row * 2 * L + o + q0, ap=[[1, 128], [1, 8128]]), reads=[r_k2], writes=[tt_])
                            op("dve", lambda up=up, Uin=Uin, c=c: nc.vector.tensor_copy(out=up[:, 63:127], in_=V(Uin, c, [(128, 64)])), reads=[Uin], writes=[up])
                            py = psY[(c // 4) % 2]
                            sl = (c % 4) * 128
                            order = [0] + [d for k_ in range(1, 64) for d in (k_, -k_)]
                            for n_, d in enumerate(order):
                                qd = (63 - d) * 128 if o == 0 else (d + 63) * 128
                                op("pe", lambda d=d, tt_=tt_, py=py, sl=sl, n_=n_, qd=qd, up=up: nc.tensor.matmul(
                                    py[0:64, sl:sl + 128], up[:, 63 - d:127 - d], tt_[:, qd:qd + 128], start=(n_ == 0), stop=(n_ == 126)),
                                   reads=[tt_, up], writes=[py], inc=(n_ == 126))
                            if c % 4 == 3:
                                c0 = c - 3
                                yb_ = ysb[(c // 4) % 2]
                                pq = psq[(c // 4) % 2]
                                op("act", lambda py=py, yb_=yb_: nc.scalar.copy(out=yb_[:], in_=py[0:64, :]), reads=[py], writes=[yb_])
                                for k4 in range(4):
                                    op("pe", lambda k4=k4, yb_=yb_, pq=pq: nc.tensor.transpose(pq[:, k4 * 64:(k4 + 1) * 64], yb_[:, k4 * 128:(k4 + 1) * 128], ident_bf[0:64, 0:64]),
                                       reads=[yb_, ident_bf], writes=[pq], inc=(k4 == 3))
                                op("dve", lambda pq=pq, c0=c0, Xm=Xm, Zo=Zo: nc.vector.tensor_tensor(out=V(Zo, c0, [(1, 4), (128, 64)]), in0=V(pq, 0, [(64, 4), (1, 64)]),
                                                                                             in1=V(Xm, c0, [(1, 4), (128, 64)]), op=ALU.mult),
                                   reads=[pq, Xm], writes=[Zo])
                    for j8 in range(8):
                        pb = psb[j8 % 2]
                        ys = yst[j8 % 2]
                        for jj in range(8):
                            j = j8 * 8 + jj
                            op("pe", lambda j=j, jj=jj, pb=pb: nc.tensor.transpose(pb[:, jj * 128:(jj + 1) * 128], Ut[:, j, :], ident_bf[:]),
                               reads=[Ut, ident_bf], writes=[pb], inc=(jj == 7))
                        op("act", lambda pb=pb, ys=ys: nc.scalar.copy(out=ys[:], in_=pb[:]), reads=[pb], writes=[ys])
                        dma("sp", ag_in[j8][g * 128:(g + 1) * 128, :], ys[:], reads=[ys], writes=[r_agin])
                    kb.barrier()
            kb.barrier()

        for st in _phase(upto >= 3):
            QZ = [[sbuf(st, [128, L], BF16) for _ in range(2)] for _ in range(2)]
            KT = [sbuf(st, [128, L], BF16) for _ in range(2)]
            Va = sbuf(st, [128, NB, 2, 129], BF16)
            for h in range(2):
                for comp in range(2):
                    oc = 1 - comp
                    op("dve", lambda h=h, comp=comp, oc=oc: nc.vector.memset(QZ[h][comp][oc * 64:(oc + 1) * 64, :], 0.0), writes=[QZ[h][comp]])
                    dma("sp", QZ[h][comp][comp * 64:(comp + 1) * 64, :], qkT_d[h * 128 + comp * 64:h * 128 + (comp + 1) * 64, :], reads=[r_qkT], writes=[QZ[h][comp]])
                dma("act", KT[h][:], qkT_d[(2 + h) * 128:(3 + h) * 128, :], reads=[r_qkT], writes=[KT[h]])
            op("pool", lambda: nc.gpsimd.memset(Va[:], 1.0), writes=[Va])
            vv = vtok.ap().rearrange("(r p) c -> p r c", p=128)
            for h in range(2):
                dma("sp", Va[:, :, h, 0:128], vv[:, :, h * 128:(h + 1) * 128], reads=[r_vtok], writes=[Va])
            Pt = [sbuf(st, [128, 1024], BF16) for _ in range(3)]
            O0 = [sbuf(st, [128, 128], F32) for _ in range(4)]
            Dd = [sbuf(st, [128, 128], F32) for _ in range(4)]
            rz = sbuf(st, [128, 8], F32)
            sj = sbuf(st, [128, 128], F32)
            sq1 = sbuf(st, [128, 8], F32)
            yab = [sbuf(st, [128, 128], BF16) for _ in range(2)]
            yaT = [sbuf(st, [128, 512], BF16) for _ in range(2)]
            psS = [psum(st, [128, 1024], F32) for _ in range(2)]
            psOb = [psum(st, [128, 512], F32) for _ in range(2)]
            psA = psum(st, [128, 1024], BF16)
            it = 0
            for h in range(2):
                for qb in range(16):
                    for comp in range(2):
                        p0 = comp * 64

                        def acc(s4):
                            return psOb[s4 // 2], (s4 % 2) * 256

                        def qk2(p, h=h, qb=qb, comp=comp):
                            ps = psS[p % 2]
                            for hf in range(2):
                                kbk = 2 * p + hf
                                op("pe", lambda: nc.tensor.matmul(ps[:, hf * 512:(hf + 1) * 512], KT[h][:, kbk * 128:(kbk + 1) * 128], QZ[h][comp][:, qb * 512:(qb + 1) * 512], start=True, stop=True),
                                   reads=[KT[h], QZ[h][comp]], writes=[ps], inc=(hf == 1))

                        qk2(0)
                        for p in range(NB // 2):
                            if p + 1 < NB // 2:
                                qk2(p + 1)
                            ps = psS[p % 2]
                            pt = Pt[p % 3]
                            op("act", lambda: nc.scalar.activation(out=pt[:], in_=ps[:], func=AF.Exp, bias=shiftb[:]), reads=[ps, shiftb], writes=[pt])
                            for hf in range(2):
                                kbk = 2 * p + hf
                                for s4 in range(4):
                                    pb_, c0_ = acc(s4)
                                    op("pe", lambda: nc.tensor.matmul(pb_[:, c0_:c0_ + 129], pt[:, hf * 512 + s4 * 128:hf * 512 + (s4 + 1) * 128], Va[:, kbk, h, :],
                                                                      start=(kbk == 0 and s4 % 2 == 0), stop=(kbk == NB - 1), skip_group_check=True),
                                       reads=[pt, Va], writes=[pb_], inc=(hf == 1 and s4 == 3))
                        for s4 in range(4):
                            rc = rz[:, s4:s4 + 1]
                            pb_, c0_ = acc(s4)
                            op("dve", lambda: nc.vector.reciprocal(out=rc, in_=pb_[:, c0_ + 128:c0_ + 129]), reads=[pb_], writes=[rz])
                            if comp == 0:
                                op("dve", lambda: nc.vector.tensor_scalar(O0[s4][:], pb_[:, c0_:c0_ + 128], rc, None, ALU.mult), reads=[pb_, rz], writes=[O0[s4]])
                            else:
                                op("dve", lambda: nc.vector.tensor_tensor(out=rc, in0=rc, in1=nlam[:], op=ALU.mult), reads=[rz, nlam], writes=[rz])
                                op("dve", lambda: nc.vector.scalar_tensor_tensor(out=Dd[s4][:], in0=pb_[:, c0_:c0_ + 128], scalar=rc, in1=O0[s4][:], op0=ALU.mult, op1=ALU.add),
                                   reads=[pb_, rz, O0[s4]], writes=[Dd[s4]])
                    yt = yaT[it % 2]
                    it += 1
                    for s4 in range(4):
                        yb = yab[s4 % 2]
                        op("act", lambda s4=s4: nc.scalar.activation(out=sj[:], in_=Dd[s4][:], func=AF.Square, accum_out=sq1[:, s4:s4 + 1]), reads=[Dd[s4]], writes=[sj, sq1])
                        op("act", lambda s4=s4: nc.scalar.activation(out=sq1[:, 4 + s4:5 + s4], in_=sq1[:, s4:s4 + 1], func=AF.Sqrt, scale=1.0 / 128, bias=epsb[:]), reads=[sq1, epsb], writes=[sq1])
                        op("dve", lambda s4=s4: nc.vector.reciprocal(out=sq1[:, 4 + s4:5 + s4], in_=sq1[:, 4 + s4:5 + s4]), reads=[sq1], writes=[sq1])
                        op("dve", lambda s4=s4, yb=yb: nc.vector.scalar_tensor_tensor(out=yb[:], in0=Dd[s4][:], scalar=sq1[:, 4 + s4:5 + s4], in1=subg[:], op0=ALU.mult, op1=ALU.mult),
                           reads=[Dd[s4], sq1, subg], writes=[yb])
                        op("pe", lambda s4=s4, yb=yb: nc.tensor.transpose(psA[:, s4 * 128:(s4 + 1) * 128], yb[:], ident_bf[:]), reads=[yb, ident_bf], writes=[psA])
                    op("act", lambda yt=yt: nc.scalar.copy(out=yt[:], in_=psA[:, 0:512]), reads=[psA], writes=[yt])
                    dma("sp", ag_in[qb // 2][256 + h * 128:256 + (h + 1) * 128, (qb % 2) * 512:(qb % 2 + 1) * 512], yt[:], reads=[yt], writes=[r_agin])
            kb.barrier()

        if upto >= 4:
            for i in range(8):
                op("pool", lambda i=i: nc.gpsimd.collective_compute("AllGather", ALU.bypass, replica_groups=[[0, 1, 2, 3], [4, 5, 6, 7]],
                                                                    ins=[ag_in[i].ap().opt()], outs=[ag_out[i].ap().opt()]), reads=[r_agin], writes=[r_agout])

        for st in _phase(upto >= 5):
            xs = [sbuf(st, [128, D], F32) for _ in range(4)]
            xns = [sbuf(st, [128, D], BF16) for _ in range(1)] * 2
            junk = sbuf(st, [128, D], BF16)
            hT = sbuf(st, [128, 16, 512], BF16)
            ssq = sbuf(st, [128, 32], F32)
            rt = sbuf(st, [128, 32], F32)
            rstd = sbuf(st, [128, 32], F32)
            wb = [sbuf(st, [128, 16, 256], BF16) for _ in range(3)]
            rT = [sbuf(st, [128, 512], F32) for _ in range(2)]
            psTb = [psum(st, [128, 1024], BF16) for _ in range(2)]
            psM = [psum(st, [128, 512], F32) for _ in range(5)]
            psX = psum(st, [128, 512], F32)

            def back_to_tokens(pm, gcol, dblk, tix):
                rt_ = rT[tix % 2]
                op("act", lambda: nc.scalar.activation(out=rt_[:], in_=pm[:], func=AF.Identity, scale=modT[:, gcol + dblk:gcol + dblk + 1]), reads=[pm, modT], writes=[rt_])
                for s in range(4):
                    op("pe", lambda s=s: nc.tensor.transpose(psX[:, s * 128:(s + 1) * 128], rt_[:, s * 128:(s + 1) * 128], ident_f[:]), reads=[rt_, ident_f], writes=[psX], inc=(s == 3))
                for s in range(4):
                    op("dve", lambda s=s: nc.vector.tensor_tensor(out=xs[s][:, dblk * 128:(dblk + 1) * 128], in0=xs[s][:, dblk * 128:(dblk + 1) * 128], in1=psX[:, s * 128:(s + 1) * 128], op=ALU.add),
                       reads=[psX, xs[s]], writes=[xs[s]])

            wcnt = [0]

            def load_w(name, c0, ncols, kblocks=16):
                w_ = wb[wcnt[0] % 3]
                q_ = "sp" if wcnt[0] % 2 == 0 else "act"
                wcnt[0] += 1
                ch_ = c0 // 256
                dma(q_, w_[:, 0:kblocks, 0:ncols], wbf[name].ap()[ch_ * 128:(ch_ + 1) * 128, :].rearrange("p (i c) -> p i c", c=256), reads=[r_wbf[name]], writes=[w_])
                return w_

            tix = 0
            for tt in range(4):
                for s in range(4):
                    dma("sp", xs[s][:], xo[(tt * 4 + s) * 128:(tt * 4 + s + 1) * 128, :], writes=[xs[s]])
                    r = tt * 4 + s
                    norm_block(xs[s], (rstd, r), junk, xns[s % 2], psTb, hT, s * 128, G1, 0, (ssq, r), (rt, r))
                with contextlib.ExitStack() as s2:
                    ghT = sbuf(s2, [128, 32, 512], BF16)
                    Yh = sbuf(s2, [128, 8, 512], BF16)
                    Ya = sbuf(s2, [128, 8, 512], BF16)
                    yld = [sbuf(s2, [128, 8, 512], BF16)] * 2
                    mT = sbuf(s2, [128, 16, 512], BF16)
                    t1 = sbuf(s2, [128, 512], F32)
                    t2 = sbuf(s2, [128, 512], F32)
                    for n8 in range(16):
                        w_ = load_w("w_g", n8 * 256, 256)
                        for fb in range(2):
                            pm = psM[(n8 * 2 + fb) % 3]
                            for i in range(16):
                                op("pe", lambda i=i, fb=fb, pm=pm, w_=w_: nc.tensor.matmul(pm[:], w_[:, i, fb * 128:(fb + 1) * 128], hT[:, i, :], start=(i == 0), stop=(i == 15)),
                                   reads=[w_, hT, hT.r2], writes=[pm], inc=(i == 15))
                            op("act", lambda pm=pm, n8=n8, fb=fb: nc.scalar.activation(out=ghT[:, n8 * 2 + fb, :], in_=pm[:], func=AF.Sigmoid), reads=[pm], writes=[ghT])
                    for which, dst in ((0, Yh), (1, Ya)):
                        for q in range(4):
                            yl = yld[q % 2]
                            for rr in range(4):
                                src = ag_out[q * 2 + tt // 2].ap()[rr * 512 + which * 256:rr * 512 + which * 256 + 256, (tt % 2) * 512:(tt % 2 + 1) * 512]
                                dma("sp" if rr % 2 == 0 else "act", yl[:, rr * 2:rr * 2 + 2, :], src.rearrange("(k p) n -> p k n", p=128), reads=[r_agout], writes=[yl])
                            if q == 0:
                                op("dve", lambda yl=yl, dst=dst, q=q: nc.vector.tensor_scalar(dst[:], yl[:], selt[:, q:q + 1], None, ALU.mult), reads=[yl, selt], writes=[dst])
                            else:
                                op("dve", lambda yl=yl, dst=dst, q=q: nc.vector.scalar_tensor_tensor(out=dst[:], in0=yl[:], scalar=selt[:, q:q + 1], in1=dst[:], op0=ALU.mult, op1=ALU.add),
                                   reads=[yl, selt, dst], writes=[dst])
                    for n4 in range(8):
                        wh = load_w("w_ph", n4 * 256, 256, 8)
                        wa_ = load_w("w_pa", n4 * 256, 256, 8)
                        for fb in range(2):
                            dblk = n4 * 2 + fb
                            pmh = psM[(2 * dblk) % 4]
                            pma = psM[(2 * dblk + 1) % 4]
                            for k_ in range(8):
                                op("pe", lambda k_=k_, fb=fb: nc.tensor.matmul(pmh[:], wh[:, k_, fb * 128:(fb + 1) * 128], Yh[:, k_, :], start=(k_ == 0), stop=(k_ == 7)), reads=[wh, Yh], writes=[pmh], inc=(k_ == 7))
                            for k_ in range(8):
                                op("pe", lambda k_=k_, fb=fb: nc.tensor.matmul(pma[:], wa_[:, k_, fb * 128:(fb + 1) * 128], Ya[:, k_, :], start=(k_ == 0), stop=(k_ == 7)), reads=[wa_, Ya], writes=[pma], inc=(k_ == 7))
                            op("dve", lambda dblk=dblk: nc.vector.tensor_tensor(out=t1[:], in0=pmh[:], in1=ghT[:, dblk, :], op=ALU.mult), reads=[pmh, ghT], writes=[t1])
                            op("dve", lambda dblk=dblk: nc.vector.tensor_tensor(out=t2[:], in0=pma[:], in1=ghT[:, 16 + dblk, :], op=ALU.mult), reads=[pma, ghT], writes=[t2])
                            op("dve", lambda dblk=dblk: nc.vector.tensor_tensor(out=mT[:, dblk, :], in0=t1[:], in1=t2[:], op=ALU.add), reads=[t1, t2], writes=[mT])
                    for n4 in range(8):
                        w_ = load_w("w_o", n4 * 256, 256)
                        for fb in range(2):
                            dblk = n4 * 2 + fb
                            pm = psM[dblk % 3]
                            for i in range(16):
                                op("pe", lambda i=i, fb=fb, pm=pm, w_=w_: nc.tensor.matmul(pm[:], w_[:, i, fb * 128:(fb + 1) * 128], mT[:, i, :], start=(i == 0), stop=(i == 15)),
                                   reads=[w_, mT], writes=[pm], inc=(i == 15))
                            back_to_tokens(pm, 32, dblk, tix)
                            tix += 1
                    kb.barrier()
                for s in range(4):
                    r = 16 + tt * 4 + s
                    norm_block(xs[s], (rstd, r), junk, xns[s % 2], psTb, hT, s * 128, G2, 48, (ssq, r), (rt, r))
                with contextlib.ExitStack() as s2:
                    actT = sbuf(s2, [128, 44, 512], BF16)
                    sg = [sbuf(s2, [128, 512], F32) for _ in range(2)]
                    wd = [sbuf(s2, [128, 44, 128], BF16) for _ in range(2)]
                    for n11 in range(22):
                        wg_ = load_w("w_gate", n11 * 256, 256)
                        wu_ = load_w("w_up", n11 * 256, 256)
                        for fb in range(2):
                            f = n11 * 2 + fb
                            pg = psM[(2 * f) % 4]
                            pu = psM[(2 * f + 1) % 4]
                            for i in range(16):
                                op("pe", lambda i=i, fb=fb: nc.tensor.matmul(pg[:], wg_[:, i, fb * 128:(fb + 1) * 128], hT[:, i, :], start=(i == 0), stop=(i == 15)), reads=[wg_, hT, hT.r2], writes=[pg], inc=(i == 15))
                            for i in range(16):
                                op("pe", lambda i=i, fb=fb: nc.tensor.matmul(pu[:], wu_[:, i, fb * 128:(fb + 1) * 128], hT[:, i, :], start=(i == 0), stop=(i == 15)), reads=[wu_, hT, hT.r2], writes=[pu], inc=(i == 15))
                            sg_ = sg[f % 2]
                            op("act", lambda sg_=sg_: nc.scalar.activation(out=sg_[:], in_=pg[:], func=AF.Silu), reads=[pg], writes=[sg_])
                            op("dve", lambda sg_=sg_, f=f: nc.vector.tensor_tensor(out=actT[:, f, :], in0=pu[:], in1=sg_[:], op=ALU.mult), reads=[pu, sg_], writes=[actT])
                    for dblk in range(16):
                        wd_ = wd[dblk % 2]
                        dma("sp" if dblk % 2 == 0 else "act", wd_[:], wbf["w_down"].ap()[dblk * 128:(dblk + 1) * 128, :].rearrange("p (f c) -> p f c", c=128), reads=[r_wbf["w_down"]], writes=[wd_])
                        pm = psM[dblk % 3]
                        for f in range(44):
                            op("pe", lambda f=f, pm=pm, wd_=wd_: nc.tensor.matmul(pm[:], wd_[:, f, :], actT[:, f, :], start=(f == 0), stop=(f == 43)), reads=[wd_, actT], writes=[pm], inc=(f == 43))
                        back_to_tokens(pm, 80, dblk, tix)
                        tix += 1
                    kb.barrier()
                for s in range(4):
                    dma("sp", out[(tt * 4 + s) * 128:(tt * 4 + s + 1) * 128, :], xs[s][:], reads=[xs[s]], writes=[r_out])
            kb.barrier()
    return nc


_CACHE = {}
_UPTO = 99


def _bf(a):
    return np.ascontiguousarray(a).astype(ml_dtypes.bfloat16)


def kernel(**inputs):
    f32 = np.float32
    g = {k: np.asarray(v) for k, v in inputs.items()}
    x = g["x"].astype(f32)
    l = 0

    def pp(v):
        return np.ascontiguousarray(v.reshape(16, 128).T).astype(f32)

    s_hy, s_q, s_k, s_v = 3072, 4096, 5120, 6144
    w_in = g["w_in"][l]
    tt = np.linspace(0.0, 1.0, L, dtype=f32)
    idx = np.arange(2 * L)
    posn = np.where(idx >= L, idx - L, np.clip(L - idx, 0, L - 1))
    wv = (2.0 * math.pi / L) * posn.astype(f32)
    bands = np.linspace(1e-4, 15, 16, dtype=f32)
    zT2 = np.concatenate([tt[posn][None, :], np.cos(bands[:, None] * wv[None, :]), -np.sin(bands[:, None] * wv[None, :])], axis=0).astype(f32)
    zT2 = np.ascontiguousarray(np.concatenate([zT2, zT2[:, ::-1]], axis=1))
    tgrid = tt[posn][None, :].astype(f32)
    tgrid = np.ascontiguousarray(np.concatenate([tgrid, tgrid[:, ::-1]], axis=1))
    min_decay = math.log(1e-2) / 0.3
    max_decay = math.log(1e-2) / 1.5
    deltas = np.abs(np.linspace(min_decay, max_decay, 1024, dtype=f32))
    invf = (500000.0 ** (-np.arange(8, dtype=f32) * (2.0 / 16))).astype(f32)
    ident = np.eye(128, dtype=f32)

    shared = {
        "w_ada": np.ascontiguousarray(g["w_ada"][l]).astype(f32),
        "badaT": np.ascontiguousarray(g["b_ada"][l].reshape(96, 128).T).astype(f32),
        "n1g": pp(g["norm1_g"][l]), "n2g": pp(g["norm2_g"][l]),
        "w_g": np.ascontiguousarray(w_in[:, s_v:]).astype(f32),
        "fw1": g["hy_filt_w1"][l].astype(f32), "fw2": g["hy_filt_w2"][l].astype(f32), "fw3": g["hy_filt_w3"][l].astype(f32),
        "fbias": np.ascontiguousarray(np.stack([g["hy_filt_b1"][l], g["hy_filt_b2"][l], g["hy_filt_b3"][l]], axis=1)).astype(f32),
        "ffreq": np.ascontiguousarray(g["hy_filt_freq"][l].T).astype(f32),
        "gqk_bc": np.ascontiguousarray(np.broadcast_to(np.concatenate([np.tile(g["q_norm_g"][l], 4), np.tile(g["k_norm_g"][l], 4)])[None, :], (128, 512))).astype(f32),
        "lamv": np.concatenate([g["lam_q1"][l], g["lam_k1"][l], g["lam_q2"][l], g["lam_k2"][l]])[None, :].astype(f32),
        "subln_bc": np.ascontiguousarray(np.broadcast_to(g["subln_g"][l][None, :], (128, 128))).astype(f32),
        "w_ph": g["w_proj_hy"][l].astype(f32), "w_pa": g["w_proj_att"][l].astype(f32), "w_o": g["w_out"][l].astype(f32),
        "w_gate": g["w_gate"][l].astype(f32), "w_up": g["w_up"][l].astype(f32), "w_down": g["w_down"][l].astype(f32),
        "ident_bf": ident.astype(ml_dtypes.bfloat16), "ident_f": ident,
        "jrev": np.ascontiguousarray(ident[::-1]).astype(ml_dtypes.bfloat16),
        "zT2": zT2, "tgrid": tgrid,
        "invf_bc": np.ascontiguousarray(np.broadcast_to(invf[None, :], (128, 8))).astype(f32),
    }
    in_maps = []
    for core in range(8):
        b, j = core // 4, core % 4
        ch = slice(256 * j, 256 * j + 256)
        hy_cols = np.concatenate([np.arange(256 * j, 256 * j + 256) + o for o in (0, 1024, 2048)])
        qkv_cols = np.concatenate([np.arange(256 * j, 256 * j + 256) + o for o in (s_hy, s_q, s_k)])
        cw = g["hy_conv_w"][l][:, hy_cols]
        convw = np.ascontiguousarray(cw.reshape(3, 6, 128).transpose(2, 1, 0).reshape(128, 18)).astype(f32)
        convb = np.ascontiguousarray(g["hy_conv_b"][l][hy_cols].reshape(6, 128).T).astype(f32)
        fwo = g["hy_filt_w_out"][l].reshape(64, 2, 2, 1024)[:, :, :, ch].reshape(64, 1024)
        hyb = g["hy_bias"][l][:, ch].reshape(1, 512)
        selv = np.zeros((128, 4), f32)
        selv[:, j] = 1.0
        m = dict(shared)
        m.update({
            "xb": np.ascontiguousarray(x[b]), "xo": np.ascontiguousarray(x[b, 2048 * j:2048 * (j + 1)]),
            "cb": pp(g["c"][b]),
            "pos": np.ascontiguousarray(g["positions"][b].reshape(NB, 128).T).astype(np.int32),
            "w_hy": np.ascontiguousarray(w_in[:, hy_cols]).astype(f32),
            "w_qkv": np.ascontiguousarray(w_in[:, qkv_cols]).astype(f32),
            "convw": convw, "convb": convb,
            "fwout": np.ascontiguousarray(fwo).astype(f32),
            "ndelta": np.ascontiguousarray((-deltas[ch]).reshape(2, 128).T).astype(f32),
            "hyb_bc": np.ascontiguousarray(np.broadcast_to(hyb, (128, 512))).astype(f32),
            "hybT": np.ascontiguousarray(g["hy_bias"][l][:, ch].reshape(4, 128).T).astype(f32),
            "sel": selv,
        })
        in_maps.append(m)
    if "nc" not in _CACHE:
        _CACHE["nc"] = build(_UPTO)
    res = run_bass_kernel_spmd(_CACHE["nc"], in_maps, core_ids=list(range(8)))
    outp = np.zeros((2, L, D), f32)
    for core in range(8):
        b, j = core // 4, core % 4
        outp[b, 2048 * j:2048 * (j + 1)] = np.asarray(res.results[core]["out"]).astype(f32)
    return outp
```

```python
import math, contextlib
import numpy as np
import ml_dtypes
import concourse.bass as bass
import concourse.mybir as mybir
from concourse.bass_utils import run_bass_kernel_spmd

F32, BF16, I32 = mybir.dt.float32, mybir.dt.bfloat16, mybir.dt.int32
AF = mybir.ActivationFunctionType
ALU = mybir.AluOpType
AX = mybir.AxisListType

D = 2048
L = 8192
NB = 64
FF = 5632
EPS = 1e-6
LAM_INIT = 0.8 - 0.6 * math.exp(0.0)
TWO_PI = 2.0 * math.pi
PI_B = 3.1415925
NDS = 6
SHIFT = -8.0


def _phase(active):
    if active:
        with contextlib.ExitStack() as st:
            yield st


class Res:
    __slots__ = ("w", "rd")

    def __init__(self):
        self.w = None
        self.rd = {}


class Eng:
    def __init__(self, name, eng, sem):
        self.name, self.eng, self.sem = name, eng, sem
        self.count = 0
        self.waited = {}


class T:
    def __init__(self, t):
        self.t = t
        self.r = Res()
        self.r2 = Res()

    def __getitem__(self, k):
        return self.t[k]


class KB:
    def __init__(self, nc, st):
        self.nc = nc
        mk = lambda n: st.enter_context(nc.semaphore(n))
        self.E = {n: Eng(n, e, mk("s_" + n)) for n, e in
                  [("pe", nc.tensor), ("act", nc.scalar), ("dve", nc.vector), ("pool", nc.gpsimd), ("sp", nc.sync)]}
        self.dsem = {q: [mk(f"d_{q}{i}") for i in range(NDS)] for q in ("sp", "pool", "act")}
        self.dval = {q: [0] * NDS for q in ("sp", "pool", "act")}
        self.di = {q: 0 for q in ("sp", "pool", "act")}

    def _wait(self, E, tok):
        kind, a, v = tok
        if kind == "c":
            if a is E and E.name == "pe":
                return
            sem, key = a.sem, a.name
        else:
            sem, key = a
        if E.waited.get(key, 0) >= v:
            return
        E.eng.wait_ge(sem, v)
        E.waited[key] = v

    def _deps(self, E, reads, writes):
        for r in reads:
            if r.w is not None:
                self._wait(E, r.w)
        for w in writes:
            if w.w is not None:
                self._wait(E, w.w)
            for t in w.rd.values():
                self._wait(E, t)

    def _commit(self, tok, key, reads, writes):
        for r in reads:
            r.rd[key] = tok
        for w in writes:
            w.w = tok
            w.rd = {}

    def op(self, en, fn, reads=(), writes=(), inc=True):
        E = self.E[en]
        reads = [x.r if isinstance(x, T) else x for x in reads]
        writes = [x.r if isinstance(x, T) else x for x in writes]
        self._deps(E, reads, writes)
        inst = fn()
        if inc:
            E.count += 1
            inst.then_inc(E.sem, 1)
            self._commit(("c", E, E.count), E.name, reads, writes)
        else:
            self._commit(("c", E, E.count + 1), E.name, reads, writes)

    def dma(self, q, out, in_, reads=(), writes=(), **kw):
        E = self.E[q]
        reads = [x.r if isinstance(x, T) else x for x in reads]
        writes = [x.r if isinstance(x, T) else x for x in writes]
        self._deps(E, reads, writes)
        i = self.di[q] % NDS
        self.di[q] += 1
        sem = self.dsem[q][i]
        v = self.dval[q][i]
        key = f"{q}{i}"
        if v > 0:
            self._wait(E, ("d", (sem, key), v))
        inst = E.eng.dma_start(out=out, in_=in_, **kw)
        inst.then_inc(sem, 16)
        self.dval[q][i] = v + 16
        self._commit(("d", (sem, key), v + 16), key, reads, writes)

    def barrier(self):
        for E in self.E.values():
            for Fe in self.E.values():
                if Fe is not E and Fe.count > 0:
                    sem, key, v = Fe.sem, Fe.name, Fe.count
                    if E.waited.get(key, 0) < v:
                        E.eng.wait_ge(sem, v)
                        E.waited[key] = v
            for q in self.dsem:
                for i in range(NDS):
                    v = self.dval[q][i]
                    if v > 0:
                        self._wait(E, ("d", (self.dsem[q][i], f"{q}{i}"), v))


def V(t, off, dims, p0=0, pn=None):
    tt = t.t if isinstance(t, T) else t
    base = tt[:]
    ps, pc = base.ap[0]
    if pn is None:
        pn = pc
    return bass.AP(tensor=base.tensor, offset=base.offset + p0 * ps + off, ap=[[ps, pn]] + [list(d) for d in dims])


def build(upto=99, dbg=None):
    nc = bass.Bass("TRN2", target_bir_lowering=False)
    dt_in = lambda n, s, d=F32: nc.dram_tensor(n, s, d, kind="ExternalInput").ap()
    xb = dt_in("xb", [L, D])
    xo = dt_in("xo", [2048, D])
    cb = dt_in("cb", [128, 16])
    posd = dt_in("pos", [128, NB], I32)
    w_ada = dt_in("w_ada", [D, 6 * D])
    badaT = dt_in("badaT", [128, 96])
    n1g = dt_in("n1g", [128, 16])
    n2g = dt_in("n2g", [128, 16])
    w_hy = dt_in("w_hy", [D, 768])
    w_qkv = dt_in("w_qkv", [D, 768])
    w_g = dt_in("w_g", [D, 4096])
    convw = dt_in("convw", [128, 18])
    convb = dt_in("convb", [128, 6])
    fw1 = dt_in("fw1", [33, 64])
    fw2 = dt_in("fw2", [64, 64])
    fw3 = dt_in("fw3", [64, 64])
    fbias = dt_in("fbias", [64, 3])
    ffreq = dt_in("ffreq", [64, 3])
    fwout = dt_in("fwout", [64, 1024])
    ndelta = dt_in("ndelta", [128, 2])
    hyb_bc = dt_in("hyb_bc", [128, 512])
    gqk_bc = dt_in("gqk_bc", [128, 512])
    lamv = dt_in("lamv", [1, 256])
    subln_bc = dt_in("subln_bc", [128, 128])
    w_ph = dt_in("w_ph", [1024, D])
    w_pa = dt_in("w_pa", [1024, D])
    w_o = dt_in("w_o", [D, D])
    w_gate = dt_in("w_gate", [D, FF])
    w_up = dt_in("w_up", [D, FF])
    w_down = dt_in("w_down", [FF, D])
    sel = dt_in("sel", [128, 4])
    ident_bf_d = dt_in("ident_bf", [128, 128], BF16)
    ident_f_d = dt_in("ident_f", [128, 128])
    zT2 = dt_in("zT2", [33, 4 * L])
    tgrid = dt_in("tgrid", [1, 4 * L])
    jrev_d = dt_in("jrev", [128, 128], BF16)
    hybT_d = dt_in("hybT", [128, 4])
    invf_bc = dt_in("invf_bc", [128, 8])
    out = nc.dram_tensor("out", [2048, D], F32, kind="ExternalOutput").ap()

    hy_pre = nc.dram_tensor("hy_pre", [768, L + 2], BF16)
    qkT_d = nc.dram_tensor("qkT_d", [512, L], BF16)
    vtok = nc.dram_tensor("vtok", [L, 256], BF16)
    k2 = nc.dram_tensor("k2", [512, 2 * L], BF16)
    ag_in_hy = [nc.dram_tensor(f"ag_in_hy{i}", [256, 1024], BF16) for i in range(8)]
    ag_in_at = [nc.dram_tensor(f"ag_in_at{i}", [256, 1024], BF16) for i in range(8)]
    ag_out_hy = [nc.dram_tensor(f"ag_out_hy{i}", [1024, 1024], BF16) for i in range(8)]
    ag_out_at = [nc.dram_tensor(f"ag_out_at{i}", [1024, 1024], BF16) for i in range(8)]
    r_agin_at, r_agout_at = Res(), Res()
    r_hy_pre, r_qkT, r_vtok, r_k2, r_agin, r_agout, r_out = [Res() for _ in range(7)]
    wsrc = {"w_g": w_g, "w_ph": w_ph, "w_pa": w_pa, "w_o": w_o, "w_gate": w_gate, "w_up": w_up, "w_down": w_down}
    wcw = {k: (128 if k == "w_down" else 256) for k in wsrc}
    wkb = {k: v.shape[0] // 128 for k, v in wsrc.items()}
    wbf = {k: nc.dram_tensor(k + "_bf", [(v.shape[1] // wcw[k]) * 128, wkb[k] * wcw[k]], BF16) for k, v in wsrc.items()}
    r_wbf = {k: Res() for k in wsrc}

    with contextlib.ExitStack() as top:
        kb = KB(nc, top)
        op, dma = kb.op, kb.dma
        _cnt = [0]

        def wrap(dst, src, shift, tmp, dres, sres, tres):
            op("dve", lambda: nc.vector.tensor_scalar(dst, src, float(shift), None, ALU.add), reads=[sres], writes=[dres])
            op("dve", lambda: nc.vector.tensor_scalar(tmp, dst, PI_B, -TWO_PI, ALU.is_gt, ALU.mult), reads=[dres], writes=[tres])
            op("dve", lambda: nc.vector.tensor_tensor(out=dst, in0=dst, in1=tmp, op=ALU.add), reads=[dres, tres], writes=[dres])
            op("dve", lambda: nc.vector.tensor_scalar(tmp, dst, -PI_B, TWO_PI, ALU.is_lt, ALU.mult), reads=[dres], writes=[tres])
            op("dve", lambda: nc.vector.tensor_tensor(out=dst, in0=dst, in1=tmp, op=ALU.add), reads=[dres, tres], writes=[dres])

        def sbuf(st, shape, dt, name=None):
            _cnt[0] += 1
            return T(st.enter_context(nc.sbuf_tensor(f"{name or 'sb'}_{_cnt[0]}", list(shape), dt)))

        def psum(st, shape, dt, name=None):
            _cnt[0] += 1
            return T(st.enter_context(nc.psum_tensor(f"{name or 'ps'}_{_cnt[0]}", list(shape), dt)))

        ident_bf = sbuf(top, [128, 128], BF16, "identb")
        ident_f = sbuf(top, [128, 128], F32, "identf")
        modT = sbuf(top, [128, 96], F32, "modT")
        G1 = sbuf(top, [128, 16], F32, "G1")
        G2 = sbuf(top, [128, 16], F32, "G2")
        cos_t = sbuf(top, [128, NB * 8], F32, "cos")
        sin_t = sbuf(top, [128, NB * 8], F32, "sin")
        nlam = sbuf(top, [128, 1], F32, "nlam")
        epsb = sbuf(top, [128, 1], F32, "epsb")
        shiftb = sbuf(top, [128, 1], F32, "shiftb")
        gqk = sbuf(top, [128, 512], F32, "gqk")
        subg = sbuf(top, [128, 128], F32, "subg")
        selt = sbuf(top, [128, 4], F32, "selt")
        dma("sp", ident_bf[:], ident_bf_d, writes=[ident_bf])
        dma("sp", ident_f[:], ident_f_d, writes=[ident_f])
        dma("sp", gqk[:], gqk_bc, writes=[gqk])
        dma("sp", subg[:], subln_bc, writes=[subg])
        dma("sp", selt[:], sel, writes=[selt])
        op("dve", lambda: nc.vector.memset(epsb[:], EPS), writes=[epsb])
        op("dve", lambda: nc.vector.memset(shiftb[:], SHIFT), writes=[shiftb])
        op("dve", lambda: nc.vector.tensor_scalar(gqk[:, 0:256], gqk[:, 0:256], 0.125, None, ALU.mult), reads=[gqk], writes=[gqk])
        op("dve", lambda: nc.vector.tensor_scalar(subg[:], subg[:], 1.0 - LAM_INIT, None, ALU.mult), reads=[subg], writes=[subg])

        def precast():
            for k_, src_ in wsrc.items():
                cw_ = wcw[k_]
                sv = src_.rearrange("(i p) n -> p i n", p=128)
                for ch_ in range(src_.shape[1] // cw_):
                    dv = wbf[k_].ap()[ch_ * 128:(ch_ + 1) * 128, :].rearrange("p (i c) -> p i c", c=cw_)
                    dma("pool", dv, sv[:, :, ch_ * cw_:(ch_ + 1) * cw_], writes=[r_wbf[k_]])

        with contextlib.ExitStack() as st:
            cS = sbuf(st, [128, 16], F32)
            wa = [sbuf(st, [128, 16, 512], F32) for _ in range(2)]
            modrow = sbuf(st, [1, 6 * D], F32)
            bT = sbuf(st, [128, 96], F32)
            one11 = sbuf(st, [1, 128], F32)
            n1t = sbuf(st, [128, 16], F32)
            n2t = sbuf(st, [128, 16], F32)
            psr = [psum(st, [128, 512], F32) for _ in range(2)]
            psT = psum(st, [128, 512], F32)
            dma("sp", cS[:], cb, writes=[cS])
            dma("sp", bT[:], badaT, writes=[bT])
            dma("sp", n1t[:], n1g, writes=[n1t])
            dma("sp", n2t[:], n2g, writes=[n2t])
            op("dve", lambda: nc.vector.memset(one11[:], 1.0), writes=[one11])
            op("act", lambda: nc.scalar.activation(out=cS[:], in_=cS[:], func=AF.Silu), reads=[cS], writes=[cS])
            wav = w_ada.rearrange("(i p) n -> p i n", p=128)
            for n in range(24):
                w_ = wa[n % 2]
                dma("sp" if n % 2 == 0 else "act", w_[:], wav[:, :, n * 512:(n + 1) * 512], writes=[w_])
                pr = psr[n % 2]
                for i in range(16):
                    op("pe", lambda i=i, w_=w_, pr=pr: nc.tensor.matmul(pr[0:1, :], cS[:, i:i + 1], w_[:, i, :], start=(i == 0), stop=(i == 15)),
                       reads=[cS, w_], writes=[pr], inc=(i == 15))
                op("dve", lambda n=n, pr=pr: nc.vector.tensor_copy(out=modrow[0:1, n * 512:(n + 1) * 512], in_=pr[0:1, :]), reads=[pr], writes=[modrow])
            for k in range(96):
                op("pe", lambda k=k: nc.tensor.matmul(psT[:, k:k + 1], modrow[0:1, k * 128:(k + 1) * 128], one11[0:1, 0:1], start=True, stop=True),
                   reads=[modrow, one11], writes=[psT], inc=(k == 95))
            op("dve", lambda: nc.vector.tensor_tensor(out=modT[:], in0=psT[:, 0:96], in1=bT[:], op=ALU.add), reads=[psT, bT], writes=[modT])
            op("dve", lambda: nc.vector.scalar_tensor_tensor(out=G1[:], in0=modT[:, 16:32], scalar=1.0, in1=n1t[:], op0=ALU.add, op1=ALU.mult), reads=[modT, n1t], writes=[G1])
            op("dve", lambda: nc.vector.scalar_tensor_tensor(out=G2[:], in0=modT[:, 64:80], scalar=1.0, in1=n2t[:], op0=ALU.add, op1=ALU.mult), reads=[modT, n2t], writes=[G2])
            posi = sbuf(st, [128, NB], I32)
            posf = sbuf(st, [128, NB], F32)
            invf = sbuf(st, [128, 8], F32)
            ang = sbuf(st, [128, NB * 8], F32)
            angi = sbuf(st, [128, NB * 8], I32)
            angf = sbuf(st, [128, NB * 8], F32)
            dma("sp", posi[:], posd, writes=[posi])
            dma("sp", invf[:], invf_bc, writes=[invf])
            op("dve", lambda: nc.vector.tensor_copy(out=posf[:], in_=posi[:]), reads=[posi], writes=[posf])
            op("dve", lambda: nc.vector.tensor_tensor(out=V(ang, 0, [(8, NB), (1, 8)]), in0=V(posf, 0, [(1, NB), (0, 8)]), in1=V(invf, 0, [(0, NB), (1, 8)]), op=ALU.mult),
               reads=[posf, invf], writes=[ang])
            op("dve", lambda: nc.vector.tensor_scalar(ang[:], ang[:], 1.0 / TWO_PI, None, ALU.mult), reads=[ang], writes=[ang])
            op("dve", lambda: nc.vector.tensor_copy(out=angi[:], in_=ang[:]), reads=[ang], writes=[angi])
            op("dve", lambda: nc.vector.tensor_copy(out=angf[:], in_=angi[:]), reads=[angi], writes=[angf])
            op("dve", lambda: nc.vector.tensor_tensor(out=ang[:], in0=ang[:], in1=angf[:], op=ALU.subtract), reads=[ang, angf], writes=[ang])
            op("dve", lambda: nc.vector.tensor_scalar(ang[:], ang[:], TWO_PI, None, ALU.mult), reads=[ang], writes=[ang])
            wtmp = sbuf(st, [128, NB * 8], F32)
            wrap(angf[:], ang[:], 0.0, wtmp[:], angf, ang, wtmp)
            op("act", lambda: nc.scalar.activation(out=sin_t[:], in_=angf[:], func=AF.Sin), reads=[angf], writes=[sin_t])
            wrap(angf[:], ang[:], math.pi / 2, wtmp[:], angf, ang, wtmp)
            op("act", lambda: nc.scalar.activation(out=cos_t[:], in_=angf[:], func=AF.Sin), reads=[angf], writes=[cos_t])
            lv = sbuf(st, [1, 256], F32)
            lp = sbuf(st, [1, 128], F32)
            ls = sbuf(st, [1, 2], F32)
            dma("sp", lv[:], lamv, writes=[lv])
            op("dve", lambda: nc.vector.tensor_tensor(out=V(lp, 0, [(64, 2), (1, 64)], 0, 1), in0=V(lv, 0, [(128, 2), (1, 64)], 0, 1), in1=V(lv, 64, [(128, 2), (1, 64)], 0, 1), op=ALU.mult),
               reads=[lv], writes=[lp])
            op("dve", lambda: nc.vector.tensor_reduce(out=ls[:], in_=V(lp, 0, [(64, 2), (1, 64)], 0, 1), axis=AX.X, op=ALU.add), reads=[lp], writes=[ls])
            op("act", lambda: nc.scalar.activation(out=ls[:], in_=ls[:], func=AF.Exp), reads=[ls], writes=[ls])
            op("dve", lambda: nc.vector.tensor_tensor(out=ls[0:1, 0:1], in0=ls[0:1, 1:2], in1=ls[0:1, 0:1], op=ALU.subtract), reads=[ls], writes=[ls])
            op("dve", lambda: nc.vector.tensor_scalar(ls[0:1, 0:1], ls[0:1, 0:1], -LAM_INIT, None, ALU.add), reads=[ls], writes=[ls])
            op("pe", lambda: nc.tensor.matmul(psT[:, 100:101], one11[0:1, :], ls[0:1, 0:1], start=True, stop=True), reads=[one11, ls, modT], writes=[psT])
            op("dve", lambda: nc.vector.tensor_copy(out=nlam[:], in_=psT[:, 100:101]), reads=[psT], writes=[nlam])
            kb.barrier()

        def S1(i):
            return modT[:, i:i + 1]

        def norm_block(xt, rstd_col, junk, xn, psTb, hT, col0, Gm, s_off, ssq_col, rt_col):
            op("act", lambda: nc.scalar.activation(out=junk[:], in_=xt[:], func=AF.Square, accum_out=ssq_col[0][:, ssq_col[1]:ssq_col[1] + 1]),
               reads=[xt], writes=[junk, ssq_col[0]])
            op("act", lambda: nc.scalar.activation(out=rt_col[0][:, rt_col[1]:rt_col[1] + 1], in_=ssq_col[0][:, ssq_col[1]:ssq_col[1] + 1], func=AF.Sqrt, scale=1.0 / D, bias=epsb[:]),
               reads=[ssq_col[0], epsb], writes=[rt_col[0]])
            op("dve", lambda: nc.vector.reciprocal(out=rstd_col[0][:, rstd_col[1]:rstd_col[1] + 1], in_=rt_col[0][:, rt_col[1]:rt_col[1] + 1]),
               reads=[rt_col[0]], writes=[rstd_col[0]])
            op("act", lambda: nc.scalar.activation(out=xn[:], in_=xt[:], func=AF.Identity, scale=rstd_col[0][:, rstd_col[1]:rstd_col[1] + 1]),
               reads=[xt, rstd_col[0]], writes=[xn])
            for half in range(2):
                ph_ = psTb[half]
                for i8 in range(8):
                    i = half * 8 + i8
                    op("pe", lambda i=i, i8=i8, ph_=ph_: nc.tensor.transpose(ph_[:, i8 * 128:(i8 + 1) * 128], xn[:, i * 128:(i + 1) * 128], ident_bf[:]),
                       reads=[xn, ident_bf], writes=[ph_], inc=(i8 == 7))
                for i8 in range(8):
                    i = half * 8 + i8
                    if half == 0:
                        op("act", lambda i=i, i8=i8, ph_=ph_: nc.scalar.activation(out=hT[:, i, col0:col0 + 128], in_=ph_[:, i8 * 128:(i8 + 1) * 128], func=AF.Identity,
                                                                                  scale=Gm[:, i:i + 1], bias=modT[:, s_off + i:s_off + i + 1]),
                           reads=[ph_, Gm, modT], writes=[hT.r])
                    else:
                        op("dve", lambda i=i, i8=i8, ph_=ph_: nc.vector.tensor_scalar(hT[:, i, col0:col0 + 128], ph_[:, i8 * 128:(i8 + 1) * 128], Gm[:, i:i + 1],
                                                                                     modT[:, s_off + i:s_off + i + 1], ALU.mult, ALU.add),
                           reads=[ph_, Gm, modT], writes=[hT.r2])

        for st in _phase(upto >= 1):
            whyb = sbuf(st, [128, 16, 768], BF16)
            wqkvb = sbuf(st, [128, 16, 768], BF16)
            dma("pool", whyb[:], w_hy.rearrange("(i p) n -> p i n", p=128), writes=[whyb])
            dma("pool", wqkvb[:], w_qkv.rearrange("(i p) n -> p i n", p=128), writes=[wqkvb])
            wob = sbuf(st, [64, 1024], BF16)
            dma("pool", wob[:], fwout, writes=[wob])
            precast()
            zt = sbuf(st, [128, 2], BF16)
            op("dve", lambda: nc.vector.memset(zt[:], 0.0), writes=[zt])
            for cbk in range(6):
                dma("sp", bass.AP(tensor=hy_pre, offset=cbk * 128 * (L + 2), ap=[[L + 2, 128], [L + 1, 2], [1, 1]]), V(zt, 0, [(1, 2), (1, 1)]), reads=[zt], writes=[r_hy_pre], allow_slow_non_contiguous=True)
            xts = [sbuf(st, [128, D], F32) for _ in range(3)]
            xns = [sbuf(st, [128, D], BF16) for _ in range(2)]
            junk = sbuf(st, [128, D], BF16)
            hTs = [sbuf(st, [128, 16, 512], BF16) for _ in range(2)]
            ssq = sbuf(st, [128, NB], F32)
            rt = sbuf(st, [128, NB], F32)
            rstd = sbuf(st, [128, NB], F32)
            hst = [sbuf(st, [128, 512], BF16) for _ in range(2)]
            sqj = sbuf(st, [128, 512], F32)
            ss8 = sbuf(st, [128, 8], F32)
            rt8 = sbuf(st, [128, 8], F32)
            rs8 = sbuf(st, [128, 8], F32)
            qn = sbuf(st, [128, 512], F32)
            rtmp = sbuf(st, [128, 4 * 64], F32)
            qbf = sbuf(st, [128, 512], BF16)
            qst = [sbuf(st, [128, 4, 512], BF16) for _ in range(2)]
            vst = [sbuf(st, [128, 4, 256], BF16) for _ in range(2)]
            psTb = [psum(st, [128, 1024], BF16) for _ in range(2)]
            psH = [psum(st, [128, 512], F32) for _ in range(2)]
            psQ0s = [psum(st, [128, 512], F32) for _ in range(2)]
            psQ1s = [psum(st, [128, 512], F32) for _ in range(2)]
            w1t = sbuf(st, [33, 64], F32)
            w2t = sbuf(st, [64, 64], F32)
            w3t = sbuf(st, [64, 64], F32)
            fbt = sbuf(st, [64, 3], F32)
            fft = sbuf(st, [64, 3], F32)
            fbs = sbuf(st, [64, 3], F32)
            ndl = sbuf(st, [128, 2], F32)
            hbT = sbuf(st, [128, 4], F32)
            fsets = [dict(zl=sbuf(st, [33, 512], F32), s3=sbuf(st, [64, 512], BF16), aa=[sbuf(st, [64, 512], F32) for _ in range(2)],
                          wtm=sbuf(st, [64, 512], F32), tg=sbuf(st, [128, 512], F32), dec=[sbuf(st, [128, 512], F32) for _ in range(2)],
                          k2t=[sbuf(st, [128, 512], BF16) for _ in range(2)]) for _ in range(2)]
            pmf = psH[1]
            dma("sp", w1t[:], fw1, writes=[w1t])
            dma("sp", w2t[:], fw2, writes=[w2t])
            dma("sp", w3t[:], fw3, writes=[w3t])
            dma("sp", fbt[:], fbias, writes=[fbt])
            dma("sp", fft[:], ffreq, writes=[fft])
            dma("sp", ndl[:], ndelta, writes=[ndl])
            dma("sp", hbT[:], hybT_d, writes=[hbT])
            op("dve", lambda: nc.vector.tensor_tensor(out=fbs[:], in0=fbt[:], in1=fft[:], op=ALU.mult), reads=[fbt, fft], writes=[fbs])

            def filt_gen(ti, fs):
                rev = ti >= 32
                tl = ti % 32
                o = 0 if rev else 1
                zl, s3, tgt, wtm = fs["zl"], fs["s3"], fs["tg"], fs["wtm"]
                dma("sp", zl[:], zT2[:, ti * 512:(ti + 1) * 512], writes=[zl])
                dma("sp", tgt[:], bass.AP(tensor=tgrid.tensor, offset=ti * 512, ap=[[0, 128], [1, 512]]), writes=[tgt])
                cur = None
                for l_, wl in enumerate([w1t, w2t, w3t]):
                    a_ = fs["aa"][l_ % 2]
                    if l_ == 0:
                        op("pe", lambda: nc.tensor.matmul(pmf[0:64, :], w1t[:], zl[:], start=True, stop=True), reads=[w1t, zl], writes=[pmf])
                    else:
                        op("pe", lambda: nc.tensor.matmul(pmf[0:64, :], wl[:], cur[:], start=True, stop=True), reads=[wl, cur], writes=[pmf])
                    op("act", lambda: nc.scalar.activation(out=a_[:], in_=pmf[0:64, :], func=AF.Identity, scale=fft[:, l_:l_ + 1], bias=fbs[:, l_:l_ + 1]),
                       reads=[pmf, fft, fbs], writes=[a_])
                    wrap(a_[:], a_[:], 0.0, wtm[:], a_, a_, wtm)
                    if l_ < 2:
                        op("act", lambda: nc.scalar.activation(out=a_[:], in_=a_[:], func=AF.Sin), reads=[a_], writes=[a_])
                        cur = a_
                    else:
                        op("act", lambda: nc.scalar.activation(out=s3[:], in_=a_[:], func=AF.Sin), reads=[a_], writes=[s3])
                    yield
                rdir = (1 if tl < 16 else 0) if not rev else (0 if tl < 16 else 1)
                for cbk in range(2):
                    dc = fs["dec"][cbk]
                    kt = fs["k2t"][cbk]
                    op("act", lambda: nc.scalar.activation(out=dc[:], in_=tgt[:], func=AF.Exp, scale=ndl[:, cbk:cbk + 1]), reads=[tgt, ndl], writes=[dc])
                    c0 = o * 512 + rdir * 256 + cbk * 128
                    op("pe", lambda: nc.tensor.matmul(pmf[:], wob[:, c0:c0 + 128], s3[:], start=True, stop=True), reads=[wob, s3], writes=[pmf])
                    op("dve", lambda: nc.vector.tensor_tensor(out=kt[:], in0=pmf[:], in1=dc[:], op=ALU.mult), reads=[pmf, dc], writes=[kt])
                    zc = (0 if tl == 0 else None) if not rev else (511 if tl == 31 else None)
                    bc_ = (0 if tl == 16 else None) if not rev else (511 if tl == 15 else None)
                    if zc is not None:
                        op("dve", lambda: nc.vector.memset(kt[:, zc:zc + 1], 0.0), writes=[kt])
                    if bc_ is not None:
                        op("dve", lambda: nc.vector.tensor_scalar(kt[:, bc_:bc_ + 1], kt[:, bc_:bc_ + 1], hbT[:, o * 2 + cbk:o * 2 + cbk + 1], None, ALU.add),
                           reads=[kt, hbT], writes=[kt])
                    row0 = o * 256 + cbk * 128
                    dma("sp", k2[row0:row0 + 128, tl * 512:(tl + 1) * 512], kt[:], reads=[kt], writes=[r_k2])
                    yield

            fq = {"next": 0, "gens": [None, None], "turn": 0}

            def filt_step(n=1):
                for _ in range(n):
                    k_ = fq["turn"]
                    fq["turn"] = 1 - k_
                    for _try in range(2):
                        if fq["gens"][k_] is None:
                            if fq["next"] >= 64:
                                break
                            fq["gens"][k_] = filt_gen(fq["next"], fsets[k_])
                            fq["next"] += 1
                        try:
                            next(fq["gens"][k_])
                            break
                        except StopIteration:
                            fq["gens"][k_] = None

            def load_x(r):
                if r < NB:
                    dma("act", xts[r % 3][:], xb[r * 128:(r + 1) * 128, :], writes=[xts[r % 3]])

            load_x(0)
            load_x(1)

            def norm_tile_block(tt_, s_):
                r_ = tt_ * 4 + s_
                load_x(r_ + 2)
                norm_block(xts[r_ % 3], (rstd, r_), junk, xns[r_ % 2], psTb, hTs[tt_ % 2], s_ * 128, G1, 0, (ssq, r_), (rt, r_))

            for s in range(4):
                norm_tile_block(0, s)
            for tt in range(16):
                hT = hTs[tt % 2]
                for cbk in range(6):
                    ph = psH[0]
                    for i in range(16):
                        op("pe", lambda i=i, cbk=cbk, ph=ph: nc.tensor.matmul(ph[:], whyb[:, i, cbk * 128:(cbk + 1) * 128], hT[:, i, :], start=(i == 0), stop=(i == 15)),
                           reads=[whyb, hT, hT.r2], writes=[ph], inc=(i == 15))
                    hs = hst[cbk % 2]
                    if cbk % 2 == 0:
                        op("act", lambda ph=ph, hs=hs: nc.scalar.copy(out=hs[:], in_=ph[:]), reads=[ph], writes=[hs])
                    else:
                        op("dve", lambda ph=ph, hs=hs: nc.vector.tensor_copy(out=hs[:], in_=ph[:]), reads=[ph], writes=[hs])
                    dma("sp", hy_pre[cbk * 128:(cbk + 1) * 128, 1 + tt * 512:1 + (tt + 1) * 512], hs[:], reads=[hs], writes=[r_hy_pre])
                    filt_step(2)
                qs = qst[tt % 2]
                vs = vst[tt % 2]
                def qkv_mm(s):
                    r = tt * 4 + s
                    psQ0 = psQ0s[s % 2]
                    psQ1 = psQ1s[s % 2]
                    for i in range(16):
                        op("pe", lambda i=i, s=s, psQ0=psQ0: nc.tensor.matmul(psQ0[:], hT[:, i, s * 128:(s + 1) * 128], wqkvb[:, i, 0:512], start=(i == 0), stop=(i == 15)),
                           reads=[hT, hT.r2, wqkvb], writes=[psQ0], inc=(i == 15))
                    for i in range(16):
                        op("pe", lambda i=i, s=s, psQ1=psQ1: nc.tensor.matmul(psQ1[:, 0:256], hT[:, i, s * 128:(s + 1) * 128], wqkvb[:, i, 512:768], start=(i == 0), stop=(i == 15)),
                           reads=[hT, hT.r2, wqkvb], writes=[psQ1], inc=(i == 15))
                    op("act", lambda s=s, psQ1=psQ1: nc.scalar.copy(out=vs[:, s, :], in_=psQ1[:, 0:256]), reads=[psQ1], writes=[vs])
                    filt_step(2)

                def qkv_post(s):
                    r = tt * 4 + s
                    psQ0 = psQ0s[s % 2]
                    op("act", lambda psQ0=psQ0: nc.scalar.activation(out=sqj[:], in_=psQ0[:], func=AF.Square), reads=[psQ0], writes=[sqj])
                    op("dve", lambda: nc.vector.tensor_reduce(out=ss8[:], in_=V(sqj, 0, [(64, 8), (1, 64)]), axis=AX.X, op=ALU.add), reads=[sqj], writes=[ss8])
                    op("act", lambda: nc.scalar.activation(out=rt8[:], in_=ss8[:], func=AF.Sqrt, scale=1.0 / 64, bias=epsb[:]), reads=[ss8, epsb], writes=[rt8])
                    op("dve", lambda: nc.vector.reciprocal(out=rs8[:], in_=rt8[:]), reads=[rt8], writes=[rs8])
                    op("dve", lambda psQ0=psQ0: nc.vector.tensor_tensor(out=V(qn, 0, [(64, 8), (1, 64)]), in0=V(psQ0, 0, [(64, 8), (1, 64)]), in1=V(rs8, 0, [(1, 8), (0, 64)]), op=ALU.mult),
                       reads=[psQ0, rs8], writes=[qn])
                    op("dve", lambda: nc.vector.tensor_tensor(out=qn[:], in0=qn[:], in1=gqk[:], op=ALU.mult), reads=[qn, gqk], writes=[qn])
                    x1 = V(qn, 0, [(64, 8), (1, 8)])
                    x2 = V(qn, 8, [(64, 8), (1, 8)])
                    cs = V(cos_t, r * 8, [(0, 8), (1, 8)])
                    sn = V(sin_t, r * 8, [(0, 8), (1, 8)])
                    tv = lambda k: V(rtmp, k * 64, [(8, 8), (1, 8)])
                    op("dve", lambda: nc.vector.tensor_tensor(out=tv(0), in0=x1, in1=cs, op=ALU.mult), reads=[qn, cos_t], writes=[rtmp])
                    op("dve", lambda: nc.vector.tensor_tensor(out=tv(1), in0=x2, in1=sn, op=ALU.mult), reads=[qn, sin_t], writes=[rtmp])
                    op("dve", lambda: nc.vector.tensor_tensor(out=tv(2), in0=x2, in1=cs, op=ALU.mult), reads=[qn, cos_t], writes=[rtmp])
                    op("dve", lambda: nc.vector.tensor_tensor(out=tv(3), in0=x1, in1=sn, op=ALU.mult), reads=[qn, sin_t], writes=[rtmp])
                    op("dve", lambda: nc.vector.tensor_tensor(out=x1, in0=tv(0), in1=tv(1), op=ALU.subtract), reads=[rtmp], writes=[qn])
                    op("dve", lambda: nc.vector.tensor_tensor(out=x2, in0=tv(2), in1=tv(3), op=ALU.add), reads=[rtmp], writes=[qn])
                    op("dve", lambda: nc.vector.tensor_copy(out=qbf[:], in_=qn[:]), reads=[qn], writes=[qbf])
                    pt = psTb[r % 2]
                    for blk in range(4):
                        op("pe", lambda blk=blk, pt=pt: nc.tensor.transpose(pt[:, blk * 128:(blk + 1) * 128], qbf[:, blk * 128:(blk + 1) * 128], ident_bf[:]),
                           reads=[qbf, ident_bf], writes=[pt], inc=(blk == 3))
                    op("dve", lambda s=s, pt=pt: nc.vector.tensor_copy(out=qs[:, :, s * 128:(s + 1) * 128], in_=V(pt, 0, [(128, 4), (1, 128)])), reads=[pt], writes=[qs])

                qkv_mm(0)
                for s in range(4):
                    if s + 1 < 4:
                        qkv_mm(s + 1)
                    qkv_post(s)
                    if tt + 1 < 16:
                        norm_tile_block(tt + 1, s)
                dma("sp", qkT_d.ap().rearrange("(k p) n -> p k n", p=128)[:, :, tt * 512:(tt + 1) * 512], qs[:], reads=[qs], writes=[r_qkT])
                dma("sp", vtok.ap().rearrange("(r p) c -> p r c", p=128)[:, tt * 4:(tt + 1) * 4, :], vs[:], reads=[vs], writes=[r_vtok])
            kb.barrier()

        for st in _phase(upto >= 2):
            cwt = sbuf(st, [128, 18], F32)
            cbt = sbuf(st, [128, 6], F32)
            jrev = sbuf(st, [128, 128], BF16)
            dma("sp", cwt[:], convw, writes=[cwt])
            dma("sp", cbt[:], convb, writes=[cbt])
            dma("sp", jrev[:], jrev_d, writes=[jrev])
            Ut = sbuf(st, [128, NB, 128], BF16)
            X1t = sbuf(st, [128, NB, 128], BF16)
            X2t = sbuf(st, [128, NB, 128], BF16)
            Zt = sbuf(st, [128, NB, 128], BF16)
            X1r = sbuf(st, [128, NB, 128], BF16)
            for g in range(2):
                with contextlib.ExitStack() as s2:
                    pre = sbuf(s2, [128, L + 2], BF16)
                    acc = sbuf(s2, [128, L], F32)
                    hc = sbuf(s2, [128, L], BF16)
                    psb = [psum(s2, [128, 1024], BF16) for _ in range(2)]
                    for si, dst in enumerate([Ut, X1t, X2t]):
                        cbk = si * 2 + g
                        dma("sp", pre[:], hy_pre[cbk * 128:(cbk + 1) * 128, :], reads=[r_hy_pre], writes=[pre])
                        for hh in range(4):
                            a0, a1 = hh * 2048, (hh + 1) * 2048
                            op("dve", lambda a0=a0, a1=a1, cbk=cbk: nc.vector.tensor_scalar(acc[:, a0:a1], pre[:, 1 + a0:1 + a1], cwt[:, cbk * 3 + 1:cbk * 3 + 2], cbt[:, cbk:cbk + 1], ALU.mult, ALU.add),
                               reads=[pre, cwt, cbt], writes=[acc])
                            op("dve", lambda a0=a0, a1=a1, cbk=cbk: nc.vector.scalar_tensor_tensor(out=acc[:, a0:a1], in0=pre[:, a0:a1], scalar=cwt[:, cbk * 3:cbk * 3 + 1], in1=acc[:, a0:a1], op0=ALU.mult, op1=ALU.add),
                               reads=[pre, cwt, acc], writes=[acc])
                            op("dve", lambda a0=a0, a1=a1, cbk=cbk: nc.vector.scalar_tensor_tensor(out=hc[:, a0:a1], in0=pre[:, 2 + a0:2 + a1], scalar=cwt[:, cbk * 3 + 2:cbk * 3 + 3], in1=acc[:, a0:a1], op0=ALU.mult, op1=ALU.add),
                               reads=[pre, cwt, acc], writes=[hc])
                        for j8 in range(8):
                            pb = psb[j8 % 2]
                            for jj in range(8):
                                j = j8 * 8 + jj
                                op("pe", lambda j=j, jj=jj, pb=pb: nc.tensor.transpose(pb[:, jj * 128:(jj + 1) * 128], hc[:, j * 128:(j + 1) * 128], ident_bf[:]),
                                   reads=[hc, ident_bf], writes=[pb], inc=(jj == 7))
                            op("act", lambda j8=j8, pb=pb, dst=dst: nc.scalar.copy(out=dst[:, j8 * 8:(j8 + 1) * 8, :], in_=V(pb, 0, [(128, 8), (1, 128)])), reads=[pb], writes=[dst])
                    psr_ = [psum(s2, [128, 512], F32) for _ in range(2)]
                    for j4 in range(16):
                        pr_ = psr_[j4 % 2]
                        op("pe", lambda j4=j4, pr_=pr_: nc.tensor.matmul(pr_[:], jrev[:], V(X1t, j4 * 512, [(1, 512)]), start=True, stop=True), reads=[jrev, X1t], writes=[pr_])
                        op("dve", lambda j4=j4, pr_=pr_: nc.vector.tensor_copy(out=V(X1r, j4 * 512, [(1, 512)]), in_=pr_[:]), reads=[pr_], writes=[X1r])
                    kb.barrier()
                with contextlib.ExitStack() as s2:
                    Tt = [sbuf(s2, [128, 127 * 128], BF16) for _ in range(2)]
                    psY = [psum(s2, [128, 512], F32) for _ in range(2)]
                    psb = [psum(s2, [128, 1024], BF16) for _ in range(2)]
                    psq = [psum(s2, [128, 512], BF16) for _ in range(2)]
                    yst = [sbuf(s2, [128, 1024], BF16) for _ in range(2)]
                    ysb = [sbuf(s2, [64, 512], BF16) for _ in range(2)]
                    Upc = [sbuf(s2, [128, 190], BF16) for _ in range(2)]
                    for u_ in Upc:
                        op("dve", lambda u_=u_: nc.vector.memset(u_[:], 0.0), writes=[u_])
                    for o in range(2):
                        Uin = Ut if o == 0 else Zt
                        Xm = X1r if o == 0 else X2t
                        Zo = Zt if o == 0 else Ut
                        for c in range(128):
                            tt_ = Tt[c % 2]
                            up = Upc[c % 2]
                            row = o * 256 + g * 128 + c
                            for hq in range(2):
                                q0 = hq * 8128
                                dma("sp" if hq == 0 else "act", tt_[:, q0:q0 + 8128],
                                    bass.AP(tensor=k2, offset=row * 2 * L + o + q0, ap=[[1, 128], [1, 8128]]), reads=[r_k2], writes=[tt_])
                            op("dve", lambda up=up, Uin=Uin, c=c: nc.vector.tensor_copy(out=up[:, 63:127], in_=V(Uin, c, [(128, 64)])), reads=[Uin], writes=[up])
                            py = psY[(c // 4) % 2]
                            sl = (c % 4) * 128
                            order = [0] + [d for k_ in range(1, 64) for d in (k_, -k_)]
                            for n_, d in enumerate(order):
                                qd = (63 - d) * 128 if o == 0 else (d + 63) * 128
                                op("pe", lambda d=d, tt_=tt_, py=py, sl=sl, n_=n_, qd=qd, up=up: nc.tensor.matmul(
                                    py[0:64, sl:sl + 128], up[:, 63 - d:127 - d], tt_[:, qd:qd + 128], start=(n_ == 0), stop=(n_ == 126)),
                                   reads=[tt_, up], writes=[py], inc=(n_ == 126))
                            if c % 4 == 3:
                                c0 = c - 3
                                yb_ = ysb[(c // 4) % 2]
                                pq = psq[(c // 4) % 2]
                                op("act", lambda py=py, yb_=yb_: nc.scalar.copy(out=yb_[:], in_=py[0:64, :]), reads=[py], writes=[yb_])
                                for k4 in range(4):
                                    op("pe", lambda k4=k4, yb_=yb_, pq=pq: nc.tensor.transpose(pq[:, k4 * 64:(k4 + 1) * 64], yb_[:, k4 * 128:(k4 + 1) * 128], ident_bf[0:64, 0:64]),
                                       reads=[yb_, ident_bf], writes=[pq], inc=(k4 == 3))
                                op("dve", lambda pq=pq, c0=c0, Xm=Xm, Zo=Zo: nc.vector.tensor_tensor(out=V(Zo, c0, [(1, 4), (128, 64)]), in0=V(pq, 0, [(64, 4), (1, 64)]),
                                                                                             in1=V(Xm, c0, [(1, 4), (128, 64)]), op=ALU.mult),
                                   reads=[pq, Xm], writes=[Zo])
                    for j8 in range(8):
                        pb = psb[j8 % 2]
                        ys = yst[j8 % 2]
                        for jj in range(8):
                            j = j8 * 8 + jj
                            op("pe", lambda j=j, jj=jj, pb=pb: nc.tensor.transpose(pb[:, jj * 128:(jj + 1) * 128], Ut[:, j, :], ident_bf[:]),
                               reads=[Ut, ident_bf], writes=[pb], inc=(jj == 7))
                        op("act", lambda pb=pb, ys=ys: nc.scalar.copy(out=ys[:], in_=pb[:]), reads=[pb], writes=[ys])
                        dma("sp", ag_in_hy[j8][g * 128:(g + 1) * 128, :], ys[:], reads=[ys], writes=[r_agin])
                    kb.barrier()
            kb.barrier()

        if upto >= 4:
            for i in range(8):
                op("pool", lambda i=i: nc.gpsimd.collective_compute("AllGather", ALU.bypass, replica_groups=[[0, 1, 2, 3], [4, 5, 6, 7]],
                                                                    ins=[ag_in_hy[i].ap().opt()], outs=[ag_out_hy[i].ap().opt()]), reads=[r_agin], writes=[r_agout])

        for st in _phase(upto >= 3):
            QZ = [[sbuf(st, [128, L], BF16) for _ in range(2)] for _ in range(2)]
            KT = [sbuf(st, [128, L], BF16) for _ in range(2)]
            Va = sbuf(st, [128, NB, 2, 129], BF16)
            for h in range(2):
                for comp in range(2):
                    oc = 1 - comp
                    op("dve", lambda h=h, comp=comp, oc=oc: nc.vector.memset(QZ[h][comp][oc * 64:(oc + 1) * 64, :], 0.0), writes=[QZ[h][comp]])
                    dma("sp", QZ[h][comp][comp * 64:(comp + 1) * 64, :], qkT_d[h * 128 + comp * 64:h * 128 + (comp + 1) * 64, :], reads=[r_qkT], writes=[QZ[h][comp]])
                dma("act", KT[h][:], qkT_d[(2 + h) * 128:(3 + h) * 128, :], reads=[r_qkT], writes=[KT[h]])
            op("dve", lambda: nc.vector.memset(Va[:], 1.0), writes=[Va])
            vv = vtok.ap().rearrange("(r p) c -> p r c", p=128)
            for h in range(2):
                dma("sp", Va[:, :, h, 0:128], vv[:, :, h * 128:(h + 1) * 128], reads=[r_vtok], writes=[Va])
            Pt = [sbuf(st, [128, 1024], BF16) for _ in range(3)]
            O0 = [sbuf(st, [128, 128], F32) for _ in range(4)]
            Dd = [sbuf(st, [128, 128], F32) for _ in range(4)]
            rz = sbuf(st, [128, 8], F32)
            sj = sbuf(st, [128, 128], F32)
            sq1 = sbuf(st, [128, 8], F32)
            yab = [sbuf(st, [128, 128], BF16) for _ in range(2)]
            yaT = [sbuf(st, [128, 512], BF16) for _ in range(2)]
            psS = [psum(st, [128, 1024], F32) for _ in range(2)]
            psOb = [psum(st, [128, 512], F32) for _ in range(2)]
            psA = psum(st, [128, 1024], BF16)
            it = 0
            for h in range(2):
                for qb in range(16):
                    for comp in range(2):
                        p0 = comp * 64

                        def acc(s4):
                            return psOb[s4 // 2], (s4 % 2) * 256

                        def qk2(p, h=h, qb=qb, comp=comp):
                            ps = psS[p % 2]
                            for hf in range(2):
                                kbk = 2 * p + hf
                                op("pe", lambda: nc.tensor.matmul(ps[:, hf * 512:(hf + 1) * 512], KT[h][:, kbk * 128:(kbk + 1) * 128], QZ[h][comp][:, qb * 512:(qb + 1) * 512], start=True, stop=True),
                                   reads=[KT[h], QZ[h][comp]], writes=[ps], inc=(hf == 1))

                        qk2(0)
                        for p in range(NB // 2):
                            if p + 1 < NB // 2:
                                qk2(p + 1)
                            ps = psS[p % 2]
                            pt = Pt[p % 3]
                            op("act", lambda: nc.scalar.activation(out=pt[:], in_=ps[:], func=AF.Exp, bias=shiftb[:]), reads=[ps, shiftb], writes=[pt])
                            for hf in range(2):
                                kbk = 2 * p + hf
                                for s4 in range(4):
                                    pb_, c0_ = acc(s4)
                                    op("pe", lambda: nc.tensor.matmul(pb_[:, c0_:c0_ + 129], pt[:, hf * 512 + s4 * 128:hf * 512 + (s4 + 1) * 128], Va[:, kbk, h, :],
                                                                      start=(kbk == 0 and s4 % 2 == 0), stop=(kbk == NB - 1), skip_group_check=True),
                                       reads=[pt, Va], writes=[pb_], inc=(hf == 1 and s4 == 3))
                        for s4 in range(4):
                            rc = rz[:, s4:s4 + 1]
                            pb_, c0_ = acc(s4)
                            op("dve", lambda: nc.vector.reciprocal(out=rc, in_=pb_[:, c0_ + 128:c0_ + 129]), reads=[pb_], writes=[rz])
                            if comp == 0:
                                op("dve", lambda: nc.vector.tensor_scalar(O0[s4][:], pb_[:, c0_:c0_ + 128], rc, None, ALU.mult), reads=[pb_, rz], writes=[O0[s4]])
                            else:
                                op("dve", lambda: nc.vector.tensor_tensor(out=rc, in0=rc, in1=nlam[:], op=ALU.mult), reads=[rz, nlam], writes=[rz])
                                op("dve", lambda: nc.vector.scalar_tensor_tensor(out=Dd[s4][:], in0=pb_[:, c0_:c0_ + 128], scalar=rc, in1=O0[s4][:], op0=ALU.mult, op1=ALU.add),
                                   reads=[pb_, rz, O0[s4]], writes=[Dd[s4]])
                    yt = yaT[it % 2]
                    it += 1
                    for s4 in range(4):
                        yb = yab[s4 % 2]
                        op("act", lambda s4=s4: nc.scalar.activation(out=sj[:], in_=Dd[s4][:], func=AF.Square, accum_out=sq1[:, s4:s4 + 1]), reads=[Dd[s4]], writes=[sj, sq1])
                        op("act", lambda s4=s4: nc.scalar.activation(out=sq1[:, 4 + s4:5 + s4], in_=sq1[:, s4:s4 + 1], func=AF.Sqrt, scale=1.0 / 128, bias=epsb[:]), reads=[sq1, epsb], writes=[sq1])
                        op("dve", lambda s4=s4: nc.vector.reciprocal(out=sq1[:, 4 + s4:5 + s4], in_=sq1[:, 4 + s4:5 + s4]), reads=[sq1], writes=[sq1])
                        op("dve", lambda s4=s4, yb=yb: nc.vector.scalar_tensor_tensor(out=yb[:], in0=Dd[s4][:], scalar=sq1[:, 4 + s4:5 + s4], in1=subg[:], op0=ALU.mult, op1=ALU.mult),
                           reads=[Dd[s4], sq1, subg], writes=[yb])
                        op("pe", lambda s4=s4, yb=yb: nc.tensor.transpose(psA[:, s4 * 128:(s4 + 1) * 128], yb[:], ident_bf[:]), reads=[yb, ident_bf], writes=[psA])
                    op("act", lambda yt=yt: nc.scalar.copy(out=yt[:], in_=psA[:, 0:512]), reads=[psA], writes=[yt])
                    dma("sp", ag_in_at[qb // 2][h * 128:(h + 1) * 128, (qb % 2) * 512:(qb % 2 + 1) * 512], yt[:], reads=[yt], writes=[r_agin_at])
            kb.barrier()

        if upto >= 4:
            for i in range(8):
                op("pool", lambda i=i: nc.gpsimd.collective_compute("AllGather", ALU.bypass, replica_groups=[[0, 1, 2, 3], [4, 5, 6, 7]],
                                                                    ins=[ag_in_at[i].ap().opt()], outs=[ag_out_at[i].ap().opt()]), reads=[r_agin_at], writes=[r_agout_at])

        for st in _phase(upto >= 5):
            xs = [sbuf(st, [128, D], F32) for _ in range(4)]
            xns = [sbuf(st, [128, D], BF16) for _ in range(1)] * 2
            junk = sbuf(st, [128, D], BF16)
            hT = sbuf(st, [128, 16, 512], BF16)
            ssq = sbuf(st, [128, 32], F32)
            rt = sbuf(st, [128, 32], F32)
            rstd = sbuf(st, [128, 32], F32)
            wb = [sbuf(st, [128, 16, 256], BF16) for _ in range(3)]
            rT = [sbuf(st, [128, 512], F32) for _ in range(2)]
            psTb = [psum(st, [128, 1024], BF16) for _ in range(2)]
            psM = [psum(st, [128, 512], F32) for _ in range(5)]
            psX = psum(st, [128, 512], F32)

            def back_to_tokens(pm, gcol, dblk, tix):
                rt_ = rT[tix % 2]
                op("act", lambda: nc.scalar.activation(out=rt_[:], in_=pm[:], func=AF.Identity, scale=modT[:, gcol + dblk:gcol + dblk + 1]), reads=[pm, modT], writes=[rt_])
                for s in range(4):
                    op("pe", lambda s=s: nc.tensor.transpose(psX[:, s * 128:(s + 1) * 128], rt_[:, s * 128:(s + 1) * 128], ident_f[:]), reads=[rt_, ident_f], writes=[psX], inc=(s == 3))
                for s in range(4):
                    op("dve", lambda s=s: nc.vector.tensor_tensor(out=xs[s][:, dblk * 128:(dblk + 1) * 128], in0=xs[s][:, dblk * 128:(dblk + 1) * 128], in1=psX[:, s * 128:(s + 1) * 128], op=ALU.add),
                       reads=[psX, xs[s]], writes=[xs[s]])

            wcnt = [0]

            def load_w(name, c0, ncols, kblocks=16):
                w_ = wb[wcnt[0] % 3]
                q_ = "sp" if wcnt[0] % 2 == 0 else "act"
                wcnt[0] += 1
                ch_ = c0 // 256
                dma(q_, w_[:, 0:kblocks, 0:ncols], wbf[name].ap()[ch_ * 128:(ch_ + 1) * 128, :].rearrange("p (i c) -> p i c", c=256), reads=[r_wbf[name]], writes=[w_])
                return w_

            tix = 0
            for tt in range(4):
                for s in range(4):
                    dma("sp", xs[s][:], xo[(tt * 4 + s) * 128:(tt * 4 + s + 1) * 128, :], writes=[xs[s]])
                    r = tt * 4 + s
                    norm_block(xs[s], (rstd, r), junk, xns[s % 2], psTb, hT, s * 128, G1, 0, (ssq, r), (rt, r))
                with contextlib.ExitStack() as s2:
                    ghT = sbuf(s2, [128, 32, 512], BF16)
                    Yh = sbuf(s2, [128, 8, 512], BF16)
                    Ya = sbuf(s2, [128, 8, 512], BF16)
                    yld = [sbuf(s2, [128, 8, 512], BF16)] * 2
                    mT = sbuf(s2, [128, 16, 512], BF16)
                    t1 = sbuf(s2, [128, 512], F32)
                    t2 = sbuf(s2, [128, 512], F32)
                    for n8 in range(16):
                        w_ = load_w("w_g", n8 * 256, 256)
                        for fb in range(2):
                            pm = psM[(n8 * 2 + fb) % 3]
                            for i in range(16):
                                op("pe", lambda i=i, fb=fb, pm=pm, w_=w_: nc.tensor.matmul(pm[:], w_[:, i, fb * 128:(fb + 1) * 128], hT[:, i, :], start=(i == 0), stop=(i == 15)),
                                   reads=[w_, hT, hT.r2], writes=[pm], inc=(i == 15))
                            op("act", lambda pm=pm, n8=n8, fb=fb: nc.scalar.activation(out=ghT[:, n8 * 2 + fb, :], in_=pm[:], func=AF.Sigmoid), reads=[pm], writes=[ghT])
                    for which, dst in ((0, Yh), (1, Ya)):
                        for q in range(4):
                            yl = yld[q % 2]
                            for rr in range(4):
                                src = (ag_out_hy if which == 0 else ag_out_at)[q * 2 + tt // 2].ap()[rr * 256:rr * 256 + 256, (tt % 2) * 512:(tt % 2 + 1) * 512]
                                dma("sp" if rr % 2 == 0 else "act", yl[:, rr * 2:rr * 2 + 2, :], src.rearrange("(k p) n -> p k n", p=128),
                                    reads=[r_agout if which == 0 else r_agout_at], writes=[yl])
                            if q == 0:
                                op("dve", lambda yl=yl, dst=dst, q=q: nc.vector.tensor_scalar(dst[:], yl[:], selt[:, q:q + 1], None, ALU.mult), reads=[yl, selt], writes=[dst])
                            else:
                                op("dve", lambda yl=yl, dst=dst, q=q: nc.vector.scalar_tensor_tensor(out=dst[:], in0=yl[:], scalar=selt[:, q:q + 1], in1=dst[:], op0=ALU.mult, op1=ALU.add),
                                   reads=[yl, selt, dst], writes=[dst])
                    for n4 in range(8):
                        wh = load_w("w_ph", n4 * 256, 256, 8)
                        wa_ = load_w("w_pa", n4 * 256, 256, 8)
                        for fb in range(2):
                            dblk = n4 * 2 + fb
                            pmh = psM[(2 * dblk) % 4]
                            pma = psM[(2 * dblk + 1) % 4]
                            for k_ in range(8):
                                op("pe", lambda k_=k_, fb=fb: nc.tensor.matmul(pmh[:], wh[:, k_, fb * 128:(fb + 1) * 128], Yh[:, k_, :], start=(k_ == 0), stop=(k_ == 7)), reads=[wh, Yh], writes=[pmh], inc=(k_ == 7))
                            for k_ in range(8):
                                op("pe", lambda k_=k_, fb=fb: nc.tensor.matmul(pma[:], wa_[:, k_, fb * 128:(fb + 1) * 128], Ya[:, k_, :], start=(k_ == 0), stop=(k_ == 7)), reads=[wa_, Ya], writes=[pma], inc=(k_ == 7))
                            op("dve", lambda dblk=dblk: nc.vector.tensor_tensor(out=t1[:], in0=pmh[:], in1=ghT[:, dblk, :], op=ALU.mult), reads=[pmh, ghT], writes=[t1])
                            op("dve", lambda dblk=dblk: nc.vector.tensor_tensor(out=t2[:], in0=pma[:], in1=ghT[:, 16 + dblk, :], op=ALU.mult), reads=[pma, ghT], writes=[t2])
                            op("dve", lambda dblk=dblk: nc.vector.tensor_tensor(out=mT[:, dblk, :], in0=t1[:], in1=t2[:], op=ALU.add), reads=[t1, t2], writes=[mT])
                    for n4 in range(8):
                        w_ = load_w("w_o", n4 * 256, 256)
                        for fb in range(2):
                            dblk = n4 * 2 + fb
                            pm = psM[dblk % 3]
                            for i in range(16):
                                op("pe", lambda i=i, fb=fb, pm=pm, w_=w_: nc.tensor.matmul(pm[:], w_[:, i, fb * 128:(fb + 1) * 128], mT[:, i, :], start=(i == 0), stop=(i == 15)),
                                   reads=[w_, mT], writes=[pm], inc=(i == 15))
                            back_to_tokens(pm, 32, dblk, tix)
                            tix += 1
                    kb.barrier()
                for s in range(4):
                    r = 16 + tt * 4 + s
                    norm_block(xs[s], (rstd, r), junk, xns[s % 2], psTb, hT, s * 128, G2, 48, (ssq, r), (rt, r))
                with contextlib.ExitStack() as s2:
                    actT = sbuf(s2, [128, 44, 512], BF16)
                    sg = [sbuf(s2, [128, 512], F32) for _ in range(2)]
                    wd = [sbuf(s2, [128, 44, 128], BF16) for _ in range(2)]
                    for n11 in range(22):
                        wg_ = load_w("w_gate", n11 * 256, 256)
                        wu_ = load_w("w_up", n11 * 256, 256)
                        for fb in range(2):
                            f = n11 * 2 + fb
                            pg = psM[(2 * f) % 4]
                            pu = psM[(2 * f + 1) % 4]
                            for i in range(16):
                                op("pe", lambda i=i, fb=fb: nc.tensor.matmul(pg[:], wg_[:, i, fb * 128:(fb + 1) * 128], hT[:, i, :], start=(i == 0), stop=(i == 15)), reads=[wg_, hT, hT.r2], writes=[pg], inc=(i == 15))
                            for i in range(16):
                                op("pe", lambda i=i, fb=fb: nc.tensor.matmul(pu[:], wu_[:, i, fb * 128:(fb + 1) * 128], hT[:, i, :], start=(i == 0), stop=(i == 15)), reads=[wu_, hT, hT.r2], writes=[pu], inc=(i == 15))
                            sg_ = sg[f % 2]
                            op("act", lambda sg_=sg_: nc.scalar.activation(out=sg_[:], in_=pg[:], func=AF.Silu), reads=[pg], writes=[sg_])
                            op("dve", lambda sg_=sg_, f=f: nc.vector.tensor_tensor(out=actT[:, f, :], in0=pu[:], in1=sg_[:], op=ALU.mult), reads=[pu, sg_], writes=[actT])
                    for dblk in range(16):
                        wd_ = wd[dblk % 2]
                        dma("sp" if dblk % 2 == 0 else "act", wd_[:], wbf["w_down"].ap()[dblk * 128:(dblk + 1) * 128, :].rearrange("p (f c) -> p f c", c=128), reads=[r_wbf["w_down"]], writes=[wd_])
                        pm = psM[dblk % 3]
                        for f in range(44):
                            op("pe", lambda f=f, pm=pm, wd_=wd_: nc.tensor.matmul(pm[:], wd_[:, f, :], actT[:, f, :], start=(f == 0), stop=(f == 43)), reads=[wd_, actT], writes=[pm], inc=(f == 43))
                        back_to_tokens(pm, 80, dblk, tix)
                        tix += 1
                    kb.barrier()
                for s in range(4):
                    dma("sp", out[(tt * 4 + s) * 128:(tt * 4 + s + 1) * 128, :], xs[s][:], reads=[xs[s]], writes=[r_out])
            kb.barrier()
    return nc


_CACHE = {}
_UPTO = 99


def _bf(a):
    return np.ascontiguousarray(a).astype(ml_dtypes.bfloat16)


def kernel(**inputs):
    f32 = np.float32
    g = {k: np.asarray(v) for k, v in inputs.items()}
    x = g["x"].astype(f32)
    l = 0

    def pp(v):
        return np.ascontiguousarray(v.reshape(16, 128).T).astype(f32)

    s_hy, s_q, s_k, s_v = 3072, 4096, 5120, 6144
    w_in = g["w_in"][l]
    tt = np.linspace(0.0, 1.0, L, dtype=f32)
    idx = np.arange(2 * L)
    posn = np.where(idx >= L, idx - L, np.clip(L - idx, 0, L - 1))
    wv = (2.0 * math.pi / L) * posn.astype(f32)
    bands = np.linspace(1e-4, 15, 16, dtype=f32)
    zT2 = np.concatenate([tt[posn][None, :], np.cos(bands[:, None] * wv[None, :]), -np.sin(bands[:, None] * wv[None, :])], axis=0).astype(f32)
    zT2 = np.ascontiguousarray(np.concatenate([zT2, zT2[:, ::-1]], axis=1))
    tgrid = tt[posn][None, :].astype(f32)
    tgrid = np.ascontiguousarray(np.concatenate([tgrid, tgrid[:, ::-1]], axis=1))
    min_decay = math.log(1e-2) / 0.3
    max_decay = math.log(1e-2) / 1.5
    deltas = np.abs(np.linspace(min_decay, max_decay, 1024, dtype=f32))
    invf = (500000.0 ** (-np.arange(8, dtype=f32) * (2.0 / 16))).astype(f32)
    ident = np.eye(128, dtype=f32)

    shared = {
        "w_ada": np.ascontiguousarray(g["w_ada"][l]).astype(f32),
        "badaT": np.ascontiguousarray(g["b_ada"][l].reshape(96, 128).T).astype(f32),
        "n1g": pp(g["norm1_g"][l]), "n2g": pp(g["norm2_g"][l]),
        "w_g": np.ascontiguousarray(w_in[:, s_v:]).astype(f32),
        "fw1": g["hy_filt_w1"][l].astype(f32), "fw2": g["hy_filt_w2"][l].astype(f32), "fw3": g["hy_filt_w3"][l].astype(f32),
        "fbias": np.ascontiguousarray(np.stack([g["hy_filt_b1"][l], g["hy_filt_b2"][l], g["hy_filt_b3"][l]], axis=1)).astype(f32),
        "ffreq": np.ascontiguousarray(g["hy_filt_freq"][l].T).astype(f32),
        "gqk_bc": np.ascontiguousarray(np.broadcast_to(np.concatenate([np.tile(g["q_norm_g"][l], 4), np.tile(g["k_norm_g"][l], 4)])[None, :], (128, 512))).astype(f32),
        "lamv": np.concatenate([g["lam_q1"][l], g["lam_k1"][l], g["lam_q2"][l], g["lam_k2"][l]])[None, :].astype(f32),
        "subln_bc": np.ascontiguousarray(np.broadcast_to(g["subln_g"][l][None, :], (128, 128))).astype(f32),
        "w_ph": g["w_proj_hy"][l].astype(f32), "w_pa": g["w_proj_att"][l].astype(f32), "w_o": g["w_out"][l].astype(f32),
        "w_gate": g["w_gate"][l].astype(f32), "w_up": g["w_up"][l].astype(f32), "w_down": g["w_down"][l].astype(f32),
        "ident_bf": ident.astype(ml_dtypes.bfloat16), "ident_f": ident,
        "jrev": np.ascontiguousarray(ident[::-1]).astype(ml_dtypes.bfloat16),
        "zT2": zT2, "tgrid": tgrid,
        "invf_bc": np.ascontiguousarray(np.broadcast_to(invf[None, :], (128, 8))).astype(f32),
    }
    in_maps = []
    for core in range(8):
        b, j = core // 4, core % 4
        ch = slice(256 * j, 256 * j + 256)
        hy_cols = np.concatenate([np.arange(256 * j, 256 * j + 256) + o for o in (0, 1024, 2048)])
        qkv_cols = np.concatenate([np.arange(256 * j, 256 * j + 256) + o for o in (s_hy, s_q, s_k)])
        cw = g["hy_conv_w"][l][:, hy_cols]
        convw = np.ascontiguousarray(cw.reshape(3, 6, 128).transpose(2, 1, 0).reshape(128, 18)).astype(f32)
        convb = np.ascontiguousarray(g["hy_conv_b"][l][hy_cols].reshape(6, 128).T).astype(f32)
        fwo = g["hy_filt_w_out"][l].reshape(64, 2, 2, 1024)[:, :, :, ch].reshape(64, 1024)
        hyb = g["hy_bias"][l][:, ch].reshape(1, 512)
        selv = np.zeros((128, 4), f32)
        selv[:, j] = 1.0
        m = dict(shared)
        m.update({
            "xb": np.ascontiguousarray(x[b]), "xo": np.ascontiguousarray(x[b, 2048 * j:2048 * (j + 1)]),
            "cb": pp(g["c"][b]),
            "pos": np.ascontiguousarray(g["positions"][b].reshape(NB, 128).T).astype(np.int32),
            "w_hy": np.ascontiguousarray(w_in[:, hy_cols]).astype(f32),
            "w_qkv": np.ascontiguousarray(w_in[:, qkv_cols]).astype(f32),
            "convw": convw, "convb": convb,
            "fwout": np.ascontiguousarray(fwo).astype(f32),
            "ndelta": np.ascontiguousarray((-deltas[ch]).reshape(2, 128).T).astype(f32),
            "hyb_bc": np.ascontiguousarray(np.broadcast_to(hyb, (128, 512))).astype(f32),
            "hybT": np.ascontiguousarray(g["hy_bias"][l][:, ch].reshape(4, 128).T).astype(f32),
            "sel": selv,
        })
        in_maps.append(m)
    if "nc" not in _CACHE:
        _CACHE["nc"] = build(_UPTO)
    res = run_bass_kernel_spmd(_CACHE["nc"], in_maps, core_ids=list(range(8)))
    outp = np.zeros((2, L, D), f32)
    for core in range(8):
        b, j = core // 4, core % 4
        outp[b, 2048 * j:2048 * (j + 1)] = np.asarray(res.results[core]["out"]).astype(f32)
    return outp
```

```python
import math, contextlib
import numpy as np
import ml_dtypes
import concourse.bass as bass
import concourse.mybir as mybir
from concourse.bass_utils import run_bass_kernel_spmd

F32, BF16, I32 = mybir.dt.float32, mybir.dt.bfloat16, mybir.dt.int32
AF = mybir.ActivationFunctionType
ALU = mybir.AluOpType
AX = mybir.AxisListType

D = 2048
L = 8192
NB = 64
FF = 5632
EPS = 1e-6
LAM_INIT = 0.8 - 0.6 * math.exp(0.0)
TWO_PI = 2.0 * math.pi
PI_B = 3.1415925
NDS = 6
SHIFT = -8.0


def _phase(active):
    if active:
        with contextlib.ExitStack() as st:
            yield st


class Res:
    __slots__ = ("w", "rd")

    def __init__(self):
        self.w = None
        self.rd = {}


class Eng:
    def __init__(self, name, eng, sem):
        self.name, self.eng, self.sem = name, eng, sem
        self.count = 0
        self.waited = {}


class T:
    def __init__(self, t):
        self.t = t
        self.r = Res()
        self.r2 = Res()

    def __getitem__(self, k):
        return self.t[k]


class KB:
    def __init__(self, nc, st):
        self.nc = nc
        mk = lambda n: st.enter_context(nc.semaphore(n))
        self.E = {n: Eng(n, e, mk("s_" + n)) for n, e in
                  [("pe", nc.tensor), ("act", nc.scalar), ("dve", nc.vector), ("pool", nc.gpsimd), ("sp", nc.sync)]}
        self.dsem = {q: [mk(f"d_{q}{i}") for i in range(NDS)] for q in ("sp", "pool", "act")}
        self.dval = {q: [0] * NDS for q in ("sp", "pool", "act")}
        self.di = {q: 0 for q in ("sp", "pool", "act")}

    def _wait(self, E, tok):
        kind, a, v = tok
        if kind == "c":
            if a is E and E.name == "pe":
                return
            sem, key = a.sem, a.name
        else:
            sem, key = a
        if E.waited.get(key, 0) >= v:
            return
        E.eng.wait_ge(sem, v)
        E.waited[key] = v

    def _deps(self, E, reads, writes):
        for r in reads:
            if r.w is not None:
                self._wait(E, r.w)
        for w in writes:
            if w.w is not None:
                self._wait(E, w.w)
            for t in w.rd.values():
                self._wait(E, t)

    def _commit(self, tok, key, reads, writes):
        for r in reads:
            r.rd[key] = tok
        for w in writes:
            w.w = tok
            w.rd = {}

    def op(self, en, fn, reads=(), writes=(), inc=True):
        E = self.E[en]
        reads = [x.r if isinstance(x, T) else x for x in reads]
        writes = [x.r if isinstance(x, T) else x for x in writes]
        self._deps(E, reads, writes)
        inst = fn()
        if inc:
            E.count += 1
            inst.then_inc(E.sem, 1)
            self._commit(("c", E, E.count), E.name, reads, writes)
        else:
            self._commit(("c", E, E.count + 1), E.name, reads, writes)

    def dma(self, q, out, in_, reads=(), writes=(), **kw):
        E = self.E[q]
        reads = [x.r if isinstance(x, T) else x for x in reads]
        writes = [x.r if isinstance(x, T) else x for x in writes]
        self._deps(E, reads, writes)
        i = self.di[q] % NDS
        self.di[q] += 1
        sem = self.dsem[q][i]
        v = self.dval[q][i]
        key = f"{q}{i}"
        if v > 0:
            self._wait(E, ("d", (sem, key), v))
        inst = E.eng.dma_start(out=out, in_=in_, **kw)
        inst.then_inc(sem, 16)
        self.dval[q][i] = v + 16
        self._commit(("d", (sem, key), v + 16), key, reads, writes)

    def barrier(self):
        for E in self.E.values():
            for Fe in self.E.values():
                if Fe is not E and Fe.count > 0:
                    sem, key, v = Fe.sem, Fe.name, Fe.count
                    if E.waited.get(key, 0) < v:
                        E.eng.wait_ge(sem, v)
                        E.waited[key] = v
            for q in self.dsem:
                for i in range(NDS):
                    v = self.dval[q][i]
                    if v > 0:
                        self._wait(E, ("d", (self.dsem[q][i], f"{q}{i}"), v))


def V(t, off, dims, p0=0, pn=None):
    tt = t.t if isinstance(t, T) else t
    base = tt[:]
    ps, pc = base.ap[0]
    if pn is None:
        pn = pc
    return bass.AP(tensor=base.tensor, offset=base.offset + p0 * ps + off, ap=[[ps, pn]] + [list(d) for d in dims])


def build(upto=99, dbg=None):
    nc = bass.Bass("TRN2", target_bir_lowering=False)
    dt_in = lambda n, s, d=F32: nc.dram_tensor(n, s, d, kind="ExternalInput").ap()
    xb = dt_in("xb", [L, D])
    xo = dt_in("xo", [2048, D])
    cb = dt_in("cb", [128, 16])
    posd = dt_in("pos", [128, NB], I32)
    w_ada = dt_in("w_ada", [D, 6 * D])
    badaT = dt_in("badaT", [128, 96])
    n1g = dt_in("n1g", [128, 16])
    n2g = dt_in("n2g", [128, 16])
    w_hy = dt_in("w_hy", [D, 768])
    w_qkv = dt_in("w_qkv", [D, 768])
    w_g = dt_in("w_g", [D, 4096])
    convw = dt_in("convw", [128, 18])
    convb = dt_in("convb", [128, 6])
    fw1 = dt_in("fw1", [33, 64])
    fw2 = dt_in("fw2", [64, 64])
    fw3 = dt_in("fw3", [64, 64])
    fbias = dt_in("fbias", [64, 3])
    ffreq = dt_in("ffreq", [64, 3])
    fwout = dt_in("fwout", [64, 1024])
    ndelta = dt_in("ndelta", [128, 2])
    hyb_bc = dt_in("hyb_bc", [128, 512])
    gqk_bc = dt_in("gqk_bc", [128, 512])
    lamv = dt_in("lamv", [1, 256])
    subln_bc = dt_in("subln_bc", [128, 128])
    w_ph = dt_in("w_ph", [1024, D])
    w_pa = dt_in("w_pa", [1024, D])
    w_o = dt_in("w_o", [D, D])
    w_gate = dt_in("w_gate", [D, FF])
    w_up = dt_in("w_up", [D, FF])
    w_down = dt_in("w_down", [FF, D])
    sel = dt_in("sel", [128, 4])
    ident_bf_d = dt_in("ident_bf", [128, 128], BF16)
    ident_f_d = dt_in("ident_f", [128, 128])
    zT2 = dt_in("zT2", [33, 4 * L])
    tgrid = dt_in("tgrid", [1, 4 * L])
    jrev_d = dt_in("jrev", [128, 128], BF16)
    hybT_d = dt_in("hybT", [128, 4])
    invf_bc = dt_in("invf_bc", [128, 8])
    out = nc.dram_tensor("out", [2048, D], F32, kind="ExternalOutput").ap()

    hy_pre = nc.dram_tensor("hy_pre", [768, L + 2], BF16)
    qkT_d = nc.dram_tensor("qkT_d", [512, L], BF16)
    vtok = nc.dram_tensor("vtok", [L, 256], BF16)
    k2 = nc.dram_tensor("k2", [512, 2 * L], BF16)
    ag_in_hy = [nc.dram_tensor(f"ag_in_hy{i}", [256, 1024], BF16) for i in range(8)]
    ag_in_at = [nc.dram_tensor(f"ag_in_at{i}", [256, 1024], BF16) for i in range(8)]
    ag_out_hy = [nc.dram_tensor(f"ag_out_hy{i}", [1024, 1024], BF16) for i in range(8)]
    ag_out_at = [nc.dram_tensor(f"ag_out_at{i}", [1024, 1024], BF16) for i in range(8)]
    r_agin_at, r_agout_at = Res(), Res()
    r_hy_pre, r_qkT, r_vtok, r_k2, r_agin, r_agout, r_out = [Res() for _ in range(7)]
    wsrc = {"w_g": w_g, "w_ph": w_ph, "w_pa": w_pa, "w_o": w_o, "w_gate": w_gate, "w_up": w_up, "w_down": w_down}
    wcw = {k: (128 if k == "w_down" else 256) for k in wsrc}
    wkb = {k: v.shape[0] // 128 for k, v in wsrc.items()}
    wbf = {k: nc.dram_tensor(k + "_bf", [(v.shape[1] // wcw[k]) * 128, wkb[k] * wcw[k]], BF16) for k, v in wsrc.items()}
    r_wbf = {k: Res() for k in wsrc}

    with contextlib.ExitStack() as top:
        kb = KB(nc, top)
        op, dma = kb.op, kb.dma
        _cnt = [0]

        def wrap(dst, src, shift, tmp, dres, sres, tres):
            op("dve", lambda: nc.vector.tensor_scalar(dst, src, float(shift), None, ALU.add), reads=[sres], writes=[dres])
            op("dve", lambda: nc.vector.tensor_scalar(tmp, dst, PI_B, -TWO_PI, ALU.is_gt, ALU.mult), reads=[dres], writes=[tres])
            op("dve", lambda: nc.vector.tensor_tensor(out=dst, in0=dst, in1=tmp, op=ALU.add), reads=[dres, tres], writes=[dres])
            op("dve", lambda: nc.vector.tensor_scalar(tmp, dst, -PI_B, TWO_PI, ALU.is_lt, ALU.mult), reads=[dres], writes=[tres])
            op("dve", lambda: nc.vector.tensor_tensor(out=dst, in0=dst, in1=tmp, op=ALU.add), reads=[dres, tres], writes=[dres])

        def sbuf(st, shape, dt, name=None):
            _cnt[0] += 1
            return T(st.enter_context(nc.sbuf_tensor(f"{name or 'sb'}_{_cnt[0]}", list(shape), dt)))

        def psum(st, shape, dt, name=None):
            _cnt[0] += 1
            return T(st.enter_context(nc.psum_tensor(f"{name or 'ps'}_{_cnt[0]}", list(shape), dt)))

        ident_bf = sbuf(top, [128, 128], BF16, "identb")
        ident_f = sbuf(top, [128, 128], F32, "identf")
        modT = sbuf(top, [128, 96], F32, "modT")
        G1 = sbuf(top, [128, 16], F32, "G1")
        G2 = sbuf(top, [128, 16], F32, "G2")
        cos_t = sbuf(top, [128, NB * 8], F32, "cos")
        sin_t = sbuf(top, [128, NB * 8], F32, "sin")
        nlam = sbuf(top, [128, 1], F32, "nlam")
        epsb = sbuf(top, [128, 1], F32, "epsb")
        shiftb = sbuf(top, [128, 1], F32, "shiftb")
        gqk = sbuf(top, [128, 512], F32, "gqk")
        subg = sbuf(top, [128, 128], F32, "subg")
        selt = sbuf(top, [128, 4], F32, "selt")
        dma("sp", ident_bf[:], ident_bf_d, writes=[ident_bf])
        dma("sp", ident_f[:], ident_f_d, writes=[ident_f])
        dma("sp", gqk[:], gqk_bc, writes=[gqk])
        dma("sp", subg[:], subln_bc, writes=[subg])
        dma("sp", selt[:], sel, writes=[selt])
        op("dve", lambda: nc.vector.memset(epsb[:], EPS), writes=[epsb])
        op("dve", lambda: nc.vector.memset(shiftb[:], SHIFT), writes=[shiftb])
        op("dve", lambda: nc.vector.tensor_scalar(gqk[:, 0:256], gqk[:, 0:256], 0.125, None, ALU.mult), reads=[gqk], writes=[gqk])
        op("dve", lambda: nc.vector.tensor_scalar(subg[:], subg[:], 1.0 - LAM_INIT, None, ALU.mult), reads=[subg], writes=[subg])

        def precast():
            for k_, src_ in wsrc.items():
                cw_ = wcw[k_]
                sv = src_.rearrange("(i p) n -> p i n", p=128)
                for ch_ in range(src_.shape[1] // cw_):
                    dv = wbf[k_].ap()[ch_ * 128:(ch_ + 1) * 128, :].rearrange("p (i c) -> p i c", c=cw_)
                    dma("pool", dv, sv[:, :, ch_ * cw_:(ch_ + 1) * cw_], writes=[r_wbf[k_]])

        with contextlib.ExitStack() as st:
            cS = sbuf(st, [128, 16], F32)
            wa = [sbuf(st, [128, 16, 512], F32) for _ in range(2)]
            modrow = sbuf(st, [1, 6 * D], F32)
            bT = sbuf(st, [128, 96], F32)
            one11 = sbuf(st, [1, 128], F32)
            n1t = sbuf(st, [128, 16], F32)
            n2t = sbuf(st, [128, 16], F32)
            psr = [psum(st, [128, 512], F32) for _ in range(2)]
            psT = psum(st, [128, 512], F32)
            dma("sp", cS[:], cb, writes=[cS])
            dma("sp", bT[:], badaT, writes=[bT])
            dma("sp", n1t[:], n1g, writes=[n1t])
            dma("sp", n2t[:], n2g, writes=[n2t])
            op("dve", lambda: nc.vector.memset(one11[:], 1.0), writes=[one11])
            op("act", lambda: nc.scalar.activation(out=cS[:], in_=cS[:], func=AF.Silu), reads=[cS], writes=[cS])
            wav = w_ada.rearrange("(i p) n -> p i n", p=128)
            for n in range(24):
                w_ = wa[n % 2]
                dma("sp" if n % 2 == 0 else "act", w_[:], wav[:, :, n * 512:(n + 1) * 512], writes=[w_])
                pr = psr[n % 2]
                for i in range(16):
                    op("pe", lambda i=i, w_=w_, pr=pr: nc.tensor.matmul(pr[0:1, :], cS[:, i:i + 1], w_[:, i, :], start=(i == 0), stop=(i == 15)),
                       reads=[cS, w_], writes=[pr], inc=(i == 15))
                op("dve", lambda n=n, pr=pr: nc.vector.tensor_copy(out=modrow[0:1, n * 512:(n + 1) * 512], in_=pr[0:1, :]), reads=[pr], writes=[modrow])
            for k in range(96):
                op("pe", lambda k=k: nc.tensor.matmul(psT[:, k:k + 1], modrow[0:1, k * 128:(k + 1) * 128], one11[0:1, 0:1], start=True, stop=True),
                   reads=[modrow, one11], writes=[psT], inc=(k == 95))
            op("dve", lambda: nc.vector.tensor_tensor(out=modT[:], in0=psT[:, 0:96], in1=bT[:], op=ALU.add), reads=[psT, bT], writes=[modT])
            op("dve", lambda: nc.vector.scalar_tensor_tensor(out=G1[:], in0=modT[:, 16:32], scalar=1.0, in1=n1t[:], op0=ALU.add, op1=ALU.mult), reads=[modT, n1t], writes=[G1])
            op("dve", lambda: nc.vector.scalar_tensor_tensor(out=G2[:], in0=modT[:, 64:80], scalar=1.0, in1=n2t[:], op0=ALU.add, op1=ALU.mult), reads=[modT, n2t], writes=[G2])
            posi = sbuf(st, [128, NB], I32)
            posf = sbuf(st, [128, NB], F32)
            invf = sbuf(st, [128, 8], F32)
            ang = sbuf(st, [128, NB * 8], F32)
            angi = sbuf(st, [128, NB * 8], I32)
            angf = sbuf(st, [128, NB * 8], F32)
            dma("sp", posi[:], posd, writes=[posi])
            dma("sp", invf[:], invf_bc, writes=[invf])
            op("dve", lambda: nc.vector.tensor_copy(out=posf[:], in_=posi[:]), reads=[posi], writes=[posf])
            op("dve", lambda: nc.vector.tensor_tensor(out=V(ang, 0, [(8, NB), (1, 8)]), in0=V(posf, 0, [(1, NB), (0, 8)]), in1=V(invf, 0, [(0, NB), (1, 8)]), op=ALU.mult),
               reads=[posf, invf], writes=[ang])
            op("dve", lambda: nc.vector.tensor_scalar(ang[:], ang[:], 1.0 / TWO_PI, None, ALU.mult), reads=[ang], writes=[ang])
            op("dve", lambda: nc.vector.tensor_copy(out=angi[:], in_=ang[:]), reads=[ang], writes=[angi])
            op("dve", lambda: nc.vector.tensor_copy(out=angf[:], in_=angi[:]), reads=[angi], writes=[angf])
            op("dve", lambda: nc.vector.tensor_tensor(out=ang[:], in0=ang[:], in1=angf[:], op=ALU.subtract), reads=[ang, angf], writes=[ang])
            op("dve", lambda: nc.vector.tensor_scalar(ang[:], ang[:], TWO_PI, None, ALU.mult), reads=[ang], writes=[ang])
            wtmp = sbuf(st, [128, NB * 8], F32)
            wrap(angf[:], ang[:], 0.0, wtmp[:], angf, ang, wtmp)
            op("act", lambda: nc.scalar.activation(out=sin_t[:], in_=angf[:], func=AF.Sin), reads=[angf], writes=[sin_t])
            wrap(angf[:], ang[:], math.pi / 2, wtmp[:], angf, ang, wtmp)
            op("act", lambda: nc.scalar.activation(out=cos_t[:], in_=angf[:], func=AF.Sin), reads=[angf], writes=[cos_t])
            lv = sbuf(st, [1, 256], F32)
            lp = sbuf(st, [1, 128], F32)
            ls = sbuf(st, [1, 2], F32)
            dma("sp", lv[:], lamv, writes=[lv])
            op("dve", lambda: nc.vector.tensor_tensor(out=V(lp, 0, [(64, 2), (1, 64)], 0, 1), in0=V(lv, 0, [(128, 2), (1, 64)], 0, 1), in1=V(lv, 64, [(128, 2), (1, 64)], 0, 1), op=ALU.mult),
               reads=[lv], writes=[lp])
            op("dve", lambda: nc.vector.tensor_reduce(out=ls[:], in_=V(lp, 0, [(64, 2), (1, 64)], 0, 1), axis=AX.X, op=ALU.add), reads=[lp], writes=[ls])
            op("act", lambda: nc.scalar.activation(out=ls[:], in_=ls[:], func=AF.Exp), reads=[ls], writes=[ls])
            op("dve", lambda: nc.vector.tensor_tensor(out=ls[0:1, 0:1], in0=ls[0:1, 1:2], in1=ls[0:1, 0:1], op=ALU.subtract), reads=[ls], writes=[ls])
            op("dve", lambda: nc.vector.tensor_scalar(ls[0:1, 0:1], ls[0:1, 0:1], -LAM_INIT, None, ALU.add), reads=[ls], writes=[ls])
            op("pe", lambda: nc.tensor.matmul(psT[:, 100:101], one11[0:1, :], ls[0:1, 0:1], start=True, stop=True), reads=[one11, ls, modT], writes=[psT])
            op("dve", lambda: nc.vector.tensor_copy(out=nlam[:], in_=psT[:, 100:101]), reads=[psT], writes=[nlam])
            kb.barrier()

        def S1(i):
            return modT[:, i:i + 1]

        def norm_block(xt, rstd_col, junk, xn, psTb, hT, col0, Gm, s_off, ssq_col, rt_col):
            op("act", lambda: nc.scalar.activation(out=junk[:], in_=xt[:], func=AF.Square, accum_out=ssq_col[0][:, ssq_col[1]:ssq_col[1] + 1]),
               reads=[xt], writes=[junk, ssq_col[0]])
            op("act", lambda: nc.scalar.activation(out=rt_col[0][:, rt_col[1]:rt_col[1] + 1], in_=ssq_col[0][:, ssq_col[1]:ssq_col[1] + 1], func=AF.Sqrt, scale=1.0 / D, bias=epsb[:]),
               reads=[ssq_col[0], epsb], writes=[rt_col[0]])
            op("dve", lambda: nc.vector.reciprocal(out=rstd_col[0][:, rstd_col[1]:rstd_col[1] + 1], in_=rt_col[0][:, rt_col[1]:rt_col[1] + 1]),
               reads=[rt_col[0]], writes=[rstd_col[0]])
            op("act", lambda: nc.scalar.activation(out=xn[:], in_=xt[:], func=AF.Identity, scale=rstd_col[0][:, rstd_col[1]:rstd_col[1] + 1]),
               reads=[xt, rstd_col[0]], writes=[xn])
            for half in range(2):
                ph_ = psTb[half]
                for i8 in range(8):
                    i = half * 8 + i8
                    op("pe", lambda i=i, i8=i8, ph_=ph_: nc.tensor.transpose(ph_[:, i8 * 128:(i8 + 1) * 128], xn[:, i * 128:(i + 1) * 128], ident_bf[:]),
                       reads=[xn, ident_bf], writes=[ph_], inc=(i8 == 7))
                for i8 in range(8):
                    i = half * 8 + i8
                    if half == 0:
                        op("act", lambda i=i, i8=i8, ph_=ph_: nc.scalar.activation(out=hT[:, i, col0:col0 + 128], in_=ph_[:, i8 * 128:(i8 + 1) * 128], func=AF.Identity,
                                                                                  scale=Gm[:, i:i + 1], bias=modT[:, s_off + i:s_off + i + 1]),
                           reads=[ph_, Gm, modT], writes=[hT.r])
                    else:
                        op("dve", lambda i=i, i8=i8, ph_=ph_: nc.vector.tensor_scalar(hT[:, i, col0:col0 + 128], ph_[:, i8 * 128:(i8 + 1) * 128], Gm[:, i:i + 1],
                                                                                     modT[:, s_off + i:s_off + i + 1], ALU.mult, ALU.add),
                           reads=[ph_, Gm, modT], writes=[hT.r2])

        for st in _phase(upto >= 1):
            whyb = sbuf(st, [128, 16, 768], BF16)
            wqkvb = sbuf(st, [128, 16, 768], BF16)
            dma("pool", whyb[:], w_hy.rearrange("(i p) n -> p i n", p=128), writes=[whyb])
            dma("pool", wqkvb[:], w_qkv.rearrange("(i p) n -> p i n", p=128), writes=[wqkvb])
            wob = sbuf(st, [64, 1024], BF16)
            dma("pool", wob[:], fwout, writes=[wob])
            precast()
            zt = sbuf(st, [128, 2], BF16)
            op("dve", lambda: nc.vector.memset(zt[:], 0.0), writes=[zt])
            for cbk in range(6):
                dma("sp", bass.AP(tensor=hy_pre, offset=cbk * 128 * (L + 2), ap=[[L + 2, 128], [L + 1, 2], [1, 1]]), V(zt, 0, [(1, 2), (1, 1)]), reads=[zt], writes=[r_hy_pre], allow_slow_non_contiguous=True)
            xts = [sbuf(st, [128, D], F32) for _ in range(3)]
            xns = [sbuf(st, [128, D], BF16) for _ in range(2)]
            junk = sbuf(st, [128, D], BF16)
            hTs = [sbuf(st, [128, 16, 512], BF16) for _ in range(2)]
            ssq = sbuf(st, [128, NB], F32)
            rt = sbuf(st, [128, NB], F32)
            rstd = sbuf(st, [128, NB], F32)
            hst = [sbuf(st, [128, 512], BF16) for _ in range(2)]
            sqj = sbuf(st, [128, 512], F32)
            ss8 = sbuf(st, [128, 8], F32)
            rt8 = sbuf(st, [128, 8], F32)
            rs8 = sbuf(st, [128, 8], F32)
            qn = sbuf(st, [128, 512], F32)
            rtmp = sbuf(st, [128, 4 * 64], F32)
            qbf = sbuf(st, [128, 512], BF16)
            qst = [sbuf(st, [128, 4, 512], BF16) for _ in range(2)]
            vst = [sbuf(st, [128, 4, 256], BF16) for _ in range(2)]
            psTb = [psum(st, [128, 1024], BF16) for _ in range(2)]
            psH = [psum(st, [128, 512], F32) for _ in range(2)]
            psQ0s = [psum(st, [128, 512], F32) for _ in range(2)]
            psQ1s = [psum(st, [128, 512], F32) for _ in range(2)]
            w1t = sbuf(st, [33, 64], F32)
            w2t = sbuf(st, [64, 64], F32)
            w3t = sbuf(st, [64, 64], F32)
            fbt = sbuf(st, [64, 3], F32)
            fft = sbuf(st, [64, 3], F32)
            fbs = sbuf(st, [64, 3], F32)
            ndl = sbuf(st, [128, 2], F32)
            hbT = sbuf(st, [128, 4], F32)
            fsets = [dict(zl=sbuf(st, [33, 512], F32), s3=sbuf(st, [64, 512], BF16), aa=[sbuf(st, [64, 512], F32) for _ in range(2)],
                          wtm=sbuf(st, [64, 512], F32), tg=sbuf(st, [128, 512], F32), dec=[sbuf(st, [128, 512], F32) for _ in range(2)],
                          k2t=[sbuf(st, [128, 512], BF16) for _ in range(2)]) for _ in range(2)]
            pmf = psH[1]
            dma("sp", w1t[:], fw1, writes=[w1t])
            dma("sp", w2t[:], fw2, writes=[w2t])
            dma("sp", w3t[:], fw3, writes=[w3t])
            dma("sp", fbt[:], fbias, writes=[fbt])
            dma("sp", fft[:], ffreq, writes=[fft])
            dma("sp", ndl[:], ndelta, writes=[ndl])
            dma("sp", hbT[:], hybT_d, writes=[hbT])
            op("dve", lambda: nc.vector.tensor_tensor(out=fbs[:], in0=fbt[:], in1=fft[:], op=ALU.mult), reads=[fbt, fft], writes=[fbs])

            def filt_gen(ti, fs):
                rev = ti >= 32
                tl = ti % 32
                o = 0 if rev else 1
                zl, s3, tgt, wtm = fs["zl"], fs["s3"], fs["tg"], fs["wtm"]
                dma("sp", zl[:], zT2[:, ti * 512:(ti + 1) * 512], writes=[zl])
                dma("sp", tgt[:], bass.AP(tensor=tgrid.tensor, offset=ti * 512, ap=[[0, 128], [1, 512]]), writes=[tgt])
                cur = None
                for l_, wl in enumerate([w1t, w2t, w3t]):
                    a_ = fs["aa"][l_ % 2]
                    if l_ == 0:
                        op("pe", lambda: nc.tensor.matmul(pmf[0:64, :], w1t[:], zl[:], start=True, stop=True), reads=[w1t, zl], writes=[pmf])
                    else:
                        op("pe", lambda: nc.tensor.matmul(pmf[0:64, :], wl[:], cur[:], start=True, stop=True), reads=[wl, cur], writes=[pmf])
                    op("act", lambda: nc.scalar.activation(out=a_[:], in_=pmf[0:64, :], func=AF.Identity, scale=fft[:, l_:l_ + 1], bias=fbs[:, l_:l_ + 1]),
                       reads=[pmf, fft, fbs], writes=[a_])
                    wrap(a_[:], a_[:], 0.0, wtm[:], a_, a_, wtm)
                    if l_ < 2:
                        op("act", lambda: nc.scalar.activation(out=a_[:], in_=a_[:], func=AF.Sin), reads=[a_], writes=[a_])
                        cur = a_
                    else:
                        op("act", lambda: nc.scalar.activation(out=s3[:], in_=a_[:], func=AF.Sin), reads=[a_], writes=[s3])
                    yield
                rdir = (1 if tl < 16 else 0) if not rev else (0 if tl < 16 else 1)
                for cbk in range(2):
                    dc = fs["dec"][cbk]
                    kt = fs["k2t"][cbk]
                    op("act", lambda: nc.scalar.activation(out=dc[:], in_=tgt[:], func=AF.Exp, scale=ndl[:, cbk:cbk + 1]), reads=[tgt, ndl], writes=[dc])
                    c0 = o * 512 + rdir * 256 + cbk * 128
                    op("pe", lambda: nc.tensor.matmul(pmf[:], wob[:, c0:c0 + 128], s3[:], start=True, stop=True), reads=[wob, s3], writes=[pmf])
                    op("dve", lambda: nc.vector.tensor_tensor(out=kt[:], in0=pmf[:], in1=dc[:], op=ALU.mult), reads=[pmf, dc], writes=[kt])
                    zc = (0 if tl == 0 else None) if not rev else (511 if tl == 31 else None)
                    bc_ = (0 if tl == 16 else None) if not rev else (511 if tl == 15 else None)
                    if zc is not None:
                        op("dve", lambda: nc.vector.memset(kt[:, zc:zc + 1], 0.0), writes=[kt])
                    if bc_ is not None:
                        op("dve", lambda: nc.vector.tensor_scalar(kt[:, bc_:bc_ + 1], kt[:, bc_:bc_ + 1], hbT[:, o * 2 + cbk:o * 2 + cbk + 1], None, ALU.add),
                           reads=[kt, hbT], writes=[kt])
                    row0 = o * 256 + cbk * 128
                    dma("sp", k2[row0:row0 + 128, tl * 512:(tl + 1) * 512], kt[:], reads=[kt], writes=[r_k2])
                    yield

            fq = {"next": 0, "gens": [None, None], "turn": 0}

            def filt_step(n=1):
                for _ in range(n):
                    k_ = fq["turn"]
                    fq["turn"] = 1 - k_
                    for _try in range(2):
                        if fq["gens"][k_] is None:
                            if fq["next"] >= 64:
                                break
                            fq["gens"][k_] = filt_gen(fq["next"], fsets[k_])
                            fq["next"] += 1
                        try:
                            next(fq["gens"][k_])
                            break
                        except StopIteration:
                            fq["gens"][k_] = None

            def load_x(r):
                if r < NB:
                    dma("act", xts[r % 3][:], xb[r * 128:(r + 1) * 128, :], writes=[xts[r % 3]])

            load_x(0)
            load_x(1)

            def norm_tile_block(tt_, s_):
                r_ = tt_ * 4 + s_
                load_x(r_ + 2)
                norm_block(xts[r_ % 3], (rstd, r_), junk, xns[r_ % 2], psTb, hTs[tt_ % 2], s_ * 128, G1, 0, (ssq, r_), (rt, r_))

            for s in range(4):
                norm_tile_block(0, s)
            for tt in range(16):
                hT = hTs[tt % 2]
                for cbk in range(6):
                    ph = psH[0]
                    for i in range(16):
                        op("pe", lambda i=i, cbk=cbk, ph=ph: nc.tensor.matmul(ph[:], whyb[:, i, cbk * 128:(cbk + 1) * 128], hT[:, i, :], start=(i == 0), stop=(i == 15)),
                           reads=[whyb, hT, hT.r2], writes=[ph], inc=(i == 15))
                    hs = hst[cbk % 2]
                    if cbk % 2 == 0:
                        op("act", lambda ph=ph, hs=hs: nc.scalar.copy(out=hs[:], in_=ph[:]), reads=[ph], writes=[hs])
                    else:
                        op("dve", lambda ph=ph, hs=hs: nc.vector.tensor_copy(out=hs[:], in_=ph[:]), reads=[ph], writes=[hs])
                    dma("sp", hy_pre[cbk * 128:(cbk + 1) * 128, 1 + tt * 512:1 + (tt + 1) * 512], hs[:], reads=[hs], writes=[r_hy_pre])
                    filt_step(2)
                qs = qst[tt % 2]
                vs = vst[tt % 2]
                def qkv_mm(s):
                    r = tt * 4 + s
                    psQ0 = psQ0s[s % 2]
                    psQ1 = psQ1s[s % 2]
                    for i in range(16):
                        op("pe", lambda i=i, s=s, psQ0=psQ0: nc.tensor.matmul(psQ0[:], hT[:, i, s * 128:(s + 1) * 128], wqkvb[:, i, 0:512], start=(i == 0), stop=(i == 15)),
                           reads=[hT, hT.r2, wqkvb], writes=[psQ0], inc=(i == 15))
                    for i in range(16):
                        op("pe", lambda i=i, s=s, psQ1=psQ1: nc.tensor.matmul(psQ1[:, 0:256], hT[:, i, s * 128:(s + 1) * 128], wqkvb[:, i, 512:768], start=(i == 0), stop=(i == 15)),
                           reads=[hT, hT.r2, wqkvb], writes=[psQ1], inc=(i == 15))
                    op("act", lambda s=s, psQ1=psQ1: nc.scalar.copy(out=vs[:, s, :], in_=psQ1[:, 0:256]), reads=[psQ1], writes=[vs])
                    filt_step(2)

                def qkv_post(s):
                    r = tt * 4 + s
                    psQ0 = psQ0s[s % 2]
                    op("act", lambda psQ0=psQ0: nc.scalar.activation(out=sqj[:], in_=psQ0[:], func=AF.Square), reads=[psQ0], writes=[sqj])
                    op("dve", lambda: nc.vector.tensor_reduce(out=ss8[:], in_=V(sqj, 0, [(64, 8), (1, 64)]), axis=AX.X, op=ALU.add), reads=[sqj], writes=[ss8])
                    op("act", lambda: nc.scalar.activation(out=rt8[:], in_=ss8[:], func=AF.Sqrt, scale=1.0 / 64, bias=epsb[:]), reads=[ss8, epsb], writes=[rt8])
                    op("dve", lambda: nc.vector.reciprocal(out=rs8[:], in_=rt8[:]), reads=[rt8], writes=[rs8])
                    op("dve", lambda psQ0=psQ0: nc.vector.tensor_tensor(out=V(qn, 0, [(64, 8), (1, 64)]), in0=V(psQ0, 0, [(64, 8), (1, 64)]), in1=V(rs8, 0, [(1, 8), (0, 64)]), op=ALU.mult),
                       reads=[psQ0, rs8], writes=[qn])
                    op("dve", lambda: nc.vector.tensor_tensor(out=qn[:], in0=qn[:], in1=gqk[:], op=ALU.mult), reads=[qn, gqk], writes=[qn])
                    x1 = V(qn, 0, [(64, 8), (1, 8)])
                    x2 = V(qn, 8, [(64, 8), (1, 8)])
                    cs = V(cos_t, r * 8, [(0, 8), (1, 8)])
                    sn = V(sin_t, r * 8, [(0, 8), (1, 8)])
                    tv = lambda k: V(rtmp, k * 64, [(8, 8), (1, 8)])
                    op("dve", lambda: nc.vector.tensor_tensor(out=tv(0), in0=x1, in1=cs, op=ALU.mult), reads=[qn, cos_t], writes=[rtmp])
                    op("dve", lambda: nc.vector.tensor_tensor(out=tv(1), in0=x2, in1=sn, op=ALU.mult), reads=[qn, sin_t], writes=[rtmp])
                    op("dve", lambda: nc.vector.tensor_tensor(out=tv(2), in0=x2, in1=cs, op=ALU.mult), reads=[qn, cos_t], writes=[rtmp])
                    op("dve", lambda: nc.vector.tensor_tensor(out=tv(3), in0=x1, in1=sn, op=ALU.mult), reads=[qn, sin_t], writes=[rtmp])
                    op("dve", lambda: nc.vector.tensor_tensor(out=x1, in0=tv(0), in1=tv(1), op=ALU.subtract), reads=[rtmp], writes=[qn])
                    op("dve", lambda: nc.vector.tensor_tensor(out=x2, in0=tv(2), in1=tv(3), op=ALU.add), reads=[rtmp], writes=[qn])
                    op("dve", lambda: nc.vector.tensor_copy(out=qbf[:], in_=qn[:]), reads=[qn], writes=[qbf])
                    pt = psTb[r % 2]
                    for blk in range(4):
                        op("pe", lambda blk=blk, pt=pt: nc.tensor.transpose(pt[:, blk * 128:(blk + 1) * 128], qbf[:, blk * 128:(blk + 1) * 128], ident_bf[:]),
                           reads=[qbf, ident_bf], writes=[pt], inc=(blk == 3))
                    op("dve", lambda s=s, pt=pt: nc.vector.tensor_copy(out=qs[:, :, s * 128:(s + 1) * 128], in_=V(pt, 0, [(128, 4), (1, 128)])), reads=[pt], writes=[qs])

                qkv_mm(0)
                for s in range(4):
                    if s + 1 < 4:
                        qkv_mm(s + 1)
                    qkv_post(s)
                    if tt + 1 < 16:
                        norm_tile_block(tt + 1, s)
                dma("sp", qkT_d.ap().rearrange("(k p) n -> p k n", p=128)[:, :, tt * 512:(tt + 1) * 512], qs[:], reads=[qs], writes=[r_qkT])
                dma("sp", vtok.ap().rearrange("(r p) c -> p r c", p=128)[:, tt * 4:(tt + 1) * 4, :], vs[:], reads=[vs], writes=[r_vtok])
            kb.barrier()

        for st in _phase(upto >= 2):
            cwt = sbuf(st, [128, 18], F32)
            cbt = sbuf(st, [128, 6], F32)
            jrev = sbuf(st, [128, 128], BF16)
            dma("sp", cwt[:], convw, writes=[cwt])
            dma("sp", cbt[:], convb, writes=[cbt])
            dma("sp", jrev[:], jrev_d, writes=[jrev])
            Ut = sbuf(st, [128, NB, 128], BF16)
            X1t = sbuf(st, [128, NB, 128], BF16)
            X2t = sbuf(st, [128, NB, 128], BF16)
            Zt = sbuf(st, [128, NB, 128], BF16)
            X1r = sbuf(st, [128, NB, 128], BF16)
            for g in range(2):
                with contextlib.ExitStack() as s2:
                    pre = sbuf(s2, [128, L + 2], BF16)
                    acc = sbuf(s2, [128, L], F32)
                    hc = sbuf(s2, [128, L], BF16)
                    psb = [psum(s2, [128, 1024], BF16) for _ in range(2)]
                    for si, dst in enumerate([Ut, X1t, X2t]):
                        cbk = si * 2 + g
                        dma("sp", pre[:], hy_pre[cbk * 128:(cbk + 1) * 128, :], reads=[r_hy_pre], writes=[pre])
                        for hh in range(4):
                            a0, a1 = hh * 2048, (hh + 1) * 2048
                            op("dve", lambda a0=a0, a1=a1, cbk=cbk: nc.vector.tensor_scalar(acc[:, a0:a1], pre[:, 1 + a0:1 + a1], cwt[:, cbk * 3 + 1:cbk * 3 + 2], cbt[:, cbk:cbk + 1], ALU.mult, ALU.add),
                               reads=[pre, cwt, cbt], writes=[acc])
                            op("dve", lambda a0=a0, a1=a1, cbk=cbk: nc.vector.scalar_tensor_tensor(out=acc[:, a0:a1], in0=pre[:, a0:a1], scalar=cwt[:, cbk * 3:cbk * 3 + 1], in1=acc[:, a0:a1], op0=ALU.mult, op1=ALU.add),
                               reads=[pre, cwt, acc], writes=[acc])
                            op("dve", lambda a0=a0, a1=a1, cbk=cbk: nc.vector.scalar_tensor_tensor(out=hc[:, a0:a1], in0=pre[:, 2 + a0:2 + a1], scalar=cwt[:, cbk * 3 + 2:cbk * 3 + 3], in1=acc[:, a0:a1], op0=ALU.mult, op1=ALU.add),
                               reads=[pre, cwt, acc], writes=[hc])
                        for j8 in range(8):
                            pb = psb[j8 % 2]
                            for jj in range(8):
                                j = j8 * 8 + jj
                                op("pe", lambda j=j, jj=jj, pb=pb: nc.tensor.transpose(pb[:, jj * 128:(jj + 1) * 128], hc[:, j * 128:(j + 1) * 128], ident_bf[:]),
                                   reads=[hc, ident_bf], writes=[pb], inc=(jj == 7))
                            op("act", lambda j8=j8, pb=pb, dst=dst: nc.scalar.copy(out=dst[:, j8 * 8:(j8 + 1) * 8, :], in_=V(pb, 0, [(128, 8), (1, 128)])), reads=[pb], writes=[dst])
                    psr_ = [psum(s2, [128, 512], F32) for _ in range(2)]
                    for j4 in range(16):
                        pr_ = psr_[j4 % 2]
                        op("pe", lambda j4=j4, pr_=pr_: nc.tensor.matmul(pr_[:], jrev[:], V(X1t, j4 * 512, [(1, 512)]), start=True, stop=True), reads=[jrev, X1t], writes=[pr_])
                        op("dve", lambda j4=j4, pr_=pr_: nc.vector.tensor_copy(out=V(X1r, j4 * 512, [(1, 512)]), in_=pr_[:]), reads=[pr_], writes=[X1r])
                    kb.barrier()
                with contextlib.ExitStack() as s2:
                    Tt = [sbuf(s2, [128, 127 * 128], BF16) for _ in range(2)]
                    psY = [psum(s2, [128, 512], F32) for _ in range(2)]
                    psb = [psum(s2, [128, 1024], BF16) for _ in range(2)]
                    psq = [psum(s2, [128, 512], BF16) for _ in range(2)]
                    yst = [sbuf(s2, [128, 1024], BF16) for _ in range(2)]
                    ysb = [sbuf(s2, [64, 512], BF16) for _ in range(2)]
                    Upc = [sbuf(s2, [128, 190], BF16) for _ in range(2)]
                    for u_ in Upc:
                        op("dve", lambda u_=u_: nc.vector.memset(u_[:], 0.0), writes=[u_])
                    for o in range(2):
                        Uin = Ut if o == 0 else Zt
                        Xm = X1r if o == 0 else X2t
                        Zo = Zt if o == 0 else Ut
                        for c in range(128):
                            tt_ = Tt[c % 2]
                            up = Upc[c % 2]
                            row = o * 256 + g * 128 + c
                            for hq in range(2):
                                q0 = hq * 8128
                                dma("sp" if hq == 0 else "act", tt_[:, q0:q0 + 8128],
                                    bass.AP(tensor=k2, offset=row * 2 * L + o + q0, ap=[[1, 128], [1, 8128]]), reads=[r_k2], writes=[tt_])
                            op("dve", lambda up=up, Uin=Uin, c=c: nc.vector.tensor_copy(out=up[:, 63:127], in_=V(Uin, c, [(128, 64)])), reads=[Uin], writes=[up])
                            py = psY[(c // 4) % 2]
                            sl = (c % 4) * 128
                            order = [0] + [d for k_ in range(1, 64) for d in (k_, -k_)]
                            for n_, d in enumerate(order):
                                qd = (63 - d) * 128 if o == 0 else (d + 63) * 128
                                op("pe", lambda d=d, tt_=tt_, py=py, sl=sl, n_=n_, qd=qd, up=up: nc.tensor.matmul(
                                    py[0:64, sl:sl + 128], up[:, 63 - d:127 - d], tt_[:, qd:qd + 128], start=(n_ == 0), stop=(n_ == 126)),
                                   reads=[tt_, up], writes=[py], inc=(n_ == 126))
                            if c % 4 == 3:
                                c0 = c - 3
                                yb_ = ysb[(c // 4) % 2]
                                pq = psq[(c // 4) % 2]
                                op("act", lambda py=py, yb_=yb_: nc.scalar.copy(out=yb_[:], in_=py[0:64, :]), reads=[py], writes=[yb_])
                                for k4 in range(4):
                                    op("pe", lambda k4=k4, yb_=yb_, pq=pq: nc.tensor.transpose(pq[:, k4 * 64:(k4 + 1) * 64], yb_[:, k4 * 128:(k4 + 1) * 128], ident_bf[0:64, 0:64]),
                                       reads=[yb_, ident_bf], writes=[pq], inc=(k4 == 3))
                                op("dve", lambda pq=pq, c0=c0, Xm=Xm, Zo=Zo: nc.vector.tensor_tensor(out=V(Zo, c0, [(1, 4), (128, 64)]), in0=V(pq, 0, [(64, 4), (1, 64)]),
                                                                                             in1=V(Xm, c0, [(1, 4), (128, 64)]), op=ALU.mult),
                                   reads=[pq, Xm], writes=[Zo])
                    for j8 in range(8):
                        pb = psb[j8 % 2]
                        ys = yst[j8 % 2]
                        for jj in range(8):
                            j = j8 * 8 + jj
                            op("pe", lambda j=j, jj=jj, pb=pb: nc.tensor.transpose(pb[:, jj * 128:(jj + 1) * 128], Ut[:, j, :], ident_bf[:]),
                               reads=[Ut, ident_bf], writes=[pb], inc=(jj == 7))
                        op("act", lambda pb=pb, ys=ys: nc.scalar.copy(out=ys[:], in_=pb[:]), reads=[pb], writes=[ys])
                        dma("sp", ag_in_hy[j8][g * 128:(g + 1) * 128, :], ys[:], reads=[ys], writes=[r_agin])
                    kb.barrier()
            kb.barrier()

        if upto >= 4:
            for i in range(8):
                op("pool", lambda i=i: nc.gpsimd.collective_compute("AllGather", ALU.bypass, replica_groups=[[0, 1, 2, 3], [4, 5, 6, 7]],
                                                                    ins=[ag_in_hy[i].ap().opt()], outs=[ag_out_hy[i].ap().opt()]), reads=[r_agin], writes=[r_agout])

        for st in _phase(upto >= 3):
            QZ = [[sbuf(st, [128, L], BF16) for _ in range(2)] for _ in range(2)]
            KT = [sbuf(st, [128, L], BF16) for _ in range(2)]
            Va = sbuf(st, [128, NB, 2, 129], BF16)
            for h in range(2):
                for comp in range(2):
                    oc = 1 - comp
                    op("dve", lambda h=h, comp=comp, oc=oc: nc.vector.memset(QZ[h][comp][oc * 64:(oc + 1) * 64, :], 0.0), writes=[QZ[h][comp]])
                    dma("sp", QZ[h][comp][comp * 64:(comp + 1) * 64, :], qkT_d[h * 128 + comp * 64:h * 128 + (comp + 1) * 64, :], reads=[r_qkT], writes=[QZ[h][comp]])
                dma("act", KT[h][:], qkT_d[(2 + h) * 128:(3 + h) * 128, :], reads=[r_qkT], writes=[KT[h]])
            op("dve", lambda: nc.vector.memset(Va[:], 1.0), writes=[Va])
            vv = vtok.ap().rearrange("(r p) c -> p r c", p=128)
            for h in range(2):
                dma("sp", Va[:, :, h, 0:128], vv[:, :, h * 128:(h + 1) * 128], reads=[r_vtok], writes=[Va])
            Pt = [sbuf(st, [128, 1024], BF16) for _ in range(3)]
            O0 = [sbuf(st, [128, 128], F32) for _ in range(4)]
            Dd = [sbuf(st, [128, 128], F32) for _ in range(4)]
            rz = sbuf(st, [128, 8], F32)
            sj = sbuf(st, [128, 128], F32)
            sq1 = sbuf(st, [128, 8], F32)
            yab = [sbuf(st, [128, 128], BF16) for _ in range(2)]
            yaT = [sbuf(st, [128, 512], BF16) for _ in range(2)]
            psS = [psum(st, [128, 1024], F32) for _ in range(2)]
            psOb = [psum(st, [128, 512], F32) for _ in range(2)]
            psA = psum(st, [128, 1024], BF16)
            it = 0
            for h in range(2):
                for qb in range(16):
                    for comp in range(2):
                        p0 = comp * 64

                        def acc(s4):
                            return psOb[s4 // 2], (s4 % 2) * 256

                        def qk2(p, h=h, qb=qb, comp=comp):
                            ps = psS[p % 2]
                            for hf in range(2):
                                kbk = 2 * p + hf
                                op("pe", lambda: nc.tensor.matmul(ps[:, hf * 512:(hf + 1) * 512], KT[h][:, kbk * 128:(kbk + 1) * 128], QZ[h][comp][:, qb * 512:(qb + 1) * 512], start=True, stop=True),
                                   reads=[KT[h], QZ[h][comp]], writes=[ps], inc=(hf == 1))

                        qk2(0)
                        for p in range(NB // 2):
                            if p + 1 < NB // 2:
                                qk2(p + 1)
                            ps = psS[p % 2]
                            pt = Pt[p % 3]
                            op("act", lambda: nc.scalar.activation(out=pt[:], in_=ps[:], func=AF.Exp, bias=shiftb[:]), reads=[ps, shiftb], writes=[pt])
                            for hf in range(2):
                                kbk = 2 * p + hf
                                for s4 in range(4):
                                    pb_, c0_ = acc(s4)
                                    op("pe", lambda: nc.tensor.matmul(pb_[:, c0_:c0_ + 129], pt[:, hf * 512 + s4 * 128:hf * 512 + (s4 + 1) * 128], Va[:, kbk, h, :],
                                                                      start=(kbk == 0 and s4 % 2 == 0), stop=(kbk == NB - 1), skip_group_check=True),
                                       reads=[pt, Va], writes=[pb_], inc=(hf == 1 and s4 == 3))
                        for s4 in range(4):
                            rc = rz[:, s4:s4 + 1]
                            pb_, c0_ = acc(s4)
                            op("dve", lambda: nc.vector.reciprocal(out=rc, in_=pb_[:, c0_ + 128:c0_ + 129]), reads=[pb_], writes=[rz])
                            if comp == 0:
                                op("dve", lambda: nc.vector.tensor_scalar(O0[s4][:], pb_[:, c0_:c0_ + 128], rc, None, ALU.mult), reads=[pb_, rz], writes=[O0[s4]])
                            else:
                                op("dve", lambda: nc.vector.tensor_tensor(out=rc, in0=rc, in1=nlam[:], op=ALU.mult), reads=[rz, nlam], writes=[rz])
                                op("dve", lambda: nc.vector.scalar_tensor_tensor(out=Dd[s4][:], in0=pb_[:, c0_:c0_ + 128], scalar=rc, in1=O0[s4][:], op0=ALU.mult, op1=ALU.add),
                                   reads=[pb_, rz, O0[s4]], writes=[Dd[s4]])
                    yt = yaT[it % 2]
                    it += 1
                    for s4 in range(4):
                        yb = yab[s4 % 2]
                        op("act", lambda s4=s4: nc.scalar.activation(out=sj[:], in_=Dd[s4][:], func=AF.Square, accum_out=sq1[:, s4:s4 + 1]), reads=[Dd[s4]], writes=[sj, sq1])
                        op("act", lambda s4=s4: nc.scalar.activation(out=sq1[:, 4 + s4:5 + s4], in_=sq1[:, s4:s4 + 1], func=AF.Sqrt, scale=1.0 / 128, bias=epsb[:]), reads=[sq1, epsb], writes=[sq1])
                        op("dve", lambda s4=s4: nc.vector.reciprocal(out=sq1[:, 4 + s4:5 + s4], in_=sq1[:, 4 + s4:5 + s4]), reads=[sq1], writes=[sq1])
                        op("dve", lambda s4=s4, yb=yb: nc.vector.scalar_tensor_tensor(out=yb[:], in0=Dd[s4][:], scalar=sq1[:, 4 + s4:5 + s4], in1=subg[:], op0=ALU.mult, op1=ALU.mult),
                           reads=[Dd[s4], sq1, subg], writes=[yb])
                        op("pe", lambda s4=s4, yb=yb: nc.tensor.transpose(psA[:, s4 * 128:(s4 + 1) * 128], yb[:], ident_bf[:]), reads=[yb, ident_bf], writes=[psA])
                    op("act", lambda yt=yt: nc.scalar.copy(out=yt[:], in_=psA[:, 0:512]), reads=[psA], writes=[yt])
                    dma("sp", ag_in_at[qb // 2][h * 128:(h + 1) * 128, (qb % 2) * 512:(qb % 2 + 1) * 512], yt[:], reads=[yt], writes=[r_agin_at])
            kb.barrier()

        if upto >= 4:
            for i in range(8):
                op("pool", lambda i=i: nc.gpsimd.collective_compute("AllGather", ALU.bypass, replica_groups=[[0, 1, 2, 3], [4, 5, 6, 7]],
                                                                    ins=[ag_in_at[i].ap().opt()], outs=[ag_out_at[i].ap().opt()]), reads=[r_agin_at], writes=[r_agout_at])

        for st in _phase(upto >= 5):
            xs = [sbuf(st, [128, D], F32) for _ in range(4)]
            xns = [sbuf(st, [128, D], BF16) for _ in range(1)] * 2
            junk = sbuf(st, [128, D], BF16)
            hT = sbuf(st, [128, 16, 512], BF16)
            ssq = sbuf(st, [128, 32], F32)
            rt = sbuf(st, [128, 32], F32)
            rstd = sbuf(st, [128, 32], F32)
            wb = [sbuf(st, [128, 16, 256], BF16) for _ in range(3)]
            rT = [sbuf(st, [128, 512], F32) for _ in range(2)]
            psTb = [psum(st, [128, 1024], BF16) for _ in range(2)]
            psM = [psum(st, [128, 512], F32) for _ in range(5)]
            psX = psum(st, [128, 512], F32)

            b2t_pending = []

            def b2t_flush():
                while b2t_pending:
                    rt_, dblk = b2t_pending.pop(0)
                    for s in range(4):
                        op("pe", lambda s=s: nc.tensor.transpose(psX[:, s * 128:(s + 1) * 128], rt_[:, s * 128:(s + 1) * 128], ident_f[:]), reads=[rt_, ident_f], writes=[psX], inc=(s == 3))
                    for s in range(4):
                        op("dve", lambda s=s: nc.vector.tensor_tensor(out=xs[s][:, dblk * 128:(dblk + 1) * 128], in0=xs[s][:, dblk * 128:(dblk + 1) * 128], in1=psX[:, s * 128:(s + 1) * 128], op=ALU.add),
                           reads=[psX, xs[s]], writes=[xs[s]])

            def back_to_tokens(pm, gcol, dblk, tix):
                rt_ = rT[tix % 2]
                op("act", lambda: nc.scalar.activation(out=rt_[:], in_=pm[:], func=AF.Identity, scale=modT[:, gcol + dblk:gcol + dblk + 1]), reads=[pm, modT], writes=[rt_])
                b2t_flush()
                b2t_pending.append((rt_, dblk))

            wcnt = [0]

            def load_w(name, c0, ncols, kblocks=16):
                w_ = wb[wcnt[0] % 3]
                q_ = "sp" if wcnt[0] % 2 == 0 else "act"
                wcnt[0] += 1
                ch_ = c0 // 256
                dma(q_, w_[:, 0:kblocks, 0:ncols], wbf[name].ap()[ch_ * 128:(ch_ + 1) * 128, :].rearrange("p (i c) -> p i c", c=256), reads=[r_wbf[name]], writes=[w_])
                return w_

            tix = 0
            for tt in range(4):
                for s in range(4):
                    dma("sp", xs[s][:], xo[(tt * 4 + s) * 128:(tt * 4 + s + 1) * 128, :], writes=[xs[s]])
                    r = tt * 4 + s
                    norm_block(xs[s], (rstd, r), junk, xns[s % 2], psTb, hT, s * 128, G1, 0, (ssq, r), (rt, r))
                with contextlib.ExitStack() as s2:
                    ghT = sbuf(s2, [128, 32, 512], BF16)
                    Yh = sbuf(s2, [128, 8, 512], BF16)
                    Ya = sbuf(s2, [128, 8, 512], BF16)
                    yld = [sbuf(s2, [128, 8, 512], BF16)] * 2
                    mT = sbuf(s2, [128, 16, 512], BF16)
                    t1 = sbuf(s2, [128, 512], F32)
                    t2 = sbuf(s2, [128, 512], F32)
                    for n8 in range(16):
                        w_ = load_w("w_g", n8 * 256, 256)
                        for fb in range(2):
                            pm = psM[(n8 * 2 + fb) % 3]
                            for i in range(16):
                                op("pe", lambda i=i, fb=fb, pm=pm, w_=w_: nc.tensor.matmul(pm[:], w_[:, i, fb * 128:(fb + 1) * 128], hT[:, i, :], start=(i == 0), stop=(i == 15)),
                                   reads=[w_, hT, hT.r2], writes=[pm], inc=(i == 15))
                            op("act", lambda pm=pm, n8=n8, fb=fb: nc.scalar.activation(out=ghT[:, n8 * 2 + fb, :], in_=pm[:], func=AF.Sigmoid), reads=[pm], writes=[ghT])
                    for which, dst in ((0, Yh), (1, Ya)):
                        for q in range(4):
                            yl = yld[q % 2]
                            for rr in range(4):
                                src = (ag_out_hy if which == 0 else ag_out_at)[q * 2 + tt // 2].ap()[rr * 256:rr * 256 + 256, (tt % 2) * 512:(tt % 2 + 1) * 512]
                                dma("sp" if rr % 2 == 0 else "act", yl[:, rr * 2:rr * 2 + 2, :], src.rearrange("(k p) n -> p k n", p=128),
                                    reads=[r_agout if which == 0 else r_agout_at], writes=[yl])
                            if q == 0:
                                op("dve", lambda yl=yl, dst=dst, q=q: nc.vector.tensor_scalar(dst[:], yl[:], selt[:, q:q + 1], None, ALU.mult), reads=[yl, selt], writes=[dst])
                            else:
                                op("dve", lambda yl=yl, dst=dst, q=q: nc.vector.scalar_tensor_tensor(out=dst[:], in0=yl[:], scalar=selt[:, q:q + 1], in1=dst[:], op0=ALU.mult, op1=ALU.add),
                                   reads=[yl, selt, dst], writes=[dst])
                    for n4 in range(8):
                        wh = load_w("w_ph", n4 * 256, 256, 8)
                        wa_ = load_w("w_pa", n4 * 256, 256, 8)
                        for fb in range(2):
                            dblk = n4 * 2 + fb
                            pmh = psM[(2 * dblk) % 4]
                            pma = psM[(2 * dblk + 1) % 4]
                            for k_ in range(8):
                                op("pe", lambda k_=k_, fb=fb: nc.tensor.matmul(pmh[:], wh[:, k_, fb * 128:(fb + 1) * 128], Yh[:, k_, :], start=(k_ == 0), stop=(k_ == 7)), reads=[wh, Yh], writes=[pmh], inc=(k_ == 7))
                            for k_ in range(8):
                                op("pe", lambda k_=k_, fb=fb: nc.tensor.matmul(pma[:], wa_[:, k_, fb * 128:(fb + 1) * 128], Ya[:, k_, :], start=(k_ == 0), stop=(k_ == 7)), reads=[wa_, Ya], writes=[pma], inc=(k_ == 7))
                            op("dve", lambda dblk=dblk: nc.vector.tensor_tensor(out=t1[:], in0=pmh[:], in1=ghT[:, dblk, :], op=ALU.mult), reads=[pmh, ghT], writes=[t1])
                            op("dve", lambda dblk=dblk: nc.vector.tensor_tensor(out=t2[:], in0=pma[:], in1=ghT[:, 16 + dblk, :], op=ALU.mult), reads=[pma, ghT], writes=[t2])
                            op("dve", lambda dblk=dblk: nc.vector.tensor_tensor(out=mT[:, dblk, :], in0=t1[:], in1=t2[:], op=ALU.add), reads=[t1, t2], writes=[mT])
                    for n4 in range(8):
                        w_ = load_w("w_o", n4 * 256, 256)
                        for fb in range(2):
                            dblk = n4 * 2 + fb
                            pm = psM[dblk % 3]
                            for i in range(16):
                                op("pe", lambda i=i, fb=fb, pm=pm, w_=w_: nc.tensor.matmul(pm[:], w_[:, i, fb * 128:(fb + 1) * 128], mT[:, i, :], start=(i == 0), stop=(i == 15)),
                                   reads=[w_, mT], writes=[pm], inc=(i == 15))
                            back_to_tokens(pm, 32, dblk, tix)
                            tix += 1
                    b2t_flush()
                    kb.barrier()
                for s in range(4):
                    r = 16 + tt * 4 + s
                    norm_block(xs[s], (rstd, r), junk, xns[s % 2], psTb, hT, s * 128, G2, 48, (ssq, r), (rt, r))
                with contextlib.ExitStack() as s2:
                    actT = sbuf(s2, [128, 44, 512], BF16)
                    sg = [sbuf(s2, [128, 512], F32) for _ in range(2)]
                    wd = [sbuf(s2, [128, 44, 128], BF16) for _ in range(2)]
                    for n11 in range(22):
                        wg_ = load_w("w_gate", n11 * 256, 256)
                        wu_ = load_w("w_up", n11 * 256, 256)
                        for fb in range(2):
                            f = n11 * 2 + fb
                            pg = psM[(2 * f) % 4]
                            pu = psM[(2 * f + 1) % 4]
                            for i in range(16):
                                op("pe", lambda i=i, fb=fb: nc.tensor.matmul(pg[:], wg_[:, i, fb * 128:(fb + 1) * 128], hT[:, i, :], start=(i == 0), stop=(i == 15)), reads=[wg_, hT, hT.r2], writes=[pg], inc=(i == 15))
                            for i in range(16):
                                op("pe", lambda i=i, fb=fb: nc.tensor.matmul(pu[:], wu_[:, i, fb * 128:(fb + 1) * 128], hT[:, i, :], start=(i == 0), stop=(i == 15)), reads=[wu_, hT, hT.r2], writes=[pu], inc=(i == 15))
                            sg_ = sg[f % 2]
                            op("act", lambda sg_=sg_: nc.scalar.activation(out=sg_[:], in_=pg[:], func=AF.Silu), reads=[pg], writes=[sg_])
                            op("dve", lambda sg_=sg_, f=f: nc.vector.tensor_tensor(out=actT[:, f, :], in0=pu[:], in1=sg_[:], op=ALU.mult), reads=[pu, sg_], writes=[actT])
                    for dblk in range(16):
                        wd_ = wd[dblk % 2]
                        dma("sp" if dblk % 2 == 0 else "act", wd_[:], wbf["w_down"].ap()[dblk * 128:(dblk + 1) * 128, :].rearrange("p (f c) -> p f c", c=128), reads=[r_wbf["w_down"]], writes=[wd_])
                        pm = psM[dblk % 3]
                        for f in range(44):
                            op("pe", lambda f=f, pm=pm, wd_=wd_: nc.tensor.matmul(pm[:], wd_[:, f, :], actT[:, f, :], start=(f == 0), stop=(f == 43)), reads=[wd_, actT], writes=[pm], inc=(f == 43))
                        back_to_tokens(pm, 80, dblk, tix)
                        tix += 1
                    b2t_flush()
                    kb.barrier()
                for s in range(4):
                    dma("sp", out[(tt * 4 + s) * 128:(tt * 4 + s + 1) * 128, :], xs[s][:], reads=[xs[s]], writes=[r_out])
            kb.barrier()
    return nc


_CACHE = {}
_UPTO = 99


def _bf(a):
    return np.ascontiguousarray(a).astype(ml_dtypes.bfloat16)


def kernel(**inputs):
    f32 = np.float32
    g = {k: np.asarray(v) for k, v in inputs.items()}
    x = g["x"].astype(f32)
    l = 0

    def pp(v):
        return np.ascontiguousarray(v.reshape(16, 128).T).astype(f32)

    s_hy, s_q, s_k, s_v = 3072, 4096, 5120, 6144
    w_in = g["w_in"][l]
    tt = np.linspace(0.0, 1.0, L, dtype=f32)
    idx = np.arange(2 * L)
    posn = np.where(idx >= L, idx - L, np.clip(L - idx, 0, L - 1))
    wv = (2.0 * math.pi / L) * posn.astype(f32)
    bands = np.linspace(1e-4, 15, 16, dtype=f32)
    zT2 = np.concatenate([tt[posn][None, :], np.cos(bands[:, None] * wv[None, :]), -np.sin(bands[:, None] * wv[None, :])], axis=0).astype(f32)
    zT2 = np.ascontiguousarray(np.concatenate([zT2, zT2[:, ::-1]], axis=1))
    tgrid = tt[posn][None, :].astype(f32)
    tgrid = np.ascontiguousarray(np.concatenate([tgrid, tgrid[:, ::-1]], axis=1))
    min_decay = math.log(1e-2) / 0.3
    max_decay = math.log(1e-2) / 1.5
    deltas = np.abs(np.linspace(min_decay, max_decay, 1024, dtype=f32))
    invf = (500000.0 ** (-np.arange(8, dtype=f32) * (2.0 / 16))).astype(f32)
    ident = np.eye(128, dtype=f32)

    shared = {
        "w_ada": np.ascontiguousarray(g["w_ada"][l]).astype(f32),
        "badaT": np.ascontiguousarray(g["b_ada"][l].reshape(96, 128).T).astype(f32),
        "n1g": pp(g["norm1_g"][l]), "n2g": pp(g["norm2_g"][l]),
        "w_g": np.ascontiguousarray(w_in[:, s_v:]).astype(f32),
        "fw1": g["hy_filt_w1"][l].astype(f32), "fw2": g["hy_filt_w2"][l].astype(f32), "fw3": g["hy_filt_w3"][l].astype(f32),
        "fbias": np.ascontiguousarray(np.stack([g["hy_filt_b1"][l], g["hy_filt_b2"][l], g["hy_filt_b3"][l]], axis=1)).astype(f32),
        "ffreq": np.ascontiguousarray(g["hy_filt_freq"][l].T).astype(f32),
        "gqk_bc": np.ascontiguousarray(np.broadcast_to(np.concatenate([np.tile(g["q_norm_g"][l], 4), np.tile(g["k_norm_g"][l], 4)])[None, :], (128, 512))).astype(f32),
        "lamv": np.concatenate([g["lam_q1"][l], g["lam_k1"][l], g["lam_q2"][l], g["lam_k2"][l]])[None, :].astype(f32),
        "subln_bc": np.ascontiguousarray(np.broadcast_to(g["subln_g"][l][None, :], (128, 128))).astype(f32),
        "w_ph": g["w_proj_hy"][l].astype(f32), "w_pa": g["w_proj_att"][l].astype(f32), "w_o": g["w_out"][l].astype(f32),
        "w_gate": g["w_gate"][l].astype(f32), "w_up": g["w_up"][l].astype(f32), "w_down": g["w_down"][l].astype(f32),
        "ident_bf": ident.astype(ml_dtypes.bfloat16), "ident_f": ident,
        "jrev": np.ascontiguousarray(ident[::-1]).astype(ml_dtypes.bfloat16),
        "zT2": zT2, "tgrid": tgrid,
        "invf_bc": np.ascontiguousarray(np.broadcast_to(invf[None, :], (128, 8))).astype(f32),
    }
    in_maps = []
    for core in range(8):
        b, j = core // 4, core % 4
        ch = slice(256 * j, 256 * j + 256)
        hy_cols = np.concatenate([np.arange(256 * j, 256 * j + 256) + o for o in (0, 1024, 2048)])
        qkv_cols = np.concatenate([np.arange(256 * j, 256 * j + 256) + o for o in (s_hy, s_q, s_k)])
        cw = g["hy_conv_w"][l][:, hy_cols]
        convw = np.ascontiguousarray(cw.reshape(3, 6, 128).transpose(2, 1, 0).reshape(128, 18)).astype(f32)
        convb = np.ascontiguousarray(g["hy_conv_b"][l][hy_cols].reshape(6, 128).T).astype(f32)
        fwo = g["hy_filt_w_out"][l].reshape(64, 2, 2, 1024)[:, :, :, ch].reshape(64, 1024)
        hyb = g["hy_bias"][l][:, ch].reshape(1, 512)
        selv = np.zeros((128, 4), f32)
        selv[:, j] = 1.0
        m = dict(shared)
        m.update({
            "xb": np.ascontiguousarray(x[b]), "xo": np.ascontiguousarray(x[b, 2048 * j:2048 * (j + 1)]),
            "cb": pp(g["c"][b]),
            "pos": np.ascontiguousarray(g["positions"][b].reshape(NB, 128).T).astype(np.int32),
            "w_hy": np.ascontiguousarray(w_in[:, hy_cols]).astype(f32),
            "w_qkv": np.ascontiguousarray(w_in[:, qkv_cols]).astype(f32),
            "convw": convw, "convb": convb,
            "fwout": np.ascontiguousarray(fwo).astype(f32),
            "ndelta": np.ascontiguousarray((-deltas[ch]).reshape(2, 128).T).astype(f32),
            "hyb_bc": np.ascontiguousarray(np.broadcast_to(hyb, (128, 512))).astype(f32),
            "hybT": np.ascontiguousarray(g["hy_bias"][l][:, ch].reshape(4, 128).T).astype(f32),
            "sel": selv,
        })
        in_maps.append(m)
    if "nc" not in _CACHE:
        _CACHE["nc"] = build(_UPTO)
    res = run_bass_kernel_spmd(_CACHE["nc"], in_maps, core_ids=list(range(8)))
    outp = np.zeros((2, L, D), f32)
    for core in range(8):
        b, j = core // 4, core % 4
        outp[b, 2048 * j:2048 * (j + 1)] = np.asarray(res.results[core]["out"]).astype(f32)
    return outp
```
